# Optimizing a Trainium2 kernel written in Bass

```python
import math
import jax, jax.numpy as jnp
from jax import lax
import numpy as np

D_MODEL = 1024
BATCH = 8
SEQ = 2048
DEPTH = 4
DEC_BATCH = 128
DEC_SEQ = 8
PAST_LEN = 16384
PAGE_SIZE = 128

N_META = 16
CHUNK = 64
DN_HEADS = 4
DN_HEAD_DIM = 128
DN_WIDTH = DN_HEADS * DN_HEAD_DIM
CONV_W = 4
ML_HEADS = 4
ML_HEAD_DIM = 128
ML_WIDTH = ML_HEADS * ML_HEAD_DIM
S5_GROUP = 16
S5_WIDTH = 512
S5_GROUPS = S5_WIDTH // S5_GROUP
S5_STATE = 64
N_BRANCH = 3
EPS = 1e-6
NEG = -1e30

IN_SIZES = (3 * DN_WIDTH, DN_WIDTH, DN_HEADS, DN_HEADS,
            3 * ML_WIDTH, ML_WIDTH, ML_WIDTH, ML_HEADS, ML_HEADS,
            S5_WIDTH, S5_WIDTH, N_BRANCH * D_MODEL)
IN_COLS = 4 * DN_WIDTH + 2 * DN_HEADS + 5 * ML_WIDTH + 2 * ML_HEADS + 2 * S5_WIDTH + N_BRANCH * D_MODEL

kernel_name = 'hybrid_gdn_mlstm_s5_decode_step'


def f32(a):
    return a.astype(jnp.float32)


def rms_norm(x, w):
    x32 = x.astype(jnp.float32)
    y = x32 * lax.rsqrt(jnp.mean(x32 * x32, axis=-1, keepdims=True) + EPS)
    return (y * w.astype(jnp.float32)).astype(x.dtype)


def l2_normalize(x):
    return x * lax.rsqrt(jnp.sum(x * x, axis=-1, keepdims=True) + EPS)


def split_cols(a, sizes):
    cuts, acc = [], 0
    for s in sizes[:-1]:
        acc += s
        cuts.append(acc)
    return jnp.split(a, cuts, axis=-1)


def causal_conv(buf, u, w):
    t = u.shape[1]
    up = jnp.concatenate([buf, u], axis=1)
    out = w[0] * up[:, 0:t]
    for j in range(1, CONV_W):
        out = out + w[j] * up[:, j:j + t]
    return jax.nn.silu(out), up[:, t:]


def run_chunks(step, state, xs, is_prompt):
    if not is_prompt:
        return step(state, *xs)
    state, out_meta = step(state, *[a[:, :N_META] for a in xs])
    bsz, t = xs[0].shape[0], xs[0].shape[1]
    n = (t - N_META) // CHUNK

    def to_chunks(a):
        a = a[:, N_META:]
        return jnp.swapaxes(a.reshape((bsz, n, CHUNK) + a.shape[2:]), 0, 1)

    state, out = lax.scan(lambda s, c: step(s, *c), state, tuple(to_chunks(a) for a in xs))
    out = jnp.swapaxes(out, 0, 1).reshape((bsz, n * CHUNK) + out.shape[3:])
    return state, jnp.concatenate([out_meta, out], axis=1)


def gated_delta_chunk(s, q, k, v, g, beta):
    q, k, v = (jnp.swapaxes(a, 1, 2) for a in (q, k, v))
    g, beta = jnp.swapaxes(g, 1, 2), jnp.swapaxes(beta, 1, 2)
    L = q.shape[2]
    tril = jnp.tril(jnp.ones((L, L), dtype=bool))
    strict = jnp.tril(jnp.ones((L, L), dtype=bool), -1)
    G = jnp.cumsum(g, axis=-1)
    decay = jnp.where(tril, jnp.exp(jnp.where(tril, G[..., :, None] - G[..., None, :], 0.0)), 0.0)
    kb = k * beta[..., None]
    a = jnp.where(strict, jnp.einsum('bhik,bhjk->bhij', kb, k) * decay, 0.0)
    lhs = a + jnp.eye(L, dtype=a.dtype)
    rhs = jnp.concatenate([v * beta[..., None], kb * jnp.exp(G)[..., None]], axis=-1)
    sol = lax.linalg.triangular_solve(lhs, rhs, left_side=True, lower=True, unit_diagonal=True)
    dv = v.shape[-1]
    u, w = sol[..., :dv], sol[..., dv:]
    v_new = u - jnp.einsum('bhlk,bhkv->bhlv', w, s)
    attn = jnp.einsum('bhik,bhjk->bhij', q, k) * decay
    o = jnp.einsum('bhlk,bhkv->bhlv', q * jnp.exp(G)[..., None], s) + jnp.einsum('bhij,bhjv->bhiv', attn, v_new)
    g_last = G[..., -1:]
    s = s * jnp.exp(g_last)[..., None] + jnp.einsum('bhlk,bhlv->bhkv', k * jnp.exp(g_last - G)[..., None], v_new)
    return s, jnp.swapaxes(o, 1, 2)


def mlstm_chunk(state, q, k, v, ig, lf):
    c, n, m = state
    q, k, v = (jnp.swapaxes(a, 1, 2) for a in (q, k, v))
    ig, lf = jnp.swapaxes(ig, 1, 2), jnp.swapaxes(lf, 1, 2)
    L = q.shape[2]
    tril = jnp.tril(jnp.ones((L, L), dtype=bool))
    b = jnp.cumsum(lf, axis=-1)
    inter = b + m[..., None]
    logd = jnp.where(tril, b[..., :, None] - b[..., None, :] + ig[..., None, :], NEG)
    m_t = jnp.maximum(inter, jnp.max(logd, axis=-1))
    w_intra = jnp.exp(logd - m_t[..., None])
    w_inter = jnp.exp(inter - m_t)
    sc = jnp.einsum('bhik,bhjk->bhij', q, k) * w_intra
    num = w_inter[..., None] * jnp.einsum('bhlk,bhkv->bhlv', q, c) + jnp.einsum('bhij,bhjv->bhiv', sc, v)
    den = w_inter * jnp.einsum('bhlk,bhk->bhl', q, n) + jnp.sum(sc, axis=-1)
    h = num / jnp.maximum(jnp.abs(den), jnp.exp(-m_t))[..., None]
    m_new = m_t[..., -1]
    w_state = jnp.exp(b[..., -1:] - b + ig - m_new[..., None])
    carry = jnp.exp(b[..., -1] + m - m_new)
    c = carry[..., None, None] * c + jnp.einsum('bhl,bhlk,bhlv->bhkv', w_state, k, v)
    n = carry[..., None] * n + jnp.einsum('bhl,bhlk->bhk', w_state, k)
    return (c, n, m_new), jnp.swapaxes(h, 1, 2)


def ssm_combine(x, y):
    a1, b1 = x
    a2, b2 = y
    return a1 * a2, a2 * b1 + b2


def s5_mixer(h0_re, h0_im, u, lam_re, lam_im, log_dt, b_re, b_im, c_re, c_im, d):
    bsz, t = u.shape[0], u.shape[1]
    lam = lax.complex(lam_re, lam_im)
    lam_bar = jnp.exp(lam * jnp.exp(log_dt)[:, None])
    b_bar = ((lam_bar - 1.0) / lam)[..., None] * lax.complex(b_re, b_im)
    ug = u.reshape(bsz, t, S5_GROUPS, S5_GROUP).astype(jnp.complex64)
    bu = jnp.einsum('gph,btgh->btgp', b_bar, ug)
    bu = bu.at[:, 0].add(lam_bar * lax.complex(h0_re, h0_im))
    a = jnp.broadcast_to(lam_bar, bu.shape)
    _, hs = lax.associative_scan(ssm_combine, (a, bu), axis=1)
    y = jnp.einsum('ghp,btgp->btgh', lax.complex(c_re, c_im), hs).real.reshape(bsz, t, S5_WIDTH) + d * u
    return y, hs[:, -1].real, hs[:, -1].imag


def hybrid_layer(x, state, p, is_prompt):
    conv_buf, dn_s, ml_c, ml_n, ml_m, s5_re, s5_im = state
    (norm_w, w_in, b_gate, dn_conv_w, dn_a_log, dn_dt_bias, dn_norm_w,
     ml_bias_i, ml_bias_f, ml_norm_w, s5_lambda_re, s5_lambda_im, s5_log_dt,
     s5_b_re, s5_b_im, s5_c_re, s5_c_im, s5_d, s5_w_glu, s5_b_glu,
     w_branch_dn, w_branch_ml, w_branch_s5, w_out) = p
    dt = x.dtype
    bsz, t = x.shape[0], x.shape[1]
    h = rms_norm(x, norm_w)
    proj = jnp.einsum('btd,de->bte', h, w_in).astype(jnp.float32)
    (dn_qkv, dn_z, dn_a, dn_b, ml_qkv, ml_o, ml_z, ml_i, ml_f,
     s5_u, s5_z, gate_logits) = split_cols(proj, IN_SIZES)

    qkv, new_conv = causal_conv(conv_buf, dn_qkv, f32(dn_conv_w))
    q, k, v = [a.reshape(bsz, t, DN_HEADS, DN_HEAD_DIM) for a in jnp.split(qkv, 3, axis=-1)]
    q = l2_normalize(q) * (DN_HEAD_DIM ** -0.5)
    k = l2_normalize(k)
    g = -jnp.exp(f32(dn_a_log)) * jax.nn.softplus(dn_a + f32(dn_dt_bias))
    beta = jax.nn.sigmoid(dn_b)
    new_dn_s, o_dn = run_chunks(gated_delta_chunk, dn_s, (q, k, v, g, beta), is_prompt)
    o_dn = rms_norm(o_dn, dn_norm_w) * jax.nn.silu(dn_z.reshape(bsz, t, DN_HEADS, DN_HEAD_DIM))
    o_dn = o_dn.reshape(bsz, t, DN_WIDTH)

    q, k, v = [a.reshape(bsz, t, ML_HEADS, ML_HEAD_DIM) for a in jnp.split(ml_qkv, 3, axis=-1)]
    k = k * (ML_HEAD_DIM ** -0.5)
    ig = ml_i + f32(ml_bias_i)
    lf = jax.nn.log_sigmoid(ml_f + f32(ml_bias_f))
    (new_c, new_n, new_m), h_ml = run_chunks(mlstm_chunk, (ml_c, ml_n, ml_m), (q, k, v, ig, lf), is_prompt)
    o_ml = rms_norm(h_ml, ml_norm_w).reshape(bsz, t, ML_WIDTH) * jax.nn.sigmoid(ml_o) * jax.nn.silu(ml_z)

    y5, new_re, new_im = s5_mixer(s5_re, s5_im, s5_u, f32(s5_lambda_re), f32(s5_lambda_im), f32(s5_log_dt),
                                  f32(s5_b_re), f32(s5_b_im), f32(s5_c_re), f32(s5_c_im), f32(s5_d))
    y5 = jax.nn.gelu(y5)
    y5 = y5 * jax.nn.sigmoid(y5 @ f32(s5_w_glu) + f32(s5_b_glu))
    o_s5 = y5 * jax.nn.silu(s5_z)

    gates = jax.nn.sigmoid(gate_logits + f32(b_gate)).astype(dt).reshape(bsz, t, N_BRANCH, D_MODEL)
    mixed = (gates[:, :, 0] * (o_dn.astype(dt) @ w_branch_dn)
             + gates[:, :, 1] * (o_ml.astype(dt) @ w_branch_ml)
             + gates[:, :, 2] * (o_s5.astype(dt) @ w_branch_s5))
    x = x + mixed @ w_out
    return x, (new_conv, new_dn_s, new_c, new_n, new_m, new_re, new_im)


def run_trunk(x, states, params, is_prompt):
    per_layer = []
    for layer in range(DEPTH):
        st = tuple(s[layer].astype(jnp.float32) for s in states)
        x, new = hybrid_layer(x, st, tuple(w[layer] for w in params), is_prompt)
        per_layer.append(new)
    stacked = tuple(jnp.stack([new[i] for new in per_layer]).astype(x.dtype) for i in range(len(states)))
    return x, stacked


def setup_inputs(seed: int = 0) -> dict:
    key = jax.random.key(seed)
    ks = jax.random.split(key, 40)

    def nrm(i, shape, scale):
        return jax.random.normal(ks[i], shape, jnp.float32) * scale

    def gain(i, shape):
        return 1.0 + nrm(i, shape, 0.01)

    dn_dt = jnp.exp(jax.random.uniform(ks[15], (DEPTH, DN_HEADS), jnp.float32, math.log(1e-3), math.log(1e-1)))
    return {
        'x_prompt': nrm(0, (BATCH, SEQ, D_MODEL), 1.0),
        'x_sample': nrm(1, (DEC_BATCH, DEC_SEQ, D_MODEL), 1.0),
        'state_dn_conv': nrm(2, (DEPTH, DEC_BATCH, CONV_W - 1, 3 * DN_WIDTH), 1.0),
        'state_dn_s': nrm(3, (DEPTH, DEC_BATCH, DN_HEADS, DN_HEAD_DIM, DN_HEAD_DIM), 0.1),
        'state_ml_c': nrm(4, (DEPTH, DEC_BATCH, ML_HEADS, ML_HEAD_DIM, ML_HEAD_DIM), 0.1),
        'state_ml_n': nrm(5, (DEPTH, DEC_BATCH, ML_HEADS, ML_HEAD_DIM), 0.1),
        'state_ml_m': nrm(6, (DEPTH, DEC_BATCH, ML_HEADS), 1.0),
        'state_s5_re': nrm(7, (DEPTH, DEC_BATCH, S5_GROUPS, S5_STATE), 0.5),
        'state_s5_im': nrm(8, (DEPTH, DEC_BATCH, S5_GROUPS, S5_STATE), 0.5),
        'meta_tokens': nrm(9, (N_META, D_MODEL), 1.0),
        'norm_w': gain(10, (DEPTH, D_MODEL)),
        'w_in': nrm(11, (DEPTH, D_MODEL, IN_COLS), D_MODEL ** -0.5),
        'b_gate': nrm(12, (DEPTH, N_BRANCH * D_MODEL), 0.02),
        'dn_conv_w': nrm(13, (DEPTH, CONV_W, 3 * DN_WIDTH), CONV_W ** -0.5),
        'dn_a_log': jnp.log(jax.random.uniform(ks[14], (DEPTH, DN_HEADS), jnp.float32, 1.0, 16.0)),
        'dn_dt_bias': dn_dt + jnp.log(-jnp.expm1(-dn_dt)),
        'dn_norm_w': gain(16, (DEPTH, DN_HEAD_DIM)),
        'ml_bias_i': nrm(17, (DEPTH, ML_HEADS), 0.1),
        'ml_bias_f': jnp.linspace(3.0, 6.0, ML_HEADS, dtype=jnp.float32)[None, :] + nrm(18, (DEPTH, ML_HEADS), 0.1),
        'ml_norm_w': gain(19, (DEPTH, ML_HEAD_DIM)),
        's5_lambda_re': -0.5 + nrm(20, (DEPTH, S5_GROUPS, S5_STATE), 0.01),
        's5_lambda_im': math.pi * jnp.arange(S5_STATE, dtype=jnp.float32) + nrm(21, (DEPTH, S5_GROUPS, S5_STATE), 0.01),
        's5_log_dt': jax.random.uniform(ks[22], (DEPTH, S5_GROUPS), jnp.float32, math.log(1e-3), math.log(1e-1)),
        's5_b_re': nrm(23, (DEPTH, S5_GROUPS, S5_STATE, S5_GROUP), (2 * S5_GROUP) ** -0.5),
        's5_b_im': nrm(24, (DEPTH, S5_GROUPS, S5_STATE, S5_GROUP), (2 * S5_GROUP) ** -0.5),
        's5_c_re': nrm(25, (DEPTH, S5_GROUPS, S5_GROUP, S5_STATE), (2 * S5_STATE) ** -0.5),
        's5_c_im': nrm(26, (DEPTH, S5_GROUPS, S5_GROUP, S5_STATE), (2 * S5_STATE) ** -0.5),
        's5_d': nrm(27, (DEPTH, S5_WIDTH), 1.0),
        's5_w_glu': nrm(28, (DEPTH, S5_WIDTH, S5_WIDTH), S5_WIDTH ** -0.5),
        's5_b_glu': nrm(29, (DEPTH, S5_WIDTH), 0.02),
        'w_branch_dn': nrm(30, (DEPTH, DN_WIDTH, D_MODEL), DN_WIDTH ** -0.5),
        'w_branch_ml': nrm(31, (DEPTH, ML_WIDTH, D_MODEL), ML_WIDTH ** -0.5),
        'w_branch_s5': nrm(32, (DEPTH, S5_WIDTH, D_MODEL), S5_WIDTH ** -0.5),
        'w_out': nrm(33, (DEPTH, D_MODEL, D_MODEL), D_MODEL ** -0.5),
        'final_norm_w': gain(34, (D_MODEL,)),
    }


def reference(x_prompt, x_sample, state_dn_conv, state_dn_s, state_ml_c, state_ml_n, state_ml_m,
              state_s5_re, state_s5_im, meta_tokens, norm_w, w_in, b_gate, dn_conv_w, dn_a_log,
              dn_dt_bias, dn_norm_w, ml_bias_i, ml_bias_f, ml_norm_w, s5_lambda_re, s5_lambda_im,
              s5_log_dt, s5_b_re, s5_b_im, s5_c_re, s5_c_im, s5_d, s5_w_glu, s5_b_glu,
              w_branch_dn, w_branch_ml, w_branch_s5, w_out, final_norm_w):
    params = (norm_w, w_in, b_gate, dn_conv_w, dn_a_log, dn_dt_bias, dn_norm_w,
              ml_bias_i, ml_bias_f, ml_norm_w, s5_lambda_re, s5_lambda_im, s5_log_dt,
              s5_b_re, s5_b_im, s5_c_re, s5_c_im, s5_d, s5_w_glu, s5_b_glu,
              w_branch_dn, w_branch_ml, w_branch_s5, w_out)
    dt = x_prompt.dtype
    bsz = x_prompt.shape[0]

    xp = jnp.concatenate([jnp.broadcast_to(meta_tokens[None].astype(dt), (bsz, N_META, D_MODEL)), x_prompt], axis=1)

    def zeros(*shape):
        return jnp.zeros((DEPTH, bsz) + shape, jnp.float32)

    prompt_init = (zeros(CONV_W - 1, 3 * DN_WIDTH),
                   zeros(DN_HEADS, DN_HEAD_DIM, DN_HEAD_DIM),
                   zeros(ML_HEADS, ML_HEAD_DIM, ML_HEAD_DIM),
                   zeros(ML_HEADS, ML_HEAD_DIM),
                   jnp.full((DEPTH, bsz, ML_HEADS), NEG, jnp.float32),
                   zeros(S5_GROUPS, S5_STATE),
                   zeros(S5_GROUPS, S5_STATE))
    hp, (p_dn_conv, p_dn_s, p_ml_c, p_ml_n, p_ml_m, p_s5_re, p_s5_im) = run_trunk(xp, prompt_init, params, True)
    y_prompt = rms_norm(hp, final_norm_w)[:, N_META:]

    sample_init = (state_dn_conv, state_dn_s, state_ml_c, state_ml_n, state_ml_m, state_s5_re, state_s5_im)
    hs, (s_dn_conv, s_dn_s, s_ml_c, s_ml_n, s_ml_m, s_s5_re, s_s5_im) = run_trunk(x_sample, sample_init, params, False)
    y_sample = rms_norm(hs, final_norm_w)

    return (y_prompt, y_sample,
            p_dn_conv, p_dn_s, p_ml_c, p_ml_n, p_ml_m, p_s5_re, p_s5_im,
            s_dn_conv, s_dn_s, s_ml_c, s_ml_n, s_ml_m, s_s5_re, s_s5_im)
```

```python
from contextlib import ExitStack
import numpy as np
import concourse.bass as bass
import concourse.mybir as mybir
from concourse.bass_utils import run_bass_kernel_spmd

F32 = mybir.dt.float32
BF16 = mybir.dt.bfloat16
AF = mybir.ActivationFunctionType
ALU = mybir.AluOpType
AX = mybir.AxisListType

ENGS = ['pe', 'act', 'dve', 'pool', 'sp']
NSLOT = 40


class Sched:
    def __init__(self):
        self.ops = {e: [] for e in ENGS}
        self.clock = {e: {} for e in ENGS}
        self.opclock = {}
        self.reg = {}
        self.slot_val = [0] * NSLOT
        self.next_slot = 0
        import os
        self.self_sync = set(os.environ.get('SELF_SYNC', 'act,dve').split(','))
        self.last_real = {}

    def _deps(self, reads, writes):
        deps = []
        for r in reads:
            st = self.reg.get(r)
            if st and st['w']:
                deps.append(st['w'])
        for w in writes:
            st = self.reg.get(w)
            if st:
                if st['w']:
                    deps.append(st['w'])
                deps.extend(st['r'].items())
        return deps

    def barrier(self):
        tg = []
        for e in ENGS:
            if e != 'sp' and self.last_real.get(e):
                tg.append((e, self.last_real[e]))
        for i in range(NSLOT):
            if self.slot_val[i] > 0:
                tg.append((('d', i), self.slot_val[i]))
        for e in ENGS:
            self.add(e, None, extra=[t for t in tg if not (t[0] == e and e == 'pe')])

    def add(self, eng, fn, reads=(), writes=(), dma=False, extra=()):
        idx = len(self.ops[eng]) + 1
        clk = self.clock[eng]
        need = {}
        for (k, v) in extra:
            if clk.get(k, 0) >= v:
                continue
            if need.get(k, 0) < v:
                need[k] = v
        if fn is not None and not dma:
            self.last_real[eng] = idx
        for (k, v) in self._deps(reads, writes):
            if k == eng and (eng == 'pe' or eng not in self.self_sync):
                continue
            if clk.get(k, 0) >= v:
                continue
            if need.get(k, 0) < v:
                need[k] = v
        slot = None
        if dma:
            slot = self.next_slot
            self.next_slot = (self.next_slot + 1) % NSLOT
            pv = self.slot_val[slot]
            dk = ('d', slot)
            if pv > 0 and clk.get(dk, 0) < pv and need.get(dk, 0) < pv:
                need[dk] = pv
            self.slot_val[slot] = pv + 16
        for k, v in need.items():
            oc = self.opclock.get((k, v))
            if oc:
                for kk, vv in oc.items():
                    if clk.get(kk, 0) < vv:
                        clk[kk] = vv
            if clk.get(k, 0) < v:
                clk[k] = v
            if not isinstance(k, tuple):
                self.ops[k][v - 1]['signal'] = True
        if dma:
            ckey = (('d', slot), self.slot_val[slot])
            oc = dict(clk)
            oc[ckey[0]] = ckey[1]
            self.opclock[ckey] = oc
        else:
            ckey = (eng, idx)
            self.opclock[ckey] = dict(clk)
        self.ops[eng].append(dict(fn=fn, waits=list(need.items()), signal=False, slot=slot))
        for r in reads:
            st = self.reg.setdefault(r, {'w': None, 'r': {}})
            if st['r'].get(ckey[0], 0) < ckey[1]:
                st['r'][ckey[0]] = ckey[1]
        for w in writes:
            self.reg[w] = {'w': ckey, 'r': {}}
        return ckey

    def emit(self, nc, es):
        sems = {e: es.enter_context(nc.semaphore('s_' + e)) for e in ENGS}
        dsems = [es.enter_context(nc.semaphore('sd%d' % i)) for i in range(NSLOT)]
        for e in ENGS:
            c = 0
            for op in self.ops[e]:
                if op['signal']:
                    c += 1
                op['sigval'] = c
        block = es.enter_context(nc.Block())
        ops = self.ops
        slot_val = self.slot_val

        def run(e, eng):
            for op in ops[e]:
                for (k, v) in op['waits']:
                    if isinstance(k, tuple):
                        eng.wait_ge(dsems[k[1]], v)
                    else:
                        eng.wait_ge(sems[k], ops[k][v - 1]['sigval'])
                if op['fn'] is None:
                    continue
                ins = op['fn'](eng)
                if op['slot'] is not None:
                    ins.then_inc(dsems[op['slot']], 16)
                elif op['signal']:
                    ins.then_inc(sems[e], 1)

        @block.tensor
        def _(eng):
            run('pe', eng)

        @block.scalar
        def _(eng):
            run('act', eng)

        @block.vector
        def _(eng):
            run('dve', eng)

        @block.gpsimd
        def _(eng):
            run('pool', eng)

        @block.sync
        def _(eng):
            run('sp', eng)
            for i in range(NSLOT):
                if slot_val[i] > 0:
                    eng.wait_ge(dsems[i], slot_val[i])


D = 1024
DEPTH = 4
NMETA = 16
TPR = 2048
TP = NMETA + TPR
NSEQ = 16
TS = NSEQ * 8
TT = TP + TS
IN_COLS = 8720
EPS = 1e-6
TGS = [(0, 512), (512, 512), (1024, 512), (1536, 512), (2048, 144)]
FRAMES = [(0, 16, False)] + [(16 + 128 * k, 128, False) for k in range(16)] + [(TP, 128, True)]
C_DNQKV, C_DNZ, C_DNA, C_DNB = 0, 1536, 2048, 2052
C_MLQKV, C_MLO, C_MLZ, C_MLI, C_MLF = 2056, 3592, 4104, 4616, 4620
C_S5U, C_S5Z, C_GATE = 4624, 5136, 5648
ARENA = 38912
OB_W = 4384
M0 = 3 * OB_W


def tg_of(c0, n):
    return [gi for gi, (a, m) in enumerate(TGS) if a < c0 + n and c0 < a + m]


class Carver:
    def __init__(self, ar, lo, hi):
        self.ar, self.lo, self.hi, self.p = ar, lo, hi, lo

    def f32(self, *shape):
        n = int(np.prod(shape))
        assert self.p + n <= self.hi, ('arena overflow', self.p, n, self.hi)
        v = self.ar[:, self.p:self.p + n]
        self.p += n
        if len(shape) == 2:
            return v.rearrange("p (a b) -> p a b", a=shape[0])
        if len(shape) == 3:
            return v.rearrange("p (a b c) -> p a b c", a=shape[0], b=shape[1])
        return v

    def bf16(self, *shape):
        n = int(np.prod(shape))
        w = (n + 1) // 2
        assert self.p + w <= self.hi, ('arena overflow', self.p, w, self.hi)
        v = self.ar[:, self.p:self.p + w].bitcast(BF16)[:, 0:n]
        self.p += w
        if len(shape) == 2:
            return v.rearrange("p (a b) -> p a b", a=shape[0])
        if len(shape) == 3:
            return v.rearrange("p (a b c) -> p a b c", a=shape[0], b=shape[1])
        return v


class K:
    def __init__(self, nlayers=DEPTH, dbg=None, phases=('s5', 'dn', 'ml')):
        self.nlayers = nlayers
        self.dbg = dbg
        self.phases = phases
        self.nc = bass.Bass("TRN2", target_bir_lowering=False)
        self.s = Sched()
        self.es = ExitStack()
        self.uid = 0
        self.psn = 0
        self.pinned = set()
        self.rec = None
        import os
        self.use_f32r = os.environ.get('K_F32R', '0') == '1'
        self.ps_cur = 'all'
        self.ps_sets = {'all': list(range(8)), 'A': [0, 1, 2, 3], 'B': [4, 5, 6, 7]}
        self.ps_cnt = {'all': 0, 'A': 0, 'B': 0}
        self.wn = 0
        self.xrn = 0

    def sb(self, shape, dt=F32, name=None):
        self.uid += 1
        name = name or ('t%d' % self.uid)
        return self.es.enter_context(self.nc.sbuf_tensor(name, list(shape), dt))

    def dram_in(self, name, shape):
        return self.nc.dram_tensor(name, list(shape), F32, kind="ExternalInput").ap()

    def dram_out(self, name, shape):
        return self.nc.dram_tensor(name, list(shape), F32, kind="ExternalOutput").ap()

    def ps(self, pin=False):
        banks = self.ps_sets[self.ps_cur]
        cnt = self.ps_cnt
        while banks[cnt[self.ps_cur] % len(banks)] in self.pinned:
            cnt[self.ps_cur] += 1
        i = banks[cnt[self.ps_cur] % len(banks)]
        cnt[self.ps_cur] += 1
        if pin:
            self.pinned.add(i)
        return self.psb[i], ('ps', i)

    def unpin(self, key):
        self.pinned.discard(key[1])

    def _add(self, eng, fn, r=(), w=(), dma=False):
        if self.rec is not None:
            self.rec.append((eng, fn, tuple(r), tuple(w), dma))
            return None
        return self.s.add(eng, fn, r, w, dma=dma)

    def interleave(self, gens):
        lists = []
        for i, g in enumerate(gens):
            self.rec = []
            self.ps_cur = 'AB'[i]
            g()
            lists.append(self.rec)
        self.rec = None
        self.ps_cur = 'all'
        n = max(len(x) for x in lists)
        for j in range(n):
            for lst in lists:
                lo = (j * len(lst)) // n
                hi = ((j + 1) * len(lst)) // n
                for (eng, fn, r, w, dma) in lst[lo:hi]:
                    self.s.add(eng, fn, r, w, dma=dma)

    def op(self, eng, fn, r=(), w=()):
        return self._add(eng, fn, r, w)

    def dma(self, out, in_, r=(), w=(), q='sp'):
        return self._add(q, lambda e: e.dma_start(out=out, in_=in_), r, w, dma=True)

    def mm(self, out, lhsT, rhs, start, stop, r=(), w=()):
        return self._add('pe', lambda e: e.matmul(out, lhsT, rhs, start=start, stop=stop), r, w)

    def tr(self, out, in_, ident, r=(), w=()):
        return self._add('pe', lambda e: e.transpose(out, in_, ident), r, w)

    def actf(self, out, in_, func, bias=0.0, scale=1.0, r=(), w=()):
        return self._add('act', lambda e: e.activation(out, in_, func, bias=bias, scale=scale), r, w)

    def tt(self, eng, out, a, b, op, r=(), w=()):
        return self._add(eng, lambda e: e.tensor_tensor(out, a, b, op), r, w)

    def ts(self, eng, out, a, s1, op0, s2=None, op1=None, r=(), w=()):
        if op1 is None:
            return self._add(eng, lambda e: e.tensor_scalar(out, a, s1, None, op0), r, w)
        return self._add(eng, lambda e: e.tensor_scalar(out, a, s1, s2, op0, op1), r, w)

    def stt(self, out, in0, scalar, in1, op0, op1, r=(), w=()):
        return self._add('dve', lambda e: e.scalar_tensor_tensor(out, in0, scalar, in1, op0, op1), r, w)

    def cp(self, eng, out, in_, r=(), w=()):
        if eng == 'act':
            return self._add('act', lambda e: e.copy(out, in_), r, w)
        return self._add(eng, lambda e: e.tensor_copy(out, in_), r, w)

    def memset(self, eng, out, val, w=()):
        return self._add(eng, lambda e: e.memset(out, val), (), w)

    def scan(self, out, d0, d1, init, r=(), w=()):
        return self._add('dve', lambda e: e.tensor_tensor_scan(out, d0, d1, init, ALU.mult, ALU.add), r, w)

    def consts(self):
        nc = self.nc
        self.psb = [self.es.enter_context(nc.psum_tensor('psb%d' % i, [128, 512], F32)) for i in range(8)]
        self.ident = self.sb([128, 128], F32, 'ident')
        self.ones = self.sb([128, 128], F32, 'ones')
        self.identb = self.sb([128, 128], BF16, 'identb')
        self.onesb = self.sb([128, 128], BF16, 'onesb')
        self.segmask = self.sb([128, 128], F32, 'segmask')
        self.epsb = self.sb([128, 1], F32, 'epsb')
        idt, on, sg, eb = self.ident, self.ones, self.segmask, self.epsb
        self.memset('pool', on[:], 1.0, w=['ones'])
        self.op('pool', lambda e: e.affine_select(idt[:], on[:], [[-1, 128]], ALU.is_equal, 0.0, base=0,
                                                  channel_multiplier=1), r=['ones'], w=['ident'])
        ib, ob = self.identb, self.onesb
        self.cp('pool', ib[:], idt[:], r=['ident'], w=['identb'])
        self.cp('pool', ob[:], on[:], r=['ones'], w=['onesb'])
        self.memset('pool', sg[:], 1.0, w=['segmask'])
        self.memset('pool', sg[:, :].rearrange("p (s t) -> p s t", t=8)[:, :, 0:1], 0.0, w=['segmask'])
        self.memset('pool', eb[:], EPS, w=['epsb'])
        self.oneb = self.sb([128, 1], F32, 'oneb')
        ob_ = self.oneb
        self.memset('pool', ob_[:], 1.0, w=['oneb'])

    def load_cols(self, src2d, R, dst, dstkey):
        st = self.stage_small
        pst, pk = self.ps()
        idt = self.ident
        self.dma(st[0:R, :], src2d, w=['stsm'])
        self.tr(pst[:, 0:R], st[0:R, :], idt[0:R, 0:R], r=['stsm', 'ident'], w=[pk])
        self.cp('dve', dst, pst[:, 0:R], r=[pk], w=[dstkey])

    def wblock(self, src2d, kc, ncols):
        i = self.wn % 4
        j = self.wn % 2
        self.wn += 1
        st, wb = self.wst[j], self.wbf[i]
        self.dma(st[:, 0:kc, 0:ncols], src2d.rearrange("(c p) e -> p c e", p=128), w=[('wst', j)])
        self.cp('pool', wb[:, 0:kc, 0:ncols], st[:, 0:kc, 0:ncols], r=[('wst', j)], w=[('wbf', i)])
        return wb, ('wbf', i)

    def proj(self, l, col0, ncols, consumer):
        wb, wk = self.wblock(self.w_in[l][:, col0:col0 + ncols], 8, ncols)
        hT = self.hT
        for gi, (c0, n) in enumerate(TGS):
            pst, pk = self.ps()
            for fc in range(8):
                self.mm(pst[0:ncols, 0:n], wb[:, fc, 0:ncols], hT[:, fc, c0:c0 + n], fc == 0, fc == 7,
                        r=[wk, ('hT', fc, gi)], w=[pk])
            consumer(gi, c0, n, pst, pk)

    def build(self):
        nc = self.nc
        NL = self.nlayers
        di = self.dram_in
        xp = di('xp', [TPR, D])
        xs = di('xs', [TS, D])
        meta = di('meta', [NMETA, D])
        norm_w = di('norm_w', [DEPTH, D])
        self.w_in = w_in = di('w_in', [DEPTH, D, IN_COLS])
        b_gate = di('b_gate', [DEPTH, 3 * D])
        w_br = [di('w_br%d' % b, [DEPTH, 512, D]) for b in range(3)]
        w_out = di('w_out', [DEPTH, D, D])
        fnw = di('fnw', [1, D])
        self.P = dict(
            lam_re=di('lam_re', [DEPTH, 32, 64]), lam_im=di('lam_im', [DEPTH, 32, 64]), log_dt=di('log_dt', [DEPTH, 32]),
            b_re=di('b_re', [DEPTH, 32, 64, 16]), b_im=di('b_im', [DEPTH, 32, 64, 16]),
            c_re=di('c_re', [DEPTH, 32, 16, 64]), c_im=di('c_im', [DEPTH, 32, 16, 64]),
            s5_d=di('s5_d', [DEPTH, 512]), w_glu=di('w_glu', [DEPTH, 512, 512]), b_glu=di('b_glu', [DEPTH, 512]),
            st_s5_re=di('st_s5_re', [DEPTH, NSEQ, 2048]), st_s5_im=di('st_s5_im', [DEPTH, NSEQ, 2048]),
            dn_conv_w=di('dn_conv_w', [DEPTH, 4, 1536]), dn_a_log=di('dn_a_log', [DEPTH, 4]), dn_dt_bias=di('dn_dt_bias', [DEPTH, 4]),
            dn_norm_w=di('dn_norm_w', [DEPTH, 128]),
            st_dn_conv=di('st_dn_conv', [DEPTH, NSEQ, 3, 1536]), st_dn_s=di('st_dn_s', [DEPTH, NSEQ, 4, 128, 128]),
            ml_bias_i=di('ml_bias_i', [DEPTH, 4]), ml_bias_f=di('ml_bias_f', [DEPTH, 4]), ml_norm_w=di('ml_norm_w', [DEPTH, 128]),
            st_ml_c=di('st_ml_c', [DEPTH, NSEQ, 4, 128, 128]), st_ml_n=di('st_ml_n', [DEPTH, NSEQ, 4, 128]), st_ml_m=di('st_ml_m', [DEPTH, NSEQ, 4]),
        )
        self.O = dict(
            p_s5_re=self.dram_out('p_s5_re', [DEPTH, 16, 128]), p_s5_im=self.dram_out('p_s5_im', [DEPTH, 16, 128]),
            s_s5_re=self.dram_out('s_s5_re', [DEPTH, NSEQ, 2048]), s_s5_im=self.dram_out('s_s5_im', [DEPTH, NSEQ, 2048]),
            p_dn_conv=self.dram_out('p_dn_conv', [DEPTH, 3, 1536]), p_dn_s=self.dram_out('p_dn_s', [DEPTH, 4, 128, 128]),
            s_dn_conv=self.dram_out('s_dn_conv', [DEPTH, NSEQ, 3, 1536]), s_dn_s=self.dram_out('s_dn_s', [DEPTH, NSEQ, 4, 128, 128]),
            p_ml_c=self.dram_out('p_ml_c', [DEPTH, 4, 128, 128]), p_ml_n=self.dram_out('p_ml_n', [DEPTH, 4, 128]), p_ml_m=self.dram_out('p_ml_m', [DEPTH, 4]),
            s_ml_c=self.dram_out('s_ml_c', [DEPTH, NSEQ, 4, 128, 128]), s_ml_n=self.dram_out('s_ml_n', [DEPTH, NSEQ, 4, 128]),
            s_ml_m=self.dram_out('s_ml_m', [DEPTH, NSEQ, 4]),
        )
        y_p = self.dram_out('y_p', [TPR, D])
        y_s = self.dram_out('y_s', [TS, D])
        xT_d = nc.dram_tensor('xT_d', [D, TT], F32, kind="Internal").ap()
        xT_v = xT_d.rearrange("(c p) t -> p c t", p=128)
        if self.dbg:
            self.dbg_o = self.dram_out('dbg', self.dbg[1])

        self.consts()
        self.stage_small = self.sb([128, 128], F32, 'stsm')
        nw = self.sb([128, DEPTH * 8], F32, 'nw')
        self.load_cols(norm_w.rearrange("l (c p) -> (l c) p", p=128), DEPTH * 8, nw[:, :], 'nw')
        fw = self.sb([128, 8], F32, 'fw')
        self.load_cols(fnw.rearrange("l (c p) -> (l c) p", p=128), 8, fw[:, :], 'fw')
        bg = self.sb([128, DEPTH * 24], F32, 'bg')
        self.load_cols(b_gate.rearrange("l (c p) -> (l c) p", p=128), DEPTH * 24, bg[:, :], 'bg')
        self.dcol = self.sb([128, DEPTH * 4], F32, 'dcol')
        self.load_cols(self.P['s5_d'].rearrange("l (c p) -> (l c) p", p=128), DEPTH * 4, self.dcol[:, :], 'dcol')
        self.bglu = self.sb([128, DEPTH * 4], F32, 'bglu')
        self.load_cols(self.P['b_glu'].rearrange("l (c p) -> (l c) p", p=128), DEPTH * 4, self.bglu[:, :], 'bglu')

        self.cwT = self.sb([128, DEPTH * 48], F32, 'cwT')
        cwv = self.P['dn_conv_w'].rearrange("l j (c p) -> (l j c) p", p=128)
        self.load_cols(cwv[0:96], 96, self.cwT[:, 0:96], 'cwT')
        self.load_cols(cwv[96:192], 96, self.cwT[:, 96:192], 'cwT')
        self.mlw = self.sb([128, DEPTH], F32, 'mlw')
        self.load_cols(self.P['ml_norm_w'], DEPTH, self.mlw[:, :], 'mlw')
        self.dnw = self.sb([128, DEPTH], F32, 'dnw')
        self.load_cols(self.P['dn_norm_w'], DEPTH, self.dnw[:, :], 'dnw')
        self.hT = hT = self.sb([128, 8, TT], BF16, 'hT')
        self.wst = [self.sb([128, 8, 128], F32, 'wst%d' % i) for i in range(2)]
        self.wbf = [self.sb([128, 8, 128], BF16, 'wbf%d' % i) for i in range(4)]
        self.ar = ar = self.sb([128, ARENA], F32, 'arena')
        obT = []
        for b in range(3):
            obT.append(ar[:, b * OB_W:(b + 1) * OB_W].bitcast(BF16).rearrange("p (c t) -> p c t", c=4))
        self.obT = obT
        cv = Carver(ar, M0, ARENA)
        mixed = cv.bf16(8, TT)
        acc = cv.f32(TT)
        xb = cv.f32(8, 512)
        sq = cv.f32(2, 512)
        rs = cv.f32(512)
        gsb = cv.f32(512)
        NXR = 6
        xr = [cv.f32(512) for _ in range(NXR)]
        xin2 = [cv.f32(D) for _ in range(2)]
        xo2 = [cv.f32(8, 128) for _ in range(2)]
        yT = xo2[0]
        yo = xin2

        srcs = [(meta, NMETA, 0)] + [(xp[i * 128:(i + 1) * 128, :], 128, NMETA + 128 * i) for i in range(16)] \
            + [(xs, 128, TP)]
        idt = self.ident
        self.dma(xin2[0][0:srcs[0][1], :], srcs[0][0], w=[('xin', 0)])
        for ti, (src, n, c0) in enumerate(srcs):
            bb = ti % 2
            xin, xo = xin2[bb], xo2[bb]
            if ti + 1 < len(srcs):
                self.dma(xin2[1 - bb][0:srcs[ti + 1][1], :], srcs[ti + 1][0], w=[('xin', 1 - bb)])
            for half in range(2):
                pst, pk = self.ps()
                for q in range(4):
                    fc = half * 4 + q
                    self.tr(pst[:, q * 128:q * 128 + n], xin[0:n, fc * 128:(fc + 1) * 128], idt[0:n, 0:n],
                            r=[('xin', bb), 'ident'], w=[pk])
                src_ps = pst[:, :].rearrange("p (q t) -> p q t", q=4)[:, :, 0:n]
                dst = xo[:, half * 4:half * 4 + 4, 0:n]
                self.cp('act' if half == 0 else 'dve', dst, src_ps, r=[pk], w=[('xo', bb, half)])
            self.dma(xT_v[:, :, c0:c0 + n], xo[:, :, 0:n], r=[('xo', bb, 0), ('xo', bb, 1)], w=[('xT', ti)])
        xT_keys = [('xT', ti) for ti in range(len(srcs))]

        sqh = sq.rearrange("p a n -> p (a n)").bitcast(BF16).rearrange("p (a n) -> p a n", a=4)

        def rstd_of(x_, n, xkey):
            pst, pk = self.ps()
            on = self.onesb
            for fc in range(8):
                self.actf(sqh[:, fc % 4, 0:n], x_[:, fc, 0:n], AF.Square, r=[xkey], w=[('sq', fc % 4)])
                self.mm(pst[:, 0:n], on[:, :], sqh[:, fc % 4, 0:n], fc == 0, fc == 7, r=[('sq', fc % 4), 'onesb'], w=[pk])
            self.actf(rs[:, 0:n], pst[:, 0:n], AF.Ln, bias=self.epsb[:, 0:1], scale=1.0 / D, r=[pk, 'epsb'], w=['rs'])
            self.actf(rs[:, 0:n], rs[:, 0:n], AF.Exp, scale=-0.5, r=['rs'], w=['rs'])

        for l in range(NL):
            for gi, (c0, n) in enumerate(TGS):
                rk = (xT_keys if l == 0 else []) + [('xTf', fo_, gi) for fo_ in range(8)]
                self.dma(xb[:, :, 0:n], xT_v[:, :, c0:c0 + n], r=rk, w=['xb'])
                rstd_of(xb, n, 'xb')
                for fc in range(8):
                    self.stt(hT[:, fc, c0:c0 + n], xb[:, fc, 0:n], nw[:, l * 8 + fc:l * 8 + fc + 1], rs[:, 0:n],
                             ALU.mult, ALU.mult, r=['xb', 'rs', 'nw'], w=[('hT', fc, gi)])
            self.s.barrier()
            self.s5_phase(l)
            self.s.barrier()
            self.dn_phase(l)
            self.s.barrier()
            self.ml_phase(l)
            self.s.barrier()
            if self.dbg and self.dbg[0].startswith('obT') and l == self.dbg[2]:
                b = int(self.dbg[0][3])
                cvd = Carver(ar, M0, ARENA)
                hf = cvd.f32(4, TT)
                self.cp('dve', hf, obT[b], w=['hf'])
                self.dma(self.dbg_o.rearrange("(c p) t -> p c t", p=128), hf, r=['hf'])
                self.s.barrier()

            for fo in range(8):
                for b in range(3):
                    wg, wgk = self.wblock(w_in[l][:, C_GATE + b * D + fo * 128: C_GATE + b * D + fo * 128 + 128], 8, 128)
                    wp, wpk = self.wblock(w_br[b][l][:, fo * 128:(fo + 1) * 128], 4, 128)
                    for gi, (c0, n) in enumerate(TGS):
                        pg, pgk = self.ps()
                        for fc in range(8):
                            self.mm(pg[:, 0:n], wg[:, fc, :], hT[:, fc, c0:c0 + n], fc == 0, fc == 7,
                                    r=[wgk, ('hT', fc, gi)], w=[pgk])
                        pp, ppk = self.ps()
                        for kc in range(4):
                            self.mm(pp[:, 0:n], wp[:, kc, :], obT[b][:, kc, c0:c0 + n], kc == 0, kc == 3,
                                    r=[wpk, ('obT', b)], w=[ppk])
                        bcol = l * 24 + b * 8 + fo
                        self.actf(gsb[:, 0:n], pg[:, 0:n], AF.Sigmoid, bias=bg[:, bcol:bcol + 1], r=[pgk, 'bg'], w=['gsb'])
                        if b == 0:
                            self.tt('dve', acc[:, c0:c0 + n], gsb[:, 0:n], pp[:, 0:n], ALU.mult, r=['gsb', ppk], w=[('acc', gi)])
                        else:
                            self.tt('dve', gsb[:, 0:n], gsb[:, 0:n], pp[:, 0:n], ALU.mult, r=['gsb', ppk], w=['gsb'])
                            if b == 1:
                                self.tt('pool', acc[:, c0:c0 + n], acc[:, c0:c0 + n], gsb[:, 0:n], ALU.add,
                                        r=['gsb', ('acc', gi)], w=[('acc', gi)])
                            else:
                                self.tt('pool', mixed[:, fo, c0:c0 + n], acc[:, c0:c0 + n], gsb[:, 0:n], ALU.add,
                                        r=['gsb', ('acc', gi)], w=[('mixed', fo, gi)])
            wits = [(fo, gi, c0, n) for fo in range(8) for gi, (c0, n) in enumerate(TGS)]

            def xload(j):
                fo, gi, c0, n = wits[j]
                i = j % NXR
                rk = (xT_keys if l == 0 else []) + [('xTf', fo, gi)]
                self.dma(xr[i][:, 0:n], xT_d[fo * 128:(fo + 1) * 128, c0:c0 + n], r=rk, w=[('xr', i)])
            for j in range(3):
                xload(j)
            wo = wok = None
            for j, (fo, gi, c0, n) in enumerate(wits):
                if gi == 0:
                    wo, wok = self.wblock(w_out[l][:, fo * 128:(fo + 1) * 128], 8, 128)
                po, pok = self.ps()
                for kc in range(8):
                    self.mm(po[:, 0:n], wo[:, kc, :], mixed[:, kc, c0:c0 + n], kc == 0, kc == 7,
                            r=[wok, ('mixed', kc, gi)], w=[pok])
                i = j % NXR
                xr_ = xr[i]
                self.tt('dve', xr_[:, 0:n], xr_[:, 0:n], po[:, 0:n], ALU.add, r=[('xr', i), pok], w=[('xr', i)])
                if j + 3 < len(wits):
                    xload(j + 3)
                self.dma(xT_d[fo * 128:(fo + 1) * 128, c0:c0 + n], xr_[:, 0:n], r=[('xr', i)], w=[('xTf', fo, gi)])
            self.s.barrier()

        outs = [(y_p[i * 128:(i + 1) * 128, :], NMETA + 128 * i) for i in range(16)] + [(y_s, TP)]
        allx = xT_keys if NL == 0 else [('xTf', fo, gi) for fo in range(8) for gi in range(5)]
        self.s.barrier()
        self.dma(xb[:, :, 0:128], xT_v[:, :, outs[0][1]:outs[0][1] + 128], r=allx, w=[('xbh', 0)])
        for ti, (dst, c0) in enumerate(outs):
            b = ti % 2
            n = 128
            xbh = xb[:, :, 128 * b:128 * b + 128]
            if ti + 1 < len(outs):
                c1_ = outs[ti + 1][1]
                self.dma(xb[:, :, 128 * (1 - b):128 * (1 - b) + 128], xT_v[:, :, c1_:c1_ + 128], r=allx, w=[('xbh', 1 - b)])
            pst, pk = self.ps()
            on = self.onesb
            for fc in range(8):
                self.actf(sqh[:, fc % 4, 0:n], xbh[:, fc, :], AF.Square, r=[('xbh', b)], w=[('sq', fc % 4)])
                self.mm(pst[:, 0:n], on[:, :], sqh[:, fc % 4, 0:n], fc == 0, fc == 7, r=[('sq', fc % 4), 'onesb'], w=[pk])
            self.actf(rs[:, 0:n], pst[:, 0:n], AF.Ln, bias=self.epsb[:, 0:1], scale=1.0 / D, r=[pk, 'epsb'], w=['rs'])
            self.actf(rs[:, 0:n], rs[:, 0:n], AF.Exp, scale=-0.5, r=['rs'], w=['rs'])
            for fc in range(8):
                self.stt(yT[:, fc, :], xbh[:, fc, :], fw[:, fc:fc + 1], rs[:, 0:128], ALU.mult, ALU.mult,
                         r=[('xbh', b), 'rs', 'fw'], w=[('yT', fc)])
            yo_ = yo[b]
            for half in range(2):
                pst, pk = self.ps()
                for q in range(4):
                    fc = half * 4 + q
                    self.tr(pst[:, q * 128:(q + 1) * 128], yT[:, fc, :], idt[:, :], r=[('yT', fc), 'ident'], w=[pk])
                self.cp('act' if half == 0 else 'dve', yo_[:, half * 512:(half + 1) * 512], pst[:, :], r=[pk], w=[('yo', b, half)])
            self.dma(dst, yo_[:, :], r=[('yo', b, 0), ('yo', b, 1)])

        self.s.emit(nc, self.es)
        self.es.close()
        return nc

    def dump(self, name, ap, rkeys):
        if not self.dbg or self.dbg[0] != 'dump':
            return
        o = self.dram_out('D_' + name, list(ap.shape))
        self.dma(o, ap, r=rkeys)

    def zero_ob(self, b):
        o = self.obT[b]
        self.memset('pool', o, 0.0, w=[('obT', b)])

    def build_masks(self, cv, key):
        on, idt = self.ones, self.ident
        k = lambda nm: (key, nm)
        m = {}
        GE, EQ = ALU.is_ge, ALU.is_equal

        def asel(dst, src, pattern, op, base, cm, r, w):
            self.op('pool', lambda e: e.affine_select(dst, src, pattern, op, 0.0, base=base, channel_multiplier=cm), r=r, w=w)
        for nm, pat, base, cm in [('tril', [[-1, 128]], 0, 1), ('trils', [[-1, 128]], -1, 1),
                                  ('triu', [[1, 128]], 0, -1), ('trius', [[1, 128]], -1, -1)]:
            m[nm] = cv.f32(128)
            asel(m[nm], on[:, :], pat, GE, base, cm, ['ones'], [k(nm)])
        E = cv.f32(128)
        E7 = cv.f32(128)
        tmpE = cv.f32(128)
        asel(tmpE[0:16, :], on[0:16, :], [[1, 128]], GE, 0, -8, ['ones'], [k('tmpE')])
        asel(E[0:16, :], tmpE[0:16, :], [[-1, 128]], GE, 7, 8, [k('tmpE')], [k('E')])
        asel(E7[0:16, :], on[0:16, :], [[1, 128]], EQ, -7, -8, ['ones'], [k('E7')])
        pst, pk = self.ps()
        self.mm(pst[:, 0:128], E[0:16, :], E[0:16, :], True, True, r=[k('E')], w=[pk])
        self.mm(pst[:, 128:256], E7[0:16, :], E[0:16, :], True, True, r=[k('E'), k('E7')], w=[pk])
        self.tr(pst[:, 256:272], E[0:16, :], idt[0:16, 0:16], r=[k('E'), 'ident'], w=[pk])
        for nm in ['tril', 'trils', 'triu', 'trius']:
            m[nm + '_s'] = cv.f32(128)
            self.tt('dve', m[nm + '_s'], m[nm], pst[:, 0:128], ALU.mult, r=[k(nm), pk], w=[k(nm + '_s')])
        m['sel_s'] = cv.f32(128)
        self.cp('dve', m['sel_s'], pst[:, 128:256], r=[pk], w=[k('sel_s')])
        m['ET'] = cv.f32(16)
        self.cp('dve', m['ET'], pst[:, 256:272], r=[pk], w=[k('ET')])
        m['sel_p'] = cv.f32(128)
        asel(m['sel_p'], on[:, :], [[0, 128]], EQ, -127, 1, ['ones'], [k('sel_p')])
        m['sel_m'] = cv.f32(16)
        asel(m['sel_m'][0:16, :], on[0:16, 0:16], [[0, 16]], EQ, -15, 1, ['ones'], [k('sel_m')])
        m['id16'] = cv.f32(16, 16)
        self.memset('pool', m['id16'], 1.0, w=[k('id16')])
        asel(m['id16'], m['id16'], [[1, 16], [-1, 16]], EQ, 0, 0, [k('id16')], [k('id16')])
        m['selh'] = cv.f32(4, 128)
        asel(m['selh'][0:4, :, :], on[0:4, :].unsqueeze(1).to_broadcast([4, 4, 128]), [[1, 4], [0, 128]], EQ, 0, -1, ['ones'], [k('selh')])
        m['keys'] = [k(n_) for n_ in ['tril', 'trils', 'triu', 'trius', 'tril_s', 'trils_s', 'triu_s', 'trius_s',
                                      'sel_s', 'sel_p', 'sel_m', 'ET', 'id16', 'selh']]
        return m

    def dn_phase(self, l):
        if 'dn' not in self.phases:
            return self.zero_ob(0)
        P, O = self.P, self.O
        ar, hT = self.ar, self.hT
        ob0 = self.obT[0]
        idt, on, idb, onb = self.ident, self.ones, self.identb, self.onesb
        MUL, ADD, SUB = ALU.mult, ALU.add, ALU.subtract
        cvA = Carver(ar, OB_W, 2 * OB_W)
        cv = Carver(ar, M0, ARENA)
        S_ = 'dn'
        k = lambda nm: (S_, nm)
        qkv = cv.bf16(12, TT)
        G = cvA.f32(TT)
        BETA = cvA.f32(TT)
        m = self.build_masks(cv, S_)
        MK = m['keys']
        wlo = cv.p
        raw_p = cv.f32(TP + 3)
        raw_s = cv.f32(16, 11)
        acc = cv.f32(TT)
        off_cres = cv.p
        cres = cv.f32(TT)
        GT = cv.f32(TT)
        hst = cv.f32(128)
        t48 = cv.f32(48)
        sqb = cv.bf16(512)
        rtmp = cv.f32(512)
        cst = cv.f32(128)
        dnsc = cv.f32(8)
        self.memset('pool', raw_p[:, 0:3], 0.0, w=[k('raw_p0')])
        self.memset('pool', cres[0:4, :], 1.0, w=[k('cres')])
        self.memset('pool', cres[0:4, 0:1], 0.0, w=[k('cres')])
        self.memset('pool', cres[0:4, 16:16 + 2048].rearrange("p (c t) -> p c t", t=128)[:, :, 0:1], 0.0, w=[k('cres')])
        self.memset('pool', cres[0:4, TP:TT].rearrange("p (c t) -> p c t", t=8)[:, :, 0:1], 0.0, w=[k('cres')])
        self.dma(dnsc[0:4, 0:1], P['dn_dt_bias'][l:l + 1, :].rearrange("o h -> h o"), w=[k('dnsc0')])
        self.dma(dnsc[0:4, 1:2], P['dn_a_log'][l:l + 1, :].rearrange("o h -> h o"), w=[k('dnsc1')])
        self.actf(dnsc[0:4, 2:3], dnsc[0:4, 1:2], AF.Exp, r=[k('dnsc1')], w=[k('dnsc2')])
        self.ts('dve', dnsc[0:4, 3:4], dnsc[0:4, 2:3], -1.0, MUL, r=[k('dnsc2')], w=[k('dnsc3')])

        def cons_a(gi, c0, n, pst, pk):
            self.actf(GT[0:4, c0:c0 + n], pst[0:4, 0:n], AF.Exp, bias=dnsc[0:4, 0:1], r=[pk, k('dnsc0')], w=[k('GT')])
        self.proj(l, C_DNA, 4, cons_a)
        self.actf(GT[0:4, :], GT[0:4, :], AF.Ln, bias=1.0, r=[k('GT')], w=[k('GT')])
        self.ts('dve', GT[0:4, :], GT[0:4, :], dnsc[0:4, 3:4], MUL, r=[k('GT'), k('dnsc3')], w=[k('GT')])
        self.scan(G[0:4, :], cres[0:4, :], GT[0:4, :], 0.0, r=[k('GT'), k('cres')], w=[k('G')])

        def cons_b(gi, c0, n, pst, pk):
            self.actf(BETA[0:4, c0:c0 + n], pst[0:4, 0:n], AF.Sigmoid, r=[pk], w=[k('BETA')])
        self.proj(l, C_DNB, 4, cons_b)
        cw = self.cwT
        self.s.barrier()
        cb = Carver(ar, off_cres, off_cres + 2 * TT)
        RP = [raw_p, cb.f32(TP + 3)]
        RS = [raw_s, cb.f32(16, 11)]
        SQ = [sqb, cb.bf16(512)]
        RT = [rtmp, cb.f32(512)]
        self.memset('pool', RP[1][:, 0:3], 0.0, w=[k('raw_p0')])
        for which in range(3):
            for h in range(4):
                fb = which * 4 + h
                bb = fb % 2
                raw_p, raw_s = RP[bb], RS[bb]
                k = lambda nm, bb=bb: (S_, nm + (str(bb) if nm.startswith('raw') else ''))
                self.dma(hst[0:48, :], P['st_dn_conv'][l][:, :, fb * 128:(fb + 1) * 128].rearrange("s j c -> (s j) c"), w=[k('hst')])
                pst, pk = self.ps()
                self.tr(pst[:, 0:48], hst[0:48, :], idt[0:48, 0:48], r=[k('hst'), 'ident'], w=[pk])
                self.cp('dve', raw_s[:, :, 0:3], pst[:, 0:48].rearrange("p (s j) -> p s j", j=3), r=[pk], w=[k('raw_sh')])

                def cons(gi, c0, n, pst, pk):
                    npr = min(c0 + n, TP) - c0
                    self.cp('act', raw_p[:, 3 + c0:3 + c0 + npr], pst[:, 0:npr], r=[pk], w=[k('raw_p%d' % gi)])
                    if gi == 4:
                        self.cp('act', raw_s[:, :, 3:11], pst[:, npr:npr + 128].rearrange("p (s t) -> p s t", t=8), r=[pk], w=[k('raw_s')])
                self.proj(l, C_DNQKV + which * 512 + h * 128, 128, cons)
                rall = [k('raw_p%d' % gi) for gi in range(5)] + [k('raw_p0'), k('raw_s'), k('raw_sh')]
                accs = acc[:, TP:TT].rearrange("p (s t) -> p s t", t=8)
                for j in range(4):
                    wc = cw[:, l * 48 + j * 12 + fb:l * 48 + j * 12 + fb + 1]
                    if j == 0:
                        self.op('act', lambda e, raw_p=raw_p, wc=wc: e.mul(acc[:, 0:TP], raw_p[:, 0:TP], wc), r=rall + ['cwT'], w=[k('acc')])
                        self.ts('pool', accs, raw_s[:, :, 0:8], wc, MUL, r=rall + ['cwT'], w=[k('accs')])
                    else:
                        self.stt(acc[:, 0:TP], raw_p[:, j:j + TP], wc, acc[:, 0:TP], MUL, ADD, r=rall + ['cwT', k('acc')], w=[k('acc')])
                        self.stt(accs, raw_s[:, :, j:j + 8], wc, accs, MUL, ADD, r=rall + ['cwT', k('accs')], w=[k('accs')])
                pst, pk = self.ps()
                self.tr(pst[0:3, 0:128], raw_p[:, TP:TP + 3], idt[:, :], r=rall + ['ident'], w=[pk])
                self.cp('dve', cst[0:3, :], pst[0:3, 0:128], r=[pk], w=[k('cst')])
                self.dma(O['p_dn_conv'][l][:, fb * 128:(fb + 1) * 128], cst[0:3, :], r=[k('cst')], q='pool')
                self.cp('pool', t48.rearrange("p (s j) -> p s j", j=3), raw_s[:, :, 8:11], r=rall, w=[k('t48')])
                pst, pk = self.ps()
                self.tr(pst[0:48, 0:128], t48, idt[:, :], r=[k('t48'), 'ident'], w=[pk])
                self.cp('dve', cst[0:48, :], pst[0:48, 0:128], r=[pk], w=[k('cst')])
                self.dma(O['s_dn_conv'][l][:, :, fb * 128:(fb + 1) * 128].rearrange("s j c -> (s j) c"), cst[0:48, :], r=[k('cst')], q='pool')
                dstq = qkv[:, fb, :]
                if which == 2:
                    self.actf(dstq, acc[:, :], AF.Silu, r=[k('acc'), k('accs')], w=[k('qkv')])
                else:
                    self.actf(acc[:, :], acc[:, :], AF.Silu, r=[k('acc'), k('accs')], w=[k('acc'), k('accs')])
                    for gi, (c0, n) in enumerate(TGS):
                        sqb, rtmp = SQ[gi % 2], RT[gi % 2]
                        sqk, rtk = k('sqb%d' % (gi % 2)), k('rtmp%d' % (gi % 2))
                        self.actf(sqb[:, 0:n], acc[:, c0:c0 + n], AF.Square, r=[k('acc'), k('accs')], w=[sqk])
                        pst, pk = self.ps()
                        self.mm(pst[:, 0:n], onb[:, :], sqb[:, 0:n], True, True, r=[sqk, 'onesb'], w=[pk])
                        self.actf(rtmp[:, 0:n], pst[:, 0:n], AF.Ln, bias=self.epsb[:, 0:1], r=[pk, 'epsb'], w=[rtk])
                        self.actf(rtmp[:, 0:n], rtmp[:, 0:n], AF.Exp, scale=-0.5, r=[rtk], w=[rtk])
                        sc = (128.0 ** -0.5) if which == 0 else 1.0
                        self.stt(qkv[:, fb, c0:c0 + n], acc[:, c0:c0 + n], sc, rtmp[:, 0:n], MUL, MUL,
                                 r=[k('acc'), k('accs'), rtk], w=[k('qkv%d' % gi)])
        k = lambda nm: (S_, nm)
        rtmp = RT[0]
        self.s.barrier()
        cv.p = wlo
        c2 = cvA
        W = {}
        for nm in ['E1a', 'E1b', 'eGbc', 'Pm', 'PTm', 'Zf']:
            W[nm] = cv.f32(4, 128)
        W['rstd'] = W['E1a']
        W['nbb'] = W['Zf']
        for nm in ['Ds', 'DTm', 'DTs', 'Zb', 'nKbG', 'Kdec', 'Vb', 'nWT', 'vnew', 'attnT', 'qG', 'sqv', 'Sb']:
            W[nm] = cv.bf16(4, 128)
        W['S'] = cv.f32(4, 128)
        cols = cv.f32(8, 4)
        Sseg = cv.f32(8, 128)
        Ssegb = cv.bf16(8, 128)
        nWTs = cv.bf16(8, 128)
        qGs = cv.bf16(8, 128)
        Kds = cv.bf16(8, 128)
        dnw = self.dnw
        self.memset('dve', W['S'], 0.0, w=[k('S')])
        self.memset('pool', W['Sb'], 0.0, w=[k('Sb')])
        CH = [(0, 16, 'm')] + [(16 + 128 * i, 128, 'p') for i in range(16)] + [(TP, 128, 's')]
        Wfull = W
        cols2 = [cols, cv.f32(8, 4)]

        def chunk(c0, L, kind, h0, NH, cols, kt):
            W = {nm: Wfull[nm][:, h0:h0 + NH, :] for nm in Wfull}
            kk = lambda nm: k(nm + kt)
            F32R = mybir.dt.float32r
            Wo = dict(W)
            if self.use_f32r:
                for nm_ in ('Pm', 'PTm', 'Zf'):
                    Wo[nm_] = W[nm_].bitcast(F32R)
            sfx = '_s' if kind == 's' else ''
            nlev = {'m': 3, 'p': 6, 's': 2}[kind]
            selm = {'m': m['sel_m'][0:16, 0:16], 'p': m['sel_p'], 's': m['sel_s']}[kind]
            v3 = lambda t: t[0:L, :, 0:L]
            vf = lambda t: t[:, :, 0:L]
            vt = lambda t: t[0:L, :, :]
            hv = lambda p_, rows: p_[rows, 0:NH * L].rearrange("p (h t) -> p h t", h=NH)
            hd = lambda p_, rows: p_[rows, 0:NH * 128].rearrange("p (h d) -> p h d", h=NH)
            bview = lambda p_: p_[:, :].bitcast(BF16)
            mk = lambda nm: m[nm + sfx][0:L, 0:L].unsqueeze(1).to_broadcast([L, NH, L])
            AL = slice(0, L)
            FU = slice(0, 128)
            pc, pck = self.ps()
            self.tr(pc[0:L, 0:4], G[0:4, c0:c0 + L], idt[0:4, 0:4], r=[k('G'), 'ident'], w=[pck])
            self.tr(pc[0:L, 4:8], BETA[0:4, c0:c0 + L], idt[0:4, 0:4], r=[k('BETA'), 'ident'], w=[pck])
            Gcol, bcol, nGcol, nbeG, edec, glc, nbcol = [cols[0:L, i, h0:h0 + NH] for i in range(7)]
            ck = kk('cols')
            self.cp('dve', cols[0:L, 0:2, :], pc[0:L, 0:8].rearrange("p (a h) -> p a h", h=4), r=[pck], w=[ck])
            self.ts('dve', nGcol, Gcol, -1.0, MUL, r=[ck], w=[ck])
            self.ts('dve', nbcol, bcol, -1.0, MUL, r=[ck], w=[ck])
            pg, pgk = self.ps()
            self.mm(pg[0:L, 0:NH], selm, Gcol, True, True, r=[ck] + MK, w=[pgk])
            self.tt('dve', glc, pg[0:L, 0:NH], Gcol, SUB, r=[pgk, ck], w=[ck])
            self.actf(edec, glc, AF.Exp, r=[ck], w=[ck])
            self.actf(nbeG, Gcol, AF.Exp, r=[ck], w=[ck])
            self.tt('dve', nbeG, nbeG, nbcol, MUL, r=[ck], w=[ck])
            pm, pmk = self.ps()
            pbt, pbk = self.ps()
            for hh in range(NH):
                h = h0 + hh
                self.mm(pm[:, hh * L:(hh + 1) * L], m['selh'][0:4, h, :], G[0:4, c0:c0 + L], True, True, r=[k('G')] + MK, w=[pmk])
                self.mm(pbt[:, hh * L:(hh + 1) * L], m['selh'][0:4, h, :], BETA[0:4, c0:c0 + L], True, True, r=[k('BETA')] + MK, w=[pbk])
            pmv = hv(pm, FU)
            pbv = hv(pbt, FU)
            self.actf(W['eGbc'][:, :, 0:L], pmv, AF.Exp, r=[pmk], w=[kk('eGbc')])
            self.op('act', lambda e: e.mul(Wo['Zf'][0:L, :, 0:L], pbv[0:L], -1.0), r=[pbk], w=[kk('Zf')])
            for hh in range(NH):
                self.actf(W['E1a'][0:L, hh, 0:L], pmv[0:L, hh, :], AF.Exp, bias=Gcol[:, hh:hh + 1], scale=-1.0, r=[pmk, ck], w=[kk('E1a')])
                self.actf(W['E1b'][0:L, hh, 0:L], pmv[0:L, hh, :], AF.Exp, bias=nGcol[:, hh:hh + 1], scale=1.0, r=[pmk, ck], w=[kk('E1b')])
            self.stt(v3(W['Ds']), v3(W['E1a']), 1.0, mk('trils'), ALU.min, MUL, r=[kk('E1a')] + MK, w=[kk('Ds')])
            self.stt(v3(W['DTm']), v3(W['E1b']), 1.0, mk('triu'), ALU.min, MUL, r=[kk('E1b')] + MK, w=[kk('DTm')])
            self.stt(v3(W['DTs']), v3(W['E1b']), 1.0, mk('trius'), ALU.min, MUL, r=[kk('E1b')] + MK, w=[kk('DTs')])
            pkk, pkkk = self.ps()
            pqk, pqkk = self.ps()
            for hh in range(NH):
                h = h0 + hh
                kTh = qkv[:, 4 + h, c0:c0 + L]
                qTh = qkv[:, h, c0:c0 + L]
                self.mm(pkk[0:L, hh * L:(hh + 1) * L], kTh, kTh, True, True, r=[k('qkv')], w=[pkkk])
                self.mm(pqk[0:L, hh * L:(hh + 1) * L], kTh, qTh, True, True, r=[k('qkv')], w=[pqkk])
            pkv = hv(pkk, AL)
            pqv = hv(pqk, AL)
            self.tt('dve', v3(W['E1a']), pkv, v3(W['Ds']), MUL, r=[pkkk, kk('Ds')], w=[kk('E1a')])
            self.tt('dve', v3(Wo['Pm']), v3(W['E1a']), nbcol.unsqueeze(2).to_broadcast([L, NH, L]), MUL, r=[kk('E1a'), ck], w=[kk('Pm')])
            self.tt('dve', v3(W['E1b']), pkv, v3(W['DTs']), MUL, r=[pkkk, kk('DTs')], w=[kk('E1b')])
            self.tt('dve', v3(Wo['PTm']), v3(W['E1b']), v3(W['nbb']), MUL, r=[kk('E1b'), kk('Zf')], w=[kk('PTm')])
            self.tt('dve', v3(W['attnT']), pqv, v3(W['DTm']), MUL, r=[pqkk, kk('DTm')], w=[kk('attnT')])
            self.tt('dve', v3(Wo['Zf']), v3(W['PTm']), idt[0:L, 0:L].unsqueeze(1).to_broadcast([L, NH, L]), ADD, r=[kk('PTm'), 'ident'], w=[kk('Zf')])
            NF = 3
            for lev in range(nlev):
                if lev < NF:
                    Pa, PTa, Za, pk_, ptk_, zk_ = W['Pm'], W['PTm'], W['Zf'], kk('Pm'), kk('PTm'), kk('Zf')
                else:
                    Pa, PTa, Za, pk_, ptk_, zk_ = W['Ds'], W['DTs'], W['Zb'], kk('Ds'), kk('DTs'), kk('Zb')
                    if lev == NF:
                        self.cp('act', v3(W['Ds']), v3(W['Pm']), r=[kk('Pm')], w=[kk('Ds')])
                        self.cp('pool', v3(W['DTs']), v3(W['PTm']), r=[kk('PTm')], w=[kk('DTs')])
                    self.cp('act', v3(W['Zb']), v3(W['Zf']), r=[kk('Zf')], w=[kk('Zb')])
                pP, pPk = self.ps()
                pT, pTk = self.ps()
                for hh in range(NH):
                    self.mm(pP[0:L, hh * L:(hh + 1) * L], PTa[0:L, hh, 0:L], Pa[0:L, hh, 0:L], True, True, r=[pk_, ptk_], w=[pPk])
                    self.mm(pT[0:L, hh * L:(hh + 1) * L], Pa[0:L, hh, 0:L], PTa[0:L, hh, 0:L], True, True, r=[pk_, ptk_], w=[pTk])
                self.cp('act', v3(Pa), hv(pP, AL), r=[pPk], w=[pk_])
                self.cp('dve', v3(PTa), hv(pT, AL), r=[pTk], w=[ptk_])
                pZ, pZk = self.ps()
                for hh in range(NH):
                    self.mm(pZ[0:L, hh * L:(hh + 1) * L], Pa[0:L, hh, 0:L], Za[0:L, hh, 0:L], True, True, r=[pk_, zk_], w=[pZk])
                self.tt('dve', v3(W['Zf']), v3(W['Zf']), hv(pZ, AL), ADD, r=[kk('Zf'), pZk], w=[kk('Zf')])
            self.cp('act', v3(W['Zb']), v3(W['Zf']), r=[kk('Zf')], w=[kk('Zb')])
            ptk, ptkk = self.ps()
            ptv, ptvk = self.ps()
            for hh in range(NH):
                h = h0 + hh
                self.tr(bview(ptk)[0:L, hh * 128:(hh + 1) * 128], qkv[:, 4 + h, c0:c0 + L], idb[:, :], r=[k('qkv'), 'identb'], w=[ptkk])
                self.tr(bview(ptv)[0:L, hh * 128:(hh + 1) * 128], qkv[:, 8 + h, c0:c0 + L], idb[:, :], r=[k('qkv'), 'identb'], w=[ptvk])
            ktv = hd(bview(ptk), AL)
            vtv = hd(bview(ptv), AL)
            bc = lambda col: col.unsqueeze(2).to_broadcast([L, NH, 128])
            self.tt('dve', vt(W['nKbG']), ktv, bc(nbeG), MUL, r=[ptkk, ck], w=[kk('nKbG')])
            self.tt('dve', vt(W['Kdec']), ktv, bc(edec), MUL, r=[ptkk, ck], w=[kk('Kdec')])
            self.tt('dve', vt(W['Vb']), vtv, bc(bcol), MUL, r=[ptvk, ck], w=[kk('Vb')])
            pw, pwk = self.ps()
            for hh in range(NH):
                self.mm(pw[:, hh * L:(hh + 1) * L], W['nKbG'][0:L, hh, :], W['Zb'][0:L, hh, 0:L], True, True, r=[kk('nKbG'), kk('Zb')], w=[pwk])
            self.cp('act', vf(W['nWT']), hv(pw, FU), r=[pwk], w=[kk('nWT')])
            self.tt('dve', vf(W['qG']), qkv[:, h0:h0 + NH, c0:c0 + L], vf(W['eGbc']), MUL, r=[k('qkv'), kk('eGbc')], w=[kk('qG')])
            if kind != 's':
                pvn, pvnk = self.ps()
                for hh in range(NH):
                    self.mm(pvn[0:L, hh * 128:(hh + 1) * 128], W['Zb'][0:L, hh, 0:L], W['Vb'][0:L, hh, :], True, False, r=[kk('Zb'), kk('Vb')], w=[pvnk])
                    self.mm(pvn[0:L, hh * 128:(hh + 1) * 128], W['nWT'][:, hh, 0:L], W['Sb'][:, hh, :], False, True, r=[kk('nWT'), kk('Sb')], w=[pvnk])
                self.cp('act', vt(W['vnew']), hd(pvn, AL), r=[pvnk], w=[kk('vnew')])
                po, pok = self.ps()
                for hh in range(NH):
                    self.mm(po[:, hh * L:(hh + 1) * L], W['Sb'][:, hh, :], W['qG'][:, hh, 0:L], True, False, r=[kk('Sb'), kk('qG')], w=[pok])
                    self.mm(po[:, hh * L:(hh + 1) * L], W['vnew'][0:L, hh, :], W['attnT'][0:L, hh, 0:L], False, True, r=[kk('vnew'), kk('attnT')], w=[pok])
                pS, pSk = self.ps()
                for hh in range(NH):
                    self.mm(pS[:, hh * 128:(hh + 1) * 128], W['Kdec'][0:L, hh, :], W['vnew'][0:L, hh, :], True, True, r=[kk('Kdec'), kk('vnew')], w=[pSk])
                for hh in range(NH):
                    self.stt(W['S'][:, hh, :], W['S'][:, hh, :], W['eGbc'][:, hh, L - 1:L], pS[:, hh * 128:(hh + 1) * 128], MUL, ADD,
                             r=[kk('S'), kk('eGbc'), pSk], w=[kk('S')])
                self.cp('act', W['Sb'], W['S'], r=[kk('S')], w=[kk('Sb')])
            else:
                po, pok = self.ps(pin=True)
                id16 = m['id16']
                for h in range(4):
                    pvn, pvnk = self.ps()
                    self.mm(pvn[:, 0:128], W['Zb'][:, h, :], W['Vb'][:, h, :], True, False, r=[kk('Zb'), kk('Vb')], w=[pvnk])
                    for half in range(2):
                        mview = id16[:, 8 * half:8 * half + 8, :].unsqueeze(3).to_broadcast([128, 8, 16, 8])
                        self.tt('pool', nWTs.rearrange("p s (a t) -> p s a t", t=8),
                                W['nWT'][:, h, :].rearrange("p (a t) -> p a t", t=8).unsqueeze(1).to_broadcast([128, 8, 16, 8]),
                                mview, MUL, r=[kk('nWT')] + MK, w=[k('nWTs')])
                        self.tt('pool', qGs.rearrange("p s (a t) -> p s a t", t=8),
                                W['qG'][:, h, :].rearrange("p (a t) -> p a t", t=8).unsqueeze(1).to_broadcast([128, 8, 16, 8]),
                                mview, MUL, r=[kk('qG')] + MK, w=[k('qGs')])
                        self.dma(Sseg, P['st_dn_s'][l][8 * half:8 * half + 8, h].rearrange("s a b -> a s b"), w=[k('Sseg')])
                        self.cp('act', Ssegb, Sseg, r=[k('Sseg')], w=[k('Ssegb')])
                        for s8 in range(8):
                            s = 8 * half + s8
                            self.mm(pvn[:, 0:128], nWTs[:, s8, :], Ssegb[:, s8, :], False, (s == 15), r=[k('nWTs'), k('Ssegb')], w=[pvnk])
                            self.mm(po[:, h * 128:(h + 1) * 128], Ssegb[:, s8, :], qGs[:, s8, :], (s == 0), False, r=[k('qGs'), k('Ssegb')], w=[pok])
                    self.cp('act', W['vnew'][:, h, :], pvn[:, 0:128], r=[pvnk], w=[kk('vnew')])
                    self.mm(po[:, h * 128:(h + 1) * 128], W['vnew'][:, h, :], W['attnT'][:, h, :], False, True, r=[kk('vnew'), kk('attnT')], w=[pok])
                    for half in range(2):
                        self.tt('pool', Kds, W['Kdec'][:, h, :].unsqueeze(1).to_broadcast([128, 8, 128]),
                                m['ET'][:, 8 * half:8 * half + 8].unsqueeze(2).to_broadcast([128, 8, 128]), MUL, r=[kk('Kdec')] + MK, w=[k('Kds')])
                        self.dma(Sseg, P['st_dn_s'][l][8 * half:8 * half + 8, h].rearrange("s a b -> a s b"), w=[k('Sseg')])
                        for g2 in range(2):
                            pS, pSk = self.ps()
                            for s4 in range(4):
                                s8 = 4 * g2 + s4
                                self.mm(pS[:, s4 * 128:(s4 + 1) * 128], Kds[:, s8, :], W['vnew'][:, h, :], True, True, r=[k('Kds'), kk('vnew')], w=[pSk])
                            for s4 in range(4):
                                s8 = 4 * g2 + s4
                                s = 8 * half + s8
                                self.stt(Sseg[:, s8, :], Sseg[:, s8, :], W['eGbc'][:, h, 8 * s + 7:8 * s + 8], pS[:, s4 * 128:(s4 + 1) * 128], MUL, ADD,
                                         r=[k('Sseg'), kk('eGbc'), pSk], w=[k('Sseg')])
                        self.dma(O['s_dn_s'][l][8 * half:8 * half + 8, h].rearrange("s a b -> a s b"), Sseg, r=[k('Sseg')])
            pov = hv(po, FU)
            self.actf(vf(W['sqv']), pov, AF.Square, r=[pok], w=[kk('sqv')])
            pss, pssk = self.ps()
            for hh in range(NH):
                self.mm(pss[:, hh * L:(hh + 1) * L], onb[:, :], W['sqv'][:, hh, 0:L], True, True, r=[kk('sqv'), 'onesb'], w=[pssk])
            psv = hv(pss, FU)
            self.actf(vf(W['rstd']), psv, AF.Ln, bias=self.epsb[:, 0:1], scale=1.0 / 128, r=[pssk, 'epsb'], w=[kk('E1a')])
            self.actf(vf(W['rstd']), vf(W['rstd']), AF.Exp, scale=-0.5, r=[kk('E1a')], w=[kk('E1a')])
            self.stt(ob0[:, h0:h0 + NH, c0:c0 + L], pov, dnw[:, l:l + 1], vf(W['rstd']), MUL, MUL, r=[pok, kk('E1a'), 'dnw'], w=[kk('ob0')])
            self.unpin(pok)

        for (c0, L, kind) in CH:
            if kind == 's':
                chunk(c0, L, kind, 0, 4, cols2[0], '')
            else:
                self.interleave([lambda c0=c0, L=L, kind=kind: chunk(c0, L, kind, 0, 2, cols2[0], 'A'),
                                 lambda c0=c0, L=L, kind=kind: chunk(c0, L, kind, 2, 2, cols2[1], 'B')])
                if c0 + L == TP:
                    self.s.barrier()
                    self.dma(O['p_dn_s'][l].rearrange("h a b -> a h b"), Wfull['S'], w=[])
                    self.s.barrier()
        self.s.barrier()
        for h in range(4):
            def consz(gi, c0, n, pst, pk, h=h):
                self.actf(rtmp[:, 0:n], pst[:, 0:n], AF.Silu, r=[pk], w=[k('rtmp')])
                self.tt('dve', ob0[:, h, c0:c0 + n], ob0[:, h, c0:c0 + n], rtmp[:, 0:n], MUL, r=[k('rtmp'), k('ob0')], w=[k('ob0'), ('obT', 0)])
            self.proj(l, C_DNZ + h * 128, 128, consz)

    def ml_phase(self, l):
        if 'ml' not in self.phases:
            return self.zero_ob(1)
        P, O = self.P, self.O
        ar, hT = self.ar, self.hT
        ob1 = self.obT[1]
        idt, on, idb, onb = self.ident, self.ones, self.identb, self.onesb
        MUL, ADD, SUB, MAX = ALU.mult, ALU.add, ALU.subtract, ALU.max
        cv = Carver(ar, M0, ARENA)
        S_ = 'ml'
        k = lambda nm: (S_, nm)
        m = self.build_masks(cv, S_)
        MK = m['keys']
        ALPHA = cv.f32(TT)
        WI = cv.f32(TT)
        COLS = cv.f32(18, 12)
        Nseg = cv.f32(16, 4)
        Np = cv.f32(4)
        msc = cv.f32(8)
        minit = cv.f32(16)
        wlo = cv.p
        CH = [(0, 16, 'm')] + [(16 + 128 * i, 128, 'p') for i in range(16)] + [(TP, 128, 's')]
        IG, LF, B, M, GAMMA, MP, EM = [cv.f32(TT) for _ in range(7)]
        cres = cv.f32(TT)
        st64 = cv.f32(128)
        mo = cv.f32(16)
        self.memset('pool', cres[0:4, :], 1.0, w=[k('cres')])
        self.memset('pool', cres[0:4, 0:1], 0.0, w=[k('cres')])
        self.memset('pool', cres[0:4, 16:16 + 2048].rearrange("p (c t) -> p c t", t=128)[:, :, 0:1], 0.0, w=[k('cres')])
        self.memset('pool', cres[0:4, TP:TT].rearrange("p (c t) -> p c t", t=8)[:, :, 0:1], 0.0, w=[k('cres')])
        self.dma(msc[0:4, 0:1], P['ml_bias_i'][l:l + 1, :].rearrange("o h -> h o"), w=[k('msc0')])
        self.dma(msc[0:4, 1:2], P['ml_bias_f'][l:l + 1, :].rearrange("o h -> h o"), w=[k('msc1')])
        self.ts('dve', msc[0:4, 2:3], msc[0:4, 1:2], -1.0, MUL, r=[k('msc1')], w=[k('msc2')])
        self.dma(st64[0:16, 0:4], P['st_ml_m'][l], w=[k('st64')])
        pst, pk = self.ps()
        self.tr(pst[0:4, 0:16], st64[0:16, 0:4], idt[0:16, 0:16], r=[k('st64'), 'ident'], w=[pk])
        self.cp('dve', minit[0:4, :], pst[0:4, 0:16], r=[pk], w=[k('minit')])
        self.dma(st64[0:64, :], P['st_ml_n'][l].rearrange("s h d -> (s h) d"), r=[k('st64')], w=[k('st64')])
        pst, pk = self.ps()
        self.tr(pst[:, 0:64], st64[0:64, :], idt[0:64, 0:64], r=[k('st64'), 'ident'], w=[pk])
        self.cp('dve', Nseg, pst[:, 0:64].rearrange("p (s h) -> p s h", h=4), r=[pk], w=[k('Nseg')])

        def cons_i(gi, c0, n, pst, pk):
            self.actf(IG[0:4, c0:c0 + n], pst[0:4, 0:n], AF.Identity, bias=msc[0:4, 0:1], r=[pk, k('msc0')], w=[k('IG')])
        self.proj(l, C_MLI, 4, cons_i)

        def cons_f(gi, c0, n, pst, pk):
            self.actf(LF[0:4, c0:c0 + n], pst[0:4, 0:n], AF.Exp, bias=msc[0:4, 2:3], scale=-1.0, r=[pk, k('msc2')], w=[k('LF')])
        self.proj(l, C_MLF, 4, cons_f)
        self.actf(LF[0:4, :], LF[0:4, :], AF.Ln, bias=1.0, r=[k('LF')], w=[k('LF')])
        self.ts('dve', LF[0:4, :], LF[0:4, :], -1.0, MUL, r=[k('LF')], w=[k('LF')])
        self.scan(B[0:4, :], cres[0:4, :], LF[0:4, :], 0.0, r=[k('LF'), k('cres')], w=[k('B')])
        self.op('dve', lambda e: e.tensor_tensor_scan(M[0:4, 0:TP], LF[0:4, 0:TP], IG[0:4, 0:TP], -1e30, ADD, MAX),
                r=[k('LF'), k('IG')], w=[k('M')])
        for s in range(16):
            sl = slice(TP + 8 * s, TP + 8 * s + 8)
            self.op('dve', lambda e, sl=sl, s=s: e.tensor_tensor_scan(M[0:4, sl], LF[0:4, sl], IG[0:4, sl], minit[0:4, s:s + 1], ADD, MAX),
                    r=[k('LF'), k('IG'), k('minit')], w=[k('M')])
        self.tt('dve', ALPHA[0:4, :], B[0:4, :], M[0:4, :], SUB, r=[k('B'), k('M')], w=[k('ALPHA')])
        self.tt('pool', GAMMA[0:4, :], IG[0:4, :], B[0:4, :], SUB, r=[k('B'), k('IG')], w=[k('GAMMA')])
        self.actf(EM[0:4, :], M[0:4, :], AF.Exp, scale=-1.0, r=[k('M')], w=[k('EM')])
        for ci, (c0, L, kind) in enumerate(CH):
            if kind == 'm':
                self.memset('pool', MP[0:4, 0:L], -1e30, w=[k('MP')])
            elif kind == 'p':
                self.cp('pool', MP[0:4, c0:c0 + L], M[0:4, c0 - 1:c0].to_broadcast([4, L]), r=[k('M')], w=[k('MP')])
            else:
                self.cp('pool', MP[0:4, c0:c0 + L].rearrange("p (s t) -> p s t", t=8), minit[0:4, :].unsqueeze(2).to_broadcast([4, 16, 8]),
                        r=[k('minit')], w=[k('MP')])
        self.tt('dve', MP[0:4, :], MP[0:4, :], ALPHA[0:4, :], ADD, r=[k('MP'), k('ALPHA')], w=[k('MP')])
        self.actf(WI[0:4, :], MP[0:4, :], AF.Exp, r=[k('MP')], w=[k('WI')])
        for ci, (c0, L, kind) in enumerate(CH):
            pc, pck = self.ps()
            self.tr(pc[0:L, 0:4], GAMMA[0:4, c0:c0 + L], idt[0:4, 0:4], r=[k('GAMMA'), 'ident'], w=[pck])
            self.tr(pc[0:L, 4:8], EM[0:4, c0:c0 + L], idt[0:4, 0:4], r=[k('EM'), 'ident'], w=[pck])
            self.tr(pc[0:L, 8:12], ALPHA[0:4, c0:c0 + L], idt[0:4, 0:4], r=[k('ALPHA'), 'ident'], w=[pck])
            self.cp('dve', COLS[0:L, ci, :], pc[0:L, 0:12], r=[pck], w=[k('COLS')])
        self.dma(O['p_ml_m'][l:l + 1, :].rearrange("o h -> h o"), M[0:4, TP - 1:TP], r=[k('M')])
        self.cp('dve', mo[0:4, :], M[0:4, TP:TT].rearrange("p (s t) -> p s t", t=8)[:, :, 7], r=[k('M')], w=[k('mo')])
        pst, pk = self.ps()
        self.tr(pst[0:16, 0:4], mo[0:4, :], idt[0:4, 0:4], r=[k('mo'), 'ident'], w=[pk])
        self.cp('dve', st64[0:16, 0:4], pst[0:16, 0:4], r=[pk], w=[k('st64')])
        self.dma(O['s_ml_m'][l], st64[0:16, 0:4], r=[k('st64')])
        self.s.barrier()
        mlw = self.mlw
        for h0 in (0, 2):
            cv.p = wlo
            qkv = cv.bf16(6, TT)
            W = {}
            for nm in ['E1', 'hf', 'hq', 'wil']:
                W[nm] = cv.f32(2, 128)
            for nm in ['Wm', 'scT', 'qw', 'kws', 'hn', 'Cb']:
                W[nm] = cv.bf16(2, 128)
            v1 = cv.bf16(2, 129)
            C = cv.f32(2, 128)
            sc = cv.f32(16, 2)
            nb = cv.bf16(64)
            Cseg2 = [cv.f32(8, 128), cv.f32(8, 128)]
            Csegb2 = [cv.bf16(8, 128), cv.bf16(8, 128)]
            qws = cv.bf16(8, 128)
            kwss = cv.bf16(8, 128)
            for which in range(3):
                for hh in range(2):
                    def cons(gi, c0, n, pst, pk, which=which, hh=hh):
                        dst = qkv[:, which * 2 + hh, c0:c0 + n]
                        if which == 1:
                            self.op('act', lambda e: e.mul(dst, pst[:, 0:n], 128.0 ** -0.5), r=[pk], w=[k('qkv')])
                        else:
                            self.cp('act', dst, pst[:, 0:n], r=[pk], w=[k('qkv')])
                    self.proj(l, C_MLQKV + which * 512 + (h0 + hh) * 128, 128, cons)
            self.memset('pool', v1[:, :, 128:129], 1.0, w=[k('v1o')])
            self.memset('dve', C, 0.0, w=[k('C')])
            self.memset('pool', W['Cb'], 0.0, w=[k('Cb')])
            self.memset('dve', Np[:, h0:h0 + 2], 0.0, w=[k('Np')])
            self.memset('pool', nb[:, 0:2], 0.0, w=[k('nb')])
            for hh in range(2):
                self.cp('pool', nb[:, 2:34].rearrange("p (s h) -> p s h", h=2)[:, :, hh], Nseg[:, :, h0 + hh], r=[k('Nseg')], w=[k('nb')])
            Wfull = W

            def chunk(ci, c0, L, kind, hl, NH, kt):
                W = {nm: Wfull[nm][:, hl:hl + NH, :] for nm in Wfull}
                kk = lambda nm: k(nm + kt)
                v1_ = v1[:, hl:hl + NH, :]
                C_ = C[:, hl:hl + NH, :]
                sfx = '_s' if kind == 's' else ''
                selm = {'m': m['sel_m'][0:16, 0:16], 'p': m['sel_p'], 's': m['sel_s']}[kind]
                hg = h0 + hl
                gcol = COLS[0:L, ci, 0 + hg:0 + hg + NH]
                emcol = COLS[0:L, ci, 4 + hg:4 + hg + NH]
                scc = lambda i: sc[0:L, i, hl:hl + NH]
                v3 = lambda t: t[0:L, :, 0:L]
                vf = lambda t: t[:, :, 0:L]
                vt = lambda t: t[0:L, :, :]
                bview = lambda p_: p_[:, :].bitcast(BF16)
                hv = lambda p_, rows: p_[rows, 0:NH * L].rearrange("p (h t) -> p h t", h=NH)
                hd = lambda p_, rows: p_[rows, 0:NH * 128].rearrange("p (h d) -> p h d", h=NH)
                AL, FU = slice(0, L), slice(0, 128)
                pg, pgk = self.ps()
                self.mm(pg[0:L, 0:4], selm, COLS[0:L, ci, 8:12], True, True, r=[k('COLS')] + MK, w=[pgk])
                self.tt('dve', scc(0), pg[0:L, hg:hg + NH], gcol, ADD, r=[pgk, k('COLS')], w=[kk('sc0')])
                self.actf(scc(0), scc(0), AF.Exp, r=[kk('sc0')], w=[kk('sc0')])
                pm, pmk = self.ps()
                pwi, pwik = self.ps()
                pkq, pkqk = self.ps()
                for hh in range(NH):
                    h = hg + hh
                    self.mm(pm[:, hh * L:(hh + 1) * L], m['selh'][0:4, h, :], ALPHA[0:4, c0:c0 + L], True, True, r=[k('ALPHA')] + MK, w=[pmk])
                    self.mm(pwi[:, hh * L:(hh + 1) * L], m['selh'][0:4, h, :], WI[0:4, c0:c0 + L], True, True, r=[k('WI')] + MK, w=[pwik])
                    self.mm(pkq[0:L, hh * L:(hh + 1) * L], qkv[:, 2 + hl + hh, c0:c0 + L], qkv[:, hl + hh, c0:c0 + L], True, True, r=[k('qkv')], w=[pkqk])
                pmv = hv(pm, FU)
                pwv = hv(pwi, FU)
                pkv = hv(pkq, AL)
                for hh in range(NH):
                    self.actf(W['E1'][0:L, hh, 0:L], pmv[0:L, hh, :], AF.Exp, bias=gcol[:, hh:hh + 1], r=[pmk, k('COLS')], w=[kk('E1')])
                self.stt(v3(W['Wm']), v3(W['E1']), 1.0, m['triu' + sfx][0:L, 0:L].unsqueeze(1).to_broadcast([L, NH, L]), ALU.min, MUL,
                         r=[kk('E1')] + MK, w=[kk('Wm')])
                self.tt('dve', v3(W['scT']), pkv, v3(W['Wm']), MUL, r=[pkqk, kk('Wm')], w=[kk('scT')])
                self.tt('dve', vf(W['qw']), qkv[:, hl:hl + NH, c0:c0 + L], pwv, MUL, r=[k('qkv'), pwik], w=[kk('qw')])
                if kind == 's':
                    self.cp('act', W['wil'][:, :, 0:16], pwv.rearrange("p h (s t) -> p h s t", t=8)[:, :, :, 7], r=[pwik], w=[kk('wil')])
                else:
                    self.cp('act', W['wil'][:, :, 0:1], pwv[:, :, L - 1:L], r=[pwik], w=[kk('wil')])
                ptk, ptkk = self.ps()
                ptv, ptvk = self.ps()
                for hh in range(NH):
                    self.tr(bview(ptk)[0:L, hh * 128:(hh + 1) * 128], qkv[:, 2 + hl + hh, c0:c0 + L], idb[:, :], r=[k('qkv'), 'identb'], w=[ptkk])
                    self.tr(bview(ptv)[0:L, hh * 128:(hh + 1) * 128], qkv[:, 4 + hl + hh, c0:c0 + L], idb[:, :], r=[k('qkv'), 'identb'], w=[ptvk])
                ktv = hd(bview(ptk), AL)
                vtv = hd(bview(ptv), AL)
                self.tt('dve', vt(W['kws']), ktv, scc(0).unsqueeze(2).to_broadcast([L, NH, 128]), MUL, r=[ptkk, kk('sc0')], w=[kk('kws')])
                self.cp('act', v1_[0:L, :, 0:128], vtv, r=[ptvk], w=[kk('v1')])
                pn, pnk = self.ps(pin=True)
                pd, pdk = self.ps(pin=True)
                for hh in range(NH):
                    if kind != 's':
                        self.mm(pn[0:L, hh * 128:(hh + 1) * 128], W['qw'][:, hh, 0:L], W['Cb'][:, hh, :], True, False, r=[kk('qw'), kk('Cb')], w=[pnk])
                        self.mm(pn[0:L, hh * 128:(hh + 1) * 128], W['scT'][0:L, hh, 0:L], v1_[0:L, hh, 0:128], False, True, r=[kk('scT'), kk('v1')], w=[pnk])
                        self.mm(pd[0:L, hh:hh + 1], W['qw'][:, hh, 0:L], nb[:, hl + hh:hl + hh + 1], True, False, r=[kk('qw'), kk('nb')], w=[pdk])
                        self.mm(pd[0:L, hh:hh + 1], W['scT'][0:L, hh, 0:L], v1_[0:L, hh, 128:129], False, True, r=[kk('scT'), k('v1o')], w=[pdk])
                    else:
                        self.mm(pn[:, hh * 128:(hh + 1) * 128], W['scT'][:, hh, :], v1_[:, hh, 0:128], True, False, r=[kk('scT'), kk('v1')], w=[pnk])
                        self.mm(pd[:, hh:hh + 1], W['scT'][:, hh, :], v1_[:, hh, 128:129], True, False, r=[kk('scT'), k('v1o')], w=[pdk])
                        for half in range(2):
                            mview = m['id16'][:, 8 * half:8 * half + 8, :].unsqueeze(3).to_broadcast([128, 8, 16, 8])
                            self.tt('pool', qws.rearrange("p s (a t) -> p s a t", t=8),
                                    W['qw'][:, hh, :].rearrange("p (a t) -> p a t", t=8).unsqueeze(1).to_broadcast([128, 8, 16, 8]),
                                    mview, MUL, r=[kk('qw')] + MK, w=[k('qws')])
                            Cseg, Csegb = Cseg2[half], Csegb2[half]
                            self.dma(Cseg, P['st_ml_c'][l][8 * half:8 * half + 8, h0 + hh].rearrange("s a b -> a s b"), w=[k('Cseg%d' % half)])
                            self.cp('act', Csegb, Cseg, r=[k('Cseg%d' % half)], w=[k('Csegb%d' % half)])
                            for s8 in range(8):
                                s = 8 * half + s8
                                self.mm(pn[:, hh * 128:(hh + 1) * 128], qws[:, s8, :], Csegb[:, s8, :], False, (s == 15), r=[k('qws'), k('Csegb%d' % half)], w=[pnk])
                                self.mm(pd[:, hh:hh + 1], qws[:, s8, :], nb[:, 2 + s * 2 + hh:2 + s * 2 + hh + 1], False, (s == 15), r=[k('qws'), kk('nb')], w=[pdk])
                self.ts('dve', scc(3), pd[0:L, 0:NH], -1.0, MUL, r=[pdk], w=[kk('sc3')])
                self.tt('dve', scc(1), pd[0:L, 0:NH], scc(3), MAX, r=[pdk, kk('sc3')], w=[kk('sc1')])
                self.tt('dve', scc(1), scc(1), emcol, MAX, r=[kk('sc1'), k('COLS')], w=[kk('sc1')])
                self.op('dve', lambda e: e.reciprocal(scc(1), scc(1)), r=[kk('sc1')], w=[kk('sc1')])
                pnv = hd(pn, AL)
                self.tt('dve', vt(W['hf']), pnv, scc(1).unsqueeze(2).to_broadcast([L, NH, 128]), MUL, r=[pnk, kk('sc1')], w=[kk('hf')])
                self.unpin(pnk)
                self.unpin(pdk)
                self.tt('pool', vt(W['hq']), vt(W['hf']), vt(W['hf']), MUL, r=[kk('hf')], w=[kk('hq')])
                self.op('dve', lambda e: e.reduce_sum(scc(2), W['hq'][0:L, :, :], AX.X), r=[kk('hq')], w=[kk('sc2')])
                self.actf(scc(2), scc(2), AF.Ln, bias=self.epsb[0:L, 0:1], scale=1.0 / 128, r=[kk('sc2'), 'epsb'], w=[kk('sc2')])
                self.actf(scc(2), scc(2), AF.Exp, scale=-0.5, r=[kk('sc2')], w=[kk('sc2')])
                self.tt('dve', vt(W['hn']), vt(W['hf']), scc(2).unsqueeze(2).to_broadcast([L, NH, 128]), MUL, r=[kk('hf'), kk('sc2')], w=[kk('hn')])
                pht, phtk = self.ps()
                for hh in range(NH):
                    self.tr(bview(pht)[:, hh * 128:hh * 128 + L], W['hn'][0:L, hh, :], idb[0:L, 0:L], r=[kk('hn'), 'identb'], w=[phtk])
                phv = bview(pht)[:, 0:NH * 128].rearrange("p (h t) -> p h t", h=NH)[:, :, 0:L]
                self.ts('dve', ob1[:, hg:hg + NH, c0:c0 + L], phv, mlw[:, l:l + 1], MUL, r=[phtk, 'mlw'], w=[kk('ob1')])
                if kind != 's':
                    pS, pSk = self.ps()
                    for hh in range(NH):
                        self.mm(pS[:, hh * 129:(hh + 1) * 129], W['kws'][0:L, hh, :], v1_[0:L, hh, :], True, True, r=[kk('kws'), kk('v1'), k('v1o')], w=[pSk])
                    for hh in range(NH):
                        self.stt(C_[:, hh, :], C_[:, hh, :], W['wil'][:, hh, 0:1], pS[:, hh * 129:hh * 129 + 128], MUL, ADD, r=[kk('C'), kk('wil'), pSk], w=[kk('C')])
                        self.stt(Np[:, hg + hh:hg + hh + 1], Np[:, hg + hh:hg + hh + 1], W['wil'][:, hh, 0:1], pS[:, hh * 129 + 128:hh * 129 + 129], MUL, ADD,
                                 r=[kk('Np'), kk('wil'), pSk], w=[kk('Np')])
                    self.cp('act', W['Cb'], C_, r=[kk('C')], w=[kk('Cb')])
                    self.cp('act', nb[:, hl:hl + NH], Np[:, hg:hg + NH], r=[kk('Np')], w=[kk('nb')])
                else:
                    for hh in range(NH):
                        for half in range(2):
                            self.tt('pool', kwss, W['kws'][:, hh, :].unsqueeze(1).to_broadcast([128, 8, 128]),
                                    m['ET'][:, 8 * half:8 * half + 8].unsqueeze(2).to_broadcast([128, 8, 128]), MUL, r=[kk('kws')] + MK, w=[k('kwss')])
                            Cseg = Cseg2[half]
                            ck_ = k('Cseg%d' % half)
                            self.dma(Cseg, P['st_ml_c'][l][8 * half:8 * half + 8, h0 + hh].rearrange("s a b -> a s b"), w=[ck_])
                            for g3 in range(3):
                                pS, pSk = self.ps()
                                segs = [s8 for s8 in range(3 * g3, min(3 * g3 + 3, 8))]
                                for i3, s8 in enumerate(segs):
                                    self.mm(pS[:, i3 * 129:(i3 + 1) * 129], kwss[:, s8, :], v1_[:, hh, :], True, True, r=[k('kwss'), kk('v1'), k('v1o')], w=[pSk])
                                for i3, s8 in enumerate(segs):
                                    s = 8 * half + s8
                                    self.stt(Cseg[:, s8, :], Cseg[:, s8, :], W['wil'][:, hh, s:s + 1], pS[:, i3 * 129:i3 * 129 + 128], MUL, ADD,
                                             r=[ck_, kk('wil'), pSk], w=[ck_])
                                    self.stt(Nseg[:, s, h0 + hh:h0 + hh + 1], Nseg[:, s, h0 + hh:h0 + hh + 1], W['wil'][:, hh, s:s + 1],
                                             pS[:, i3 * 129 + 128:i3 * 129 + 129], MUL, ADD, r=[k('Nseg'), kk('wil'), pSk, kk('nb')], w=[k('Nseg')])
                            self.dma(O['s_ml_c'][l][8 * half:8 * half + 8, h0 + hh].rearrange("s a b -> a s b"), Cseg, r=[ck_])

            for ci, (c0, L, kind) in enumerate(CH):
                if kind == 's':
                    self.s.barrier()
                    self.dma(O['p_ml_c'][l][h0:h0 + 2].rearrange("h a b -> a h b"), C, w=[])
                    chunk(ci, c0, L, kind, 0, 2, '')
                else:
                    self.interleave([lambda ci=ci, c0=c0, L=L, kind=kind: chunk(ci, c0, L, kind, 0, 1, 'A'),
                                     lambda ci=ci, c0=c0, L=L, kind=kind: chunk(ci, c0, L, kind, 1, 1, 'B')])
            self.s.barrier()
        pst, pk = self.ps()
        self.tr(pst[0:4, 0:128], Np, idt[:, :], r=[k('Np'), 'ident'], w=[pk])
        self.cp('dve', st64[0:4, :], pst[0:4, 0:128], r=[pk], w=[k('st64')])
        self.dma(O['p_ml_n'][l], st64[0:4, :], r=[k('st64')])
        pst, pk = self.ps()
        self.tr(pst[0:64, 0:128], Nseg.rearrange("p s h -> p (s h)"), idt[:, :], r=[k('Nseg'), 'ident'], w=[pk])
        self.cp('dve', st64[0:64, :], pst[0:64, 0:128], r=[pk], w=[k('st64')])
        self.dma(O['s_ml_n'][l].rearrange("s h d -> (s h) d"), st64[0:64, :], r=[k('st64')])
        self.s.barrier()
        cv.p = wlo
        g1 = cv.f32(512)
        g2 = cv.f32(512)
        for h in range(4):
            def conso(gi, c0, n, pst, pk, h=h):
                self.actf(g1[:, 0:n], pst[:, 0:n], AF.Sigmoid, r=[pk], w=[k('g1')])
                self.tt('dve', ob1[:, h, c0:c0 + n], ob1[:, h, c0:c0 + n], g1[:, 0:n], MUL, r=[k('g1'), k('ob1')], w=[k('ob1')])
            self.proj(l, C_MLO + h * 128, 128, conso)

            def consz(gi, c0, n, pst, pk, h=h):
                self.actf(g2[:, 0:n], pst[:, 0:n], AF.Silu, r=[pk], w=[k('g2')])
                self.tt('dve', ob1[:, h, c0:c0 + n], ob1[:, h, c0:c0 + n], g2[:, 0:n], MUL, r=[k('g2'), k('ob1')], w=[k('ob1'), ('obT', 1)])
            self.proj(l, C_MLZ + h * 128, 128, consz)

    def s5_phase(self, l):
        if 's5' not in self.phases:
            return self.zero_ob(2)
        P, O = self.P, self.O
        ar, hT = self.ar, self.hT
        ob2 = self.obT[2]
        idt, on = self.ident, self.ones
        c1 = Carver(ar, 0, 2 * OB_W)
        cv = Carver(ar, M0, ARENA)
        uT = c1.bf16(4, TT)
        COS = cv.f32(16, 128)
        SIN = cv.f32(16, 128)
        BTf = cv.bf16(4096)
        CTf = cv.bf16(4096)
        XRf = cv.f32(4096)
        BT = BTf.rearrange("p (a r c) -> p a r c", r=2, c=128)
        CT = CTf.rearrange("p (a r c) -> p a r c", r=2, c=128)
        XR = XRf.rearrange("p (a r c) -> p a r c", r=2, c=128)
        wglu = cv.bf16(4, 512)
        wz = cv.bf16(8, 512)
        sm = {}
        for nm in ['lre', 'lim', 'dtb', 'are', 'th', 'mag', 'c', 's', 'cc', 'ss', 'cs', 'lbr', 'lbi', 'nr', 'den',
                   'fr', 'fi', 't1', 't2', 'cr', 'ci', 'crn', 'cin']:
            sm[nm] = c1.f32(16)
        dtbc = c1.f32(32)
        drow = c1.f32(32)
        Brn, Bin_ = c1.f32(16, 16), c1.f32(16, 16)
        bbr, bbi = c1.f32(16, 16), c1.f32(16, 16)
        tA, tB = c1.f32(16, 16), c1.f32(16, 16)
        CN = [c1.f32(16, 16), c1.f32(16, 16)]
        cin_t = c1.f32(128)
        H0 = [c1.f32(16, 16), c1.f32(16, 16)]
        MH = [c1.f32(16, 16), c1.f32(16, 16)]
        HS = [c1.f32(16, 16), c1.f32(16, 16)]
        st16 = cv.f32(2048)
        tmpS = cv.f32(128)
        W1, W2, W3, W4 = cv.f32(4, 128), cv.f32(4, 128), cv.f32(4, 128), cv.f32(4, 128)
        ZR, ZI = cv.f32(4, 128), cv.f32(4, 128)
        FR, FI = cv.f32(4, 128), cv.f32(4, 128)
        HRb, HIb = cv.bf16(4, 128), cv.bf16(4, 128)
        y5, yq, yt = cv.f32(128), cv.f32(128), cv.f32(128)
        sgb, szb = cv.f32(512), cv.f32(512)
        dt_all = cv.f32(2048)
        dtmp1 = dt_all[:, 0:1024].rearrange("p (a c) -> p a c", a=16)
        dtmp2 = dt_all[:, 1024:2048].rearrange("p (a c) -> p a c", a=16)
        tf = dt_all.rearrange("p (f n) -> p f n", f=4)

        S = 's5'
        k = lambda nm: (S, nm)
        for fo in range(4):
            wb, wk = self.wblock(P['w_glu'][l][:, fo * 128:(fo + 1) * 128], 4, 128)
            self.cp('act', wglu[:, :, fo * 128:(fo + 1) * 128], wb[:, 0:4, :], r=[wk], w=[k('wglu')])
            wb, wk = self.wblock(self.w_in[l][:, C_S5Z + fo * 128:C_S5Z + (fo + 1) * 128], 8, 128)
            self.cp('act', wz[:, :, fo * 128:(fo + 1) * 128], wb[:, :, :], r=[wk], w=[k('wz')])
        for q in range(4):
            def cons(gi, c0, n_, pst, pk, q=q):
                self.cp('act', uT[:, q, c0:c0 + n_], pst[:, 0:n_], r=[pk], w=[k('uT%d' % q)])
            self.proj(l, C_S5U + 128 * q, 128, cons)
        self.load_cols(P['lam_re'][l].rearrange("(a b) p -> a (b p)", b=2), 16, sm['lre'], (S, 'lre'))
        self.load_cols(P['lam_im'][l].rearrange("(a b) p -> a (b p)", b=2), 16, sm['lim'], (S, 'lim'))
        self.dma(drow[0:1, :], P['log_dt'][l:l + 1, :], w=[(S, 'drow')])
        pst, pk = self.ps()
        self.mm(pst[:, 0:32], on[0:1, :], drow[0:1, :], True, True, r=[(S, 'drow'), 'ones'], w=[pk])
        self.actf(dtbc, pst[:, 0:32], AF.Exp, r=[pk], w=[(S, 'dtbc')])
        dv = dtbc.rearrange("p (a b) -> p a b", b=2)
        self.cp('dve', sm['dtb'][0:64, :], dv[0:64, :, 0], r=[(S, 'dtbc')], w=[(S, 'dtb')])
        self.cp('dve', sm['dtb'][64:128, :], dv[64:128, :, 1], r=[(S, 'dtbc')], w=[(S, 'dtb')])
        k = lambda nm: (S, nm)

        def T2(eng, o, a, b, op):
            self.tt(eng, sm[o], sm[a], sm[b], op, r=[k(a), k(b)], w=[k(o)])
        T2('dve', 'are', 'lre', 'dtb', ALU.mult)
        T2('dve', 'th', 'lim', 'dtb', ALU.mult)
        self.actf(sm['mag'], sm['are'], AF.Exp, r=[k('are')], w=[k('mag')])
        self.actf(sm['s'], sm['th'], AF.Sin, scale=1.0 / 16, r=[k('th')], w=[k('s')])
        self.actf(sm['t1'], sm['th'], AF.Sin, scale=1.0 / 32, r=[k('th')], w=[k('t1')])
        self.dump('th', sm['th'], [k('th')])
        self.dump('s16', sm['s'], [k('s')])
        self.dump('s32', sm['t1'], [k('t1')])
        T2('dve', 't2', 't1', 't1', ALU.mult)
        self.ts('dve', sm['c'], sm['t2'], -2.0, ALU.mult, 1.0, ALU.add, r=[k('t2')], w=[k('c')])
        self.dump('c16', sm['c'], [k('c')])
        for _ in range(4):
            T2('dve', 'cc', 'c', 'c', ALU.mult)
            T2('dve', 'ss', 's', 's', ALU.mult)
            T2('dve', 'cs', 'c', 's', ALU.mult)
            T2('dve', 'c', 'cc', 'ss', ALU.subtract)
            self.ts('dve', sm['s'], sm['cs'], 2.0, ALU.mult, r=[k('cs')], w=[k('s')])
        T2('dve', 'lbr', 'mag', 'c', ALU.mult)
        T2('dve', 'lbi', 'mag', 's', ALU.mult)
        self.ts('dve', sm['nr'], sm['lbr'], -1.0, ALU.add, r=[k('lbr')], w=[k('nr')])
        T2('dve', 'cc', 'lre', 'lre', ALU.mult)
        T2('dve', 'ss', 'lim', 'lim', ALU.mult)
        T2('dve', 'den', 'cc', 'ss', ALU.add)
        self.op('dve', lambda e: e.reciprocal(sm['den'], sm['den']), r=[k('den')], w=[k('den')])
        T2('dve', 't1', 'nr', 'lre', ALU.mult)
        T2('dve', 't2', 'lbi', 'lim', ALU.mult)
        T2('dve', 't1', 't1', 't2', ALU.add)
        T2('dve', 'fr', 't1', 'den', ALU.mult)
        T2('dve', 't1', 'lbi', 'lre', ALU.mult)
        T2('dve', 't2', 'nr', 'lim', ALU.mult)
        T2('dve', 't1', 't1', 't2', ALU.subtract)
        T2('dve', 'fi', 't1', 'den', ALU.mult)
        self.dma(Brn, P['b_re'][l].rearrange("(a b) p c -> (b p) a c", b=2), w=[k('Brn')])
        self.dma(Bin_, P['b_im'][l].rearrange("(a b) p c -> (b p) a c", b=2), w=[k('Bin')])
        frb = sm['fr'].unsqueeze(2).to_broadcast([128, 16, 16])
        fib = sm['fi'].unsqueeze(2).to_broadcast([128, 16, 16])
        self.tt('dve', tA, Brn, frb, ALU.mult, r=[k('Brn'), k('fr')], w=[k('tA')])
        self.tt('dve', tB, Bin_, fib, ALU.mult, r=[k('Bin'), k('fi')], w=[k('tB')])
        self.tt('dve', bbr, tA, tB, ALU.subtract, r=[k('tA'), k('tB')], w=[k('bbr')])
        self.tt('dve', tA, Bin_, frb, ALU.mult, r=[k('Bin'), k('fr'), k('bbr')], w=[k('tA')])
        self.tt('dve', tB, Brn, fib, ALU.mult, r=[k('Brn'), k('fi'), k('bbr')], w=[k('tB')])
        self.tt('dve', bbi, tA, tB, ALU.add, r=[k('tA'), k('tB')], w=[k('bbi')])

        def pad_fill(srcs_, key):
            self.memset('pool', XRf, 0.0, w=[k('XR')])
            for ri in range(2):
                for b in range(2):
                    base = XR[64 * b:64 * b + 64, :, ri, :]
                    d4 = base.rearrange("p (q j) c -> p q j c", j=4)
                    for j in range(4):
                        dst = d4[:, :, j, 32 * j + 16 * b:32 * j + 16 * b + 16]
                        src = srcs_[ri][64 * b:64 * b + 64, :, :].rearrange("p (q j) c -> p q j c", j=4)[:, :, j, :]
                        self.cp('pool', dst, src, r=[key[ri], k('XR')], w=[k('XR')])
        pad_fill([bbr, bbi], [k('bbr'), k('bbi')])
        for a in range(16):
            for ri in range(2):
                if (a * 2 + ri) % 4 == 0:
                    pst, pk = self.ps()
                j = (a * 2 + ri) % 4
                self.tr(pst[:, j * 128:(j + 1) * 128], XR[:, a, ri, :], idt[:, :], r=[k('XR'), 'ident'], w=[pk])
                if j == 3:
                    a0 = a - 1
                    self.cp('act', BTf[:, a0 * 256:(a0 + 2) * 256], pst[:, :], r=[pk], w=[k('BT')])
        for ri, nm in enumerate(['c_re', 'c_im']):
            for h in range(2):
                for a8 in range(8):
                    a = 8 * h + a8
                    self.dma(cin_t[16 * a8:16 * a8 + 16, :].rearrange("o (b p) -> o b p", b=2),
                             P[nm][l][2 * a:2 * a + 2].rearrange("b o p -> o b p"), w=[k('cin')])
                pst, pk = self.ps()
                self.tr(pst[:, 0:128], cin_t[:, :], idt[:, :], r=[k('cin'), 'ident'], w=[pk])
                dst = CN[ri][:, 8 * h:8 * h + 8, :]
                if ri == 0:
                    self.cp('act', dst, pst[:, 0:128].rearrange("p (a o) -> p a o", o=16), r=[pk], w=[k('CN%d' % ri)])
                else:
                    self.op('act', lambda e, dst=dst, pst=pst: e.mul(dst, pst[:, 0:128].rearrange("p (a o) -> p a o", o=16), -1.0),
                            r=[pk], w=[k('CN%d' % ri)])
        pad_fill(CN, [k('CN0'), k('CN1')])
        self.cp('dve', CTf, XRf, r=[k('XR')], w=[k('CT')])
        self.cp('dve', COS[:, :, 0:1], sm['c'].unsqueeze(2), r=[k('c')], w=[k('COS')])
        self.cp('dve', SIN[:, :, 0:1], sm['s'].unsqueeze(2), r=[k('s')], w=[k('SIN')])
        n = 1
        while n < 128:
            C0, S0 = COS[:, :, 0:n], SIN[:, :, 0:n]
            cm = COS[:, :, n - 1:n].to_broadcast([128, 16, n])
            smm = SIN[:, :, n - 1:n].to_broadcast([128, 16, n])
            d1, d2 = dtmp1[:, :, 0:n], dtmp2[:, :, 0:n]
            rk = [k('COS'), k('SIN')]
            self.tt('dve', d1, C0, cm, ALU.mult, r=rk, w=[k('d1')])
            self.tt('pool', d2, S0, smm, ALU.mult, r=rk, w=[k('d2')])
            self.tt('dve', COS[:, :, n:2 * n], d1, d2, ALU.subtract, r=[k('d1'), k('d2')], w=[k('COS')])
            self.tt('dve', d1, S0, cm, ALU.mult, r=rk, w=[k('d1')])
            self.tt('pool', d2, C0, smm, ALU.mult, r=rk, w=[k('d2')])
            self.tt('dve', SIN[:, :, n:2 * n], d1, d2, ALU.add, r=[k('d1'), k('d2')], w=[k('SIN')])
            n *= 2
        for nm_ in ['lre', 'lim', 'dtb', 'mag', 'c', 's', 'fr', 'fi']:
            self.dump(nm_, sm[nm_], [k(nm_)])
        self.dump('COS', COS, [k('COS')])
        self.dump('SIN', SIN, [k('SIN')])
        self.dump('bbr', bbr, [k('bbr')])
        self.dump('CN0', CN[0], [k('CN0')])
        self.dump('CN1', CN[1], [k('CN1')])
        self.dump('XR', XRf, [k('XR')])
        for ri, nm in enumerate(['st_s5_re', 'st_s5_im']):
            self.dma(st16[0:16, :], P[nm][l], w=[k('st16')])
            pst, pk = self.ps()
            for a in range(16):
                self.tr(pst[:, a * 16:(a + 1) * 16], st16[0:16, a * 128:(a + 1) * 128], idt[0:16, 0:16],
                        r=[k('st16'), 'ident'], w=[pk])
            self.cp('dve', H0[ri], pst[:, 0:256].rearrange("p (a s) -> p a s", s=16), r=[pk], w=[k('H0%d' % ri)])
            self.tt('dve', MH[ri], H0[ri], sm['mag'].unsqueeze(2).to_broadcast([128, 16, 16]), ALU.mult,
                    r=[k('H0%d' % ri), k('mag')], w=[k('MH%d' % ri)])
        self.memset('dve', sm['cr'], 0.0, w=[k('cr')])
        self.memset('dve', sm['ci'], 0.0, w=[k('ci')])
        MUL, ADD, SUB = ALU.mult, ALU.add, ALU.subtract
        self.s.barrier()
        cx = Carver(ar, 0, 0)
        xoff = [0]

        def xal(n):
            v = XRf[:, xoff[0]:xoff[0] + n]
            xoff[0] += n
            assert xoff[0] <= 4096
            return v
        MW = [xal(512).rearrange("p (a t) -> p a t", a=4) for _ in range(4)]
        ZRs = [ZR, xal(512).rearrange("p (a t) -> p a t", a=4)]
        ZIs = [ZI, xal(512).rearrange("p (a t) -> p a t", a=4)]
        y5s = [y5, xal(128), xal(128)]
        its = [(c0, n, is_s, q) for (c0, n, is_s) in FRAMES for q in range(4)]
        NI = len(its)
        tk = [k('COS'), k('SIN')]

        def views(n, is_s, q):
            if not is_s:
                v = lambda t: t[:, :, 0:n]
                pv = lambda p_: p_[:, :].rearrange("p (a t) -> p a t", a=4)[:, :, 0:n]
                C4, S4 = COS[:, 4 * q:4 * q + 4, 0:n], SIN[:, 4 * q:4 * q + 4, 0:n]
            else:
                v = lambda t: t.rearrange("p a (s t) -> p a s t", t=8)
                pv = lambda p_: p_[:, :].rearrange("p (a s t) -> p a s t", a=4, t=8)
                C4 = COS[:, 4 * q:4 * q + 4, 0:8].unsqueeze(2).to_broadcast([128, 4, 16, 8])
                S4 = SIN[:, 4 * q:4 * q + 4, 0:8].unsqueeze(2).to_broadcast([128, 4, 16, 8])
            return v, pv, C4, S4

        def stage_M(i):
            c0, n, is_s, q = its[i]
            par = i % 2
            ZRp, ZIp = ZRs[par], ZIs[par]
            zrk, zik = k('ZR%d' % par), k('ZI%d' % par)
            pzr, pzrk = self.ps()
            pzi, pzik = self.ps()
            for j in range(4):
                a = 4 * q + j
                self.mm(pzr[:, j * 128:j * 128 + n], BT[:, a, 0, :], uT[:, q, c0:c0 + n], True, True, r=[k('BT'), k('uT%d' % q)], w=[pzrk])
                self.mm(pzi[:, j * 128:j * 128 + n], BT[:, a, 1, :], uT[:, q, c0:c0 + n], True, True, r=[k('BT'), k('uT%d' % q)], w=[pzik])
            v, pv, C4, S4 = views(n, is_s, q)
            self.tt('dve', v(MW[0]), pv(pzr), C4, MUL, r=[pzrk] + tk, w=[k('MW0')])
            self.tt('dve', v(MW[1]), pv(pzi), S4, MUL, r=[pzik] + tk, w=[k('MW1')])
            self.tt('pool', v(ZRp), v(MW[0]), v(MW[1]), ADD, r=[k('MW0'), k('MW1')], w=[zrk])
            self.tt('dve', v(MW[2]), pv(pzi), C4, MUL, r=[pzik] + tk, w=[k('MW2')])
            self.tt('dve', v(MW[3]), pv(pzr), S4, MUL, r=[pzrk] + tk, w=[k('MW3')])
            self.tt('pool', v(ZIp), v(MW[2]), v(MW[3]), SUB, r=[k('MW2'), k('MW3')], w=[zik])
            if is_s:
                zr0 = ZRp.rearrange("p a (s t) -> p a s t", t=8)[:, :, :, 0]
                zi0 = ZIp.rearrange("p a (s t) -> p a s t", t=8)[:, :, :, 0]
                self.tt('pool', zr0, zr0, MH[0][:, 4 * q:4 * q + 4, :], ADD, r=[zrk, k('MH0')], w=[zrk])
                self.tt('pool', zi0, zi0, MH[1][:, 4 * q:4 * q + 4, :], ADD, r=[zik, k('MH1')], w=[zik])

        def stage_S(i):
            c0, n, is_s, q = its[i]
            par = i % 2
            ZRp, ZIp = ZRs[par], ZIs[par]
            zrk, zik = k('ZR%d' % par), k('ZI%d' % par)
            y5p, y5k = y5s[par], k('y5%d' % par)
            v, pv, C4, S4 = views(n, is_s, q)
            if is_s and q == 0:
                self.cp('dve', sm['crn'], sm['cr'], r=[k('cr')], w=[k('crn')])
                self.cp('dve', sm['cin'], sm['ci'], r=[k('ci')], w=[k('cin2')])
            for j in range(4):
                a = 4 * q + j
                if is_s:
                    self.ts('dve', tmpS[:, :], self.segmask[:, :], sm['mag'][:, a:a + 1], MUL, r=['segmask', k('mag')], w=[k('tmpS')])
                    coef = tmpS[:, 0:n]
                    ir, ii = 0.0, 0.0
                    rk = [k('tmpS')]
                else:
                    coef = sm['mag'][:, a:a + 1].to_broadcast([128, n])
                    ir, ii = sm['cr'][:, a:a + 1], sm['ci'][:, a:a + 1]
                    rk = [k('mag'), k('cr'), k('ci')]
                self.scan(FR[:, j, 0:n], coef, ZRp[:, j, 0:n], ir, r=[zrk] + rk, w=[k('FR')])
                self.scan(FI[:, j, 0:n], coef, ZIp[:, j, 0:n], ii, r=[zik] + rk, w=[k('FI')])
            self.tt('dve', v(W1), v(FR), C4, MUL, r=[k('FR')] + tk, w=[k('W1')])
            self.tt('pool', v(W2), v(FI), S4, MUL, r=[k('FI')] + tk, w=[k('W2')])
            self.tt('dve', v(W3), v(FI), C4, MUL, r=[k('FI')] + tk, w=[k('W3')])
            self.tt('pool', v(W4), v(FR), S4, MUL, r=[k('FR')] + tk, w=[k('W4')])

        def stage_S2(i):
            c0, n, is_s, q = its[i]
            par = i % 3
            y5p, y5k = y5s[par], k('y5%d' % par)
            v, pv, C4, S4 = views(n, is_s, q)
            self.tt('dve', v(HRb), v(W1), v(W2), SUB, r=[k('W1'), k('W2')], w=[k('HRb')])
            self.tt('dve', v(HIb), v(W3), v(W4), ADD, r=[k('W3'), k('W4')], w=[k('HIb')])
            if is_s:
                l7 = lambda t: t.rearrange("p a (s t) -> p a s t", t=8)[:, :, :, 7]
                self.tt('dve', HS[0][:, 4 * q:4 * q + 4, :], l7(W1), l7(W2), SUB, r=[k('W1'), k('W2')], w=[k('HS0')])
                self.tt('dve', HS[1][:, 4 * q:4 * q + 4, :], l7(W3), l7(W4), ADD, r=[k('W3'), k('W4')], w=[k('HS1')])
            else:
                self.tt('dve', sm['cr'][:, 4 * q:4 * q + 4], W1[:, :, n - 1], W2[:, :, n - 1], SUB, r=[k('W1'), k('W2')], w=[k('cr')])
                self.tt('dve', sm['ci'][:, 4 * q:4 * q + 4], W3[:, :, n - 1], W4[:, :, n - 1], ADD, r=[k('W3'), k('W4')], w=[k('ci')])
            py, pyk = self.ps()
            for j in range(4):
                a = 4 * q + j
                self.mm(py[:, 0:n], CT[:, a, 0, :], HRb[:, j, 0:n], j == 0, False, r=[k('CT'), k('HRb')], w=[pyk])
                self.mm(py[:, 0:n], CT[:, a, 1, :], HIb[:, j, 0:n], False, j == 3, r=[k('CT'), k('HIb')], w=[pyk])
            dq = self.dcol[:, l * 4 + q:l * 4 + q + 1]
            self.stt(y5p[:, 0:n], uT[:, q, c0:c0 + n], dq, py[:, 0:n], MUL, ADD, r=[k('uT%d' % q), 'dcol', pyk], w=[y5k])

        def stage_Y(i):
            c0, n, is_s, q = its[i]
            par = i % 3
            y5p, y5k = y5s[par], k('y5%d' % par)
            self.actf(yq[:, 0:n], y5p[:, 0:n], AF.Square, r=[y5k], w=[k('yq')])
            self.actf(yq[:, 0:n], yq[:, 0:n], AF.Identity, bias=self.oneb[:, 0:1], scale=0.044715, r=[k('yq'), 'oneb'], w=[k('yq')])
            self.tt('dve', yt[:, 0:n], yq[:, 0:n], y5p[:, 0:n], MUL, r=[k('yq'), y5k], w=[k('yt')])
            self.actf(yt[:, 0:n], yt[:, 0:n], AF.Sigmoid, scale=1.5957691216057308, r=[k('yt')], w=[k('yt')])
            self.tt('dve', ob2[:, q, c0:c0 + n], y5p[:, 0:n], yt[:, 0:n], MUL, r=[y5k, k('yt')], w=[('y5g', q)])

        stage_M(0)
        if NI > 1:
            stage_M(1)
        for i in range(NI + 2):
            if i < NI:
                stage_S(i)
                if i + 2 < NI:
                    stage_M(i + 2)
                stage_S2(i)
            if 0 <= i - 2 < NI:
                stage_Y(i - 2)
        self.s.barrier()
        for gi, (c0, n) in enumerate(TGS):
            for fo in range(4):
                pa, pak = self.ps()
                for kc in range(4):
                    self.mm(pa[:, 0:n], wglu[:, kc, fo * 128:(fo + 1) * 128], ob2[:, kc, c0:c0 + n], kc == 0, kc == 3,
                            r=[k('wglu'), ('y5g', kc)], w=[pak])
                pb, pbk = self.ps()
                for fc in range(8):
                    self.mm(pb[:, 0:n], wz[:, fc, fo * 128:(fo + 1) * 128], hT[:, fc, c0:c0 + n], fc == 0, fc == 7,
                            r=[k('wz'), ('hT', fc, gi)], w=[pbk])
                bcol = self.bglu[:, l * 4 + fo:l * 4 + fo + 1]
                self.actf(sgb[:, 0:n], pa[:, 0:n], AF.Sigmoid, bias=bcol, r=[pak, 'bglu'], w=[k('sgb')])
                self.actf(szb[:, 0:n], pb[:, 0:n], AF.Sigmoid, r=[pbk], w=[k('szb')])
                self.tt('pool', szb[:, 0:n], szb[:, 0:n], sgb[:, 0:n], MUL, r=[k('sgb'), k('szb')], w=[k('szb')])
                self.tt('dve', tf[:, fo, 0:n], pb[:, 0:n], szb[:, 0:n], MUL, r=[pbk, k('szb')], w=[k('tf%d' % fo)])
            for fo in range(4):
                self.tt('dve', ob2[:, fo, c0:c0 + n], ob2[:, fo, c0:c0 + n], tf[:, fo, 0:n], MUL,
                        r=[('y5g', fo), k('tf%d' % fo)], w=[('y5g', fo), ('obT', 2)])
        for ri, (nm_p, nm_s, src_p) in enumerate([('p_s5_re', 's_s5_re', 'crn'), ('p_s5_im', 's_s5_im', 'cin')]):
            pst, pk = self.ps()
            pkey = k('crn') if ri == 0 else k('cin2')
            self.tr(pst[0:16, 0:128], sm[src_p], idt[:, :], r=[pkey, 'ident'], w=[pk])
            self.cp('dve', st16[0:16, 0:128], pst[0:16, 0:128], r=[pk], w=[k('st16')])
            self.dma(O[nm_p][l], st16[0:16, 0:128], r=[k('st16')])
            for qq in range(4):
                pst, pk = self.ps()
                for j in range(4):
                    a = 4 * qq + j
                    self.tr(pst[0:16, j * 128:(j + 1) * 128], HS[ri][:, a, :], idt[:, :], r=[k('HS%d' % ri), 'ident'], w=[pk])
                self.cp('dve', st16[0:16, qq * 512:(qq + 1) * 512], pst[0:16, :], r=[pk], w=[k('st16')])
            self.dma(O[nm_s][l], st16[0:16, :], r=[k('st16')])


def build_program(nlayers=DEPTH, dbg=None, phases=('s5', 'dn', 'ml')):
    return K(nlayers, dbg, phases).build()


def make_in_maps(inp):
    maps = []
    c_ = np.ascontiguousarray
    for c in range(8):
        sl = slice(16 * c, 16 * c + 16)
        m = {
            'xp': c_(inp['x_prompt'][c]),
            'xs': c_(inp['x_sample'][sl].reshape(TS, D)),
            'meta': c_(inp['meta_tokens']),
            'norm_w': c_(inp['norm_w']),
            'w_in': c_(inp['w_in']),
            'b_gate': c_(inp['b_gate']),
            'w_br0': c_(inp['w_branch_dn']),
            'w_br1': c_(inp['w_branch_ml']),
            'w_br2': c_(inp['w_branch_s5']),
            'w_out': c_(inp['w_out']),
            'fnw': c_(inp['final_norm_w'].reshape(1, D)),
            'lam_re': c_(inp['s5_lambda_re']), 'lam_im': c_(inp['s5_lambda_im']), 'log_dt': c_(inp['s5_log_dt']),
            'b_re': c_(inp['s5_b_re']), 'b_im': c_(inp['s5_b_im']), 'c_re': c_(inp['s5_c_re']), 'c_im': c_(inp['s5_c_im']),
            's5_d': c_(inp['s5_d']), 'w_glu': c_(inp['s5_w_glu']), 'b_glu': c_(inp['s5_b_glu']),
            'st_s5_re': c_(inp['state_s5_re'][:, sl].reshape(DEPTH, NSEQ, 2048)),
            'st_s5_im': c_(inp['state_s5_im'][:, sl].reshape(DEPTH, NSEQ, 2048)),
            'dn_conv_w': c_(inp['dn_conv_w']), 'dn_a_log': c_(inp['dn_a_log']), 'dn_dt_bias': c_(inp['dn_dt_bias']),
            'dn_norm_w': c_(inp['dn_norm_w']),
            'st_dn_conv': c_(inp['state_dn_conv'][:, sl]), 'st_dn_s': c_(inp['state_dn_s'][:, sl]),
            'ml_bias_i': c_(inp['ml_bias_i']), 'ml_bias_f': c_(inp['ml_bias_f']), 'ml_norm_w': c_(inp['ml_norm_w']),
            'st_ml_c': c_(inp['state_ml_c'][:, sl]), 'st_ml_n': c_(inp['state_ml_n'][:, sl]), 'st_ml_m': c_(inp['state_ml_m'][:, sl]),
        }
        maps.append(m)
    return maps


def kernel(**inp):
    inp = {k: np.asarray(v) for k, v in inp.items()}
    import os
    ph = os.environ.get('K_PHASES')
    nc = build_program() if ph is None else build_program(nlayers=int(os.environ.get('K_NL', '1')), phases=tuple(p for p in ph.split(',') if p))
    res = run_bass_kernel_spmd(nc, make_in_maps(inp), core_ids=list(range(8)))
    r = res.results
    f = np.float32
    y_p = np.stack([r[c]['y_p'] for c in range(8)]).astype(f)
    y_s = np.concatenate([r[c]['y_s'].reshape(16, 8, D) for c in range(8)]).astype(f)

    def pst(name, shape):
        return np.stack([r[c][name].reshape((DEPTH,) + shape) for c in range(8)], axis=1).astype(f)

    def sst(name, shape):
        return np.concatenate([r[c][name].reshape((DEPTH, NSEQ) + shape) for c in range(8)], axis=1).astype(f)
    return (y_p, y_s,
            pst('p_dn_conv', (3, 1536)), pst('p_dn_s', (4, 128, 128)), pst('p_ml_c', (4, 128, 128)), pst('p_ml_n', (4, 128)),
            pst('p_ml_m', (4,)), pst('p_s5_re', (32, 64)), pst('p_s5_im', (32, 64)),
            sst('s_dn_conv', (3, 1536)), sst('s_dn_s', (4, 128, 128)), sst('s_ml_c', (4, 128, 128)), sst('s_ml_n', (4, 128)),
            sst('s_ml_m', (4,)), sst('s_s5_re', (32, 64)), sst('s_s5_im', (32, 64)))
```

```python
from contextlib import ExitStack
import numpy as np
import concourse.bass as bass
import concourse.mybir as mybir
from concourse.bass_utils import run_bass_kernel_spmd

F32 = mybir.dt.float32
BF16 = mybir.dt.bfloat16
AF = mybir.ActivationFunctionType
ALU = mybir.AluOpType
AX = mybir.AxisListType

ENGS = ['pe', 'act', 'dve', 'pool', 'sp']
NSLOT = 40


class Sched:
    def __init__(self):
        self.ops = {e: [] for e in ENGS}
        self.clock = {e: {} for e in ENGS}
        self.opclock = {}
        self.reg = {}
        self.slot_val = [0] * NSLOT
        self.next_slot = 0
        import os
        self.self_sync = set(os.environ.get('SELF_SYNC', 'act,dve').split(','))
        self.last_real = {}

    def _deps(self, reads, writes):
        deps = []
        for r in reads:
            st = self.reg.get(r)
            if st and st['w']:
                deps.append(st['w'])
        for w in writes:
            st = self.reg.get(w)
            if st:
                if st['w']:
                    deps.append(st['w'])
                deps.extend(st['r'].items())
        return deps

    def barrier(self):
        tg = []
        for e in ENGS:
            if e != 'sp' and self.last_real.get(e):
                tg.append((e, self.last_real[e]))
        for i in range(NSLOT):
            if self.slot_val[i] > 0:
                tg.append((('d', i), self.slot_val[i]))
        for e in ENGS:
            self.add(e, None, extra=[t for t in tg if not (t[0] == e and e == 'pe')])

    def add(self, eng, fn, reads=(), writes=(), dma=False, extra=()):
        idx = len(self.ops[eng]) + 1
        clk = self.clock[eng]
        need = {}
        for (k, v) in extra:
            if clk.get(k, 0) >= v:
                continue
            if need.get(k, 0) < v:
                need[k] = v
        if fn is not None and not dma:
            self.last_real[eng] = idx
        for (k, v) in self._deps(reads, writes):
            if k == eng and (eng == 'pe' or eng not in self.self_sync):
                continue
            if clk.get(k, 0) >= v:
                continue
            if need.get(k, 0) < v:
                need[k] = v
        slot = None
        if dma:
            slot = self.next_slot
            self.next_slot = (self.next_slot + 1) % NSLOT
            pv = self.slot_val[slot]
            dk = ('d', slot)
            if pv > 0 and clk.get(dk, 0) < pv and need.get(dk, 0) < pv:
                need[dk] = pv
            self.slot_val[slot] = pv + 16
        for k, v in need.items():
            oc = self.opclock.get((k, v))
            if oc:
                for kk, vv in oc.items():
                    if clk.get(kk, 0) < vv:
                        clk[kk] = vv
            if clk.get(k, 0) < v:
                clk[k] = v
            if not isinstance(k, tuple):
                self.ops[k][v - 1]['signal'] = True
        if dma:
            ckey = (('d', slot), self.slot_val[slot])
            oc = dict(clk)
            oc[ckey[0]] = ckey[1]
            self.opclock[ckey] = oc
        else:
            ckey = (eng, idx)
            self.opclock[ckey] = dict(clk)
        self.ops[eng].append(dict(fn=fn, waits=list(need.items()), signal=False, slot=slot))
        for r in reads:
            st = self.reg.setdefault(r, {'w': None, 'r': {}})
            if st['r'].get(ckey[0], 0) < ckey[1]:
                st['r'][ckey[0]] = ckey[1]
        for w in writes:
            self.reg[w] = {'w': ckey, 'r': {}}
        return ckey

    def emit(self, nc, es):
        sems = {e: es.enter_context(nc.semaphore('s_' + e)) for e in ENGS}
        dsems = [es.enter_context(nc.semaphore('sd%d' % i)) for i in range(NSLOT)]
        for e in ENGS:
            c = 0
            for op in self.ops[e]:
                if op['signal']:
                    c += 1
                op['sigval'] = c
        block = es.enter_context(nc.Block())
        ops = self.ops
        slot_val = self.slot_val

        def run(e, eng):
            for op in ops[e]:
                for (k, v) in op['waits']:
                    if isinstance(k, tuple):
                        eng.wait_ge(dsems[k[1]], v)
                    else:
                        eng.wait_ge(sems[k], ops[k][v - 1]['sigval'])
                if op['fn'] is None:
                    continue
                ins = op['fn'](eng)
                if op['slot'] is not None:
                    ins.then_inc(dsems[op['slot']], 16)
                elif op['signal']:
                    ins.then_inc(sems[e], 1)

        @block.tensor
        def _(eng):
            run('pe', eng)

        @block.scalar
        def _(eng):
            run('act', eng)

        @block.vector
        def _(eng):
            run('dve', eng)

        @block.gpsimd
        def _(eng):
            run('pool', eng)

        @block.sync
        def _(eng):
            run('sp', eng)
            for i in range(NSLOT):
                if slot_val[i] > 0:
                    eng.wait_ge(dsems[i], slot_val[i])


D = 1024
DEPTH = 4
NMETA = 16
TPR = 2048
TP = NMETA + TPR
NSEQ = 16
TS = NSEQ * 8
TT = TP + TS
IN_COLS = 8720
EPS = 1e-6
TGS = [(0, 512), (512, 512), (1024, 512), (1536, 512), (2048, 144)]
FRAMES = [(0, 16, False)] + [(16 + 128 * k, 128, False) for k in range(16)] + [(TP, 128, True)]
C_DNQKV, C_DNZ, C_DNA, C_DNB = 0, 1536, 2048, 2052
C_MLQKV, C_MLO, C_MLZ, C_MLI, C_MLF = 2056, 3592, 4104, 4616, 4620
C_S5U, C_S5Z, C_GATE = 4624, 5136, 5648
ARENA = 38912
OB_W = 4384
M0 = 3 * OB_W


def tg_of(c0, n):
    return [gi for gi, (a, m) in enumerate(TGS) if a < c0 + n and c0 < a + m]


class Carver:
    def __init__(self, ar, lo, hi):
        self.ar, self.lo, self.hi, self.p = ar, lo, hi, lo

    def f32(self, *shape):
        n = int(np.prod(shape))
        assert self.p + n <= self.hi, ('arena overflow', self.p, n, self.hi)
        v = self.ar[:, self.p:self.p + n]
        self.p += n
        if len(shape) == 2:
            return v.rearrange("p (a b) -> p a b", a=shape[0])
        if len(shape) == 3:
            return v.rearrange("p (a b c) -> p a b c", a=shape[0], b=shape[1])
        return v

    def bf16(self, *shape):
        n = int(np.prod(shape))
        w = (n + 1) // 2
        assert self.p + w <= self.hi, ('arena overflow', self.p, w, self.hi)
        v = self.ar[:, self.p:self.p + w].bitcast(BF16)[:, 0:n]
        self.p += w
        if len(shape) == 2:
            return v.rearrange("p (a b) -> p a b", a=shape[0])
        if len(shape) == 3:
            return v.rearrange("p (a b c) -> p a b c", a=shape[0], b=shape[1])
        return v


class K:
    def __init__(self, nlayers=DEPTH, dbg=None, phases=('s5', 'dn', 'ml')):
        self.nlayers = nlayers
        self.dbg = dbg
        self.phases = phases
        self.nc = bass.Bass("TRN2", target_bir_lowering=False)
        self.s = Sched()
        self.es = ExitStack()
        self.uid = 0
        self.psn = 0
        self.pinned = set()
        self.rec = None
        import os
        self.use_f32r = os.environ.get('K_F32R', '0') == '1'
        self.ps_cur = 'all'
        self.ps_sets = {'all': list(range(8)), 'A': [0, 1, 2, 3], 'B': [4, 5, 6, 7]}
        self.ps_cnt = {'all': 0, 'A': 0, 'B': 0}
        self.wn = 0
        self.xrn = 0

    def sb(self, shape, dt=F32, name=None):
        self.uid += 1
        name = name or ('t%d' % self.uid)
        return self.es.enter_context(self.nc.sbuf_tensor(name, list(shape), dt))

    def dram_in(self, name, shape):
        return self.nc.dram_tensor(name, list(shape), F32, kind="ExternalInput").ap()

    def dram_out(self, name, shape):
        return self.nc.dram_tensor(name, list(shape), F32, kind="ExternalOutput").ap()

    def ps(self, pin=False):
        banks = self.ps_sets[self.ps_cur]
        cnt = self.ps_cnt
        while banks[cnt[self.ps_cur] % len(banks)] in self.pinned:
            cnt[self.ps_cur] += 1
        i = banks[cnt[self.ps_cur] % len(banks)]
        cnt[self.ps_cur] += 1
        if pin:
            self.pinned.add(i)
        return self.psb[i], ('ps', i)

    def unpin(self, key):
        self.pinned.discard(key[1])

    def _add(self, eng, fn, r=(), w=(), dma=False):
        if self.rec is not None:
            self.rec.append((eng, fn, tuple(r), tuple(w), dma))
            return None
        return self.s.add(eng, fn, r, w, dma=dma)

    def interleave(self, gens):
        lists = []
        for i, g in enumerate(gens):
            self.rec = []
            self.ps_cur = 'AB'[i]
            g()
            lists.append(self.rec)
        self.rec = None
        self.ps_cur = 'all'
        n = max(len(x) for x in lists)
        for j in range(n):
            for lst in lists:
                lo = (j * len(lst)) // n
                hi = ((j + 1) * len(lst)) // n
                for (eng, fn, r, w, dma) in lst[lo:hi]:
                    self.s.add(eng, fn, r, w, dma=dma)

    def op(self, eng, fn, r=(), w=()):
        return self._add(eng, fn, r, w)

    def dma(self, out, in_, r=(), w=(), q='sp'):
        return self._add(q, lambda e: e.dma_start(out=out, in_=in_), r, w, dma=True)

    def mm(self, out, lhsT, rhs, start, stop, r=(), w=()):
        return self._add('pe', lambda e: e.matmul(out, lhsT, rhs, start=start, stop=stop), r, w)

    def tr(self, out, in_, ident, r=(), w=()):
        return self._add('pe', lambda e: e.transpose(out, in_, ident), r, w)

    def actf(self, out, in_, func, bias=0.0, scale=1.0, r=(), w=()):
        return self._add('act', lambda e: e.activation(out, in_, func, bias=bias, scale=scale), r, w)

    def tt(self, eng, out, a, b, op, r=(), w=()):
        return self._add(eng, lambda e: e.tensor_tensor(out, a, b, op), r, w)

    def ts(self, eng, out, a, s1, op0, s2=None, op1=None, r=(), w=()):
        if op1 is None:
            return self._add(eng, lambda e: e.tensor_scalar(out, a, s1, None, op0), r, w)
        return self._add(eng, lambda e: e.tensor_scalar(out, a, s1, s2, op0, op1), r, w)

    def stt(self, out, in0, scalar, in1, op0, op1, r=(), w=()):
        return self._add('dve', lambda e: e.scalar_tensor_tensor(out, in0, scalar, in1, op0, op1), r, w)

    def cp(self, eng, out, in_, r=(), w=()):
        if eng == 'act':
            return self._add('act', lambda e: e.copy(out, in_), r, w)
        return self._add(eng, lambda e: e.tensor_copy(out, in_), r, w)

    def memset(self, eng, out, val, w=()):
        return self._add(eng, lambda e: e.memset(out, val), (), w)

    def scan(self, out, d0, d1, init, r=(), w=()):
        return self._add('dve', lambda e: e.tensor_tensor_scan(out, d0, d1, init, ALU.mult, ALU.add), r, w)

    def consts(self):
        nc = self.nc
        self.psb = [self.es.enter_context(nc.psum_tensor('psb%d' % i, [128, 512], F32)) for i in range(8)]
        self.ident = self.sb([128, 128], F32, 'ident')
        self.ones = self.sb([128, 128], F32, 'ones')
        self.identb = self.sb([128, 128], BF16, 'identb')
        self.onesb = self.sb([128, 128], BF16, 'onesb')
        self.segmask = self.sb([128, 128], F32, 'segmask')
        self.epsb = self.sb([128, 1], F32, 'epsb')
        idt, on, sg, eb = self.ident, self.ones, self.segmask, self.epsb
        self.memset('pool', on[:], 1.0, w=['ones'])
        self.op('pool', lambda e: e.affine_select(idt[:], on[:], [[-1, 128]], ALU.is_equal, 0.0, base=0,
                                                  channel_multiplier=1), r=['ones'], w=['ident'])
        ib, ob = self.identb, self.onesb
        self.cp('pool', ib[:], idt[:], r=['ident'], w=['identb'])
        self.cp('pool', ob[:], on[:], r=['ones'], w=['onesb'])
        self.memset('pool', sg[:], 1.0, w=['segmask'])
        self.memset('pool', sg[:, :].rearrange("p (s t) -> p s t", t=8)[:, :, 0:1], 0.0, w=['segmask'])
        self.memset('pool', eb[:], EPS, w=['epsb'])
        self.oneb = self.sb([128, 1], F32, 'oneb')
        ob_ = self.oneb
        self.memset('pool', ob_[:], 1.0, w=['oneb'])

    def load_cols(self, src2d, R, dst, dstkey):
        st = self.stage_small
        pst, pk = self.ps()
        idt = self.ident
        self.dma(st[0:R, :], src2d, w=['stsm'])
        self.tr(pst[:, 0:R], st[0:R, :], idt[0:R, 0:R], r=['stsm', 'ident'], w=[pk])
        self.cp('dve', dst, pst[:, 0:R], r=[pk], w=[dstkey])

    def wblock(self, src2d, kc, ncols):
        i = self.wn % 4
        j = self.wn % 2
        self.wn += 1
        st, wb = self.wst[j], self.wbf[i]
        self.dma(st[:, 0:kc, 0:ncols], src2d.rearrange("(c p) e -> p c e", p=128), w=[('wst', j)])
        self.cp('pool', wb[:, 0:kc, 0:ncols], st[:, 0:kc, 0:ncols], r=[('wst', j)], w=[('wbf', i)])
        return wb, ('wbf', i)

    def proj(self, l, col0, ncols, consumer):
        wb, wk = self.wblock(self.w_in[l][:, col0:col0 + ncols], 8, ncols)
        hT = self.hT
        for gi, (c0, n) in enumerate(TGS):
            pst, pk = self.ps()
            for fc in range(8):
                self.mm(pst[0:ncols, 0:n], wb[:, fc, 0:ncols], hT[:, fc, c0:c0 + n], fc == 0, fc == 7,
                        r=[wk, ('hT', fc, gi)], w=[pk])
            consumer(gi, c0, n, pst, pk)

    def build(self):
        nc = self.nc
        NL = self.nlayers
        di = self.dram_in
        xp = di('xp', [TPR, D])
        xs = di('xs', [TS, D])
        meta = di('meta', [NMETA, D])
        norm_w = di('norm_w', [DEPTH, D])
        self.w_in = w_in = di('w_in', [DEPTH, D, IN_COLS])
        b_gate = di('b_gate', [DEPTH, 3 * D])
        w_br = [di('w_br%d' % b, [DEPTH, 512, D]) for b in range(3)]
        w_out = di('w_out', [DEPTH, D, D])
        fnw = di('fnw', [1, D])
        self.P = dict(
            lam_re=di('lam_re', [DEPTH, 32, 64]), lam_im=di('lam_im', [DEPTH, 32, 64]), log_dt=di('log_dt', [DEPTH, 32]),
            b_re=di('b_re', [DEPTH, 32, 64, 16]), b_im=di('b_im', [DEPTH, 32, 64, 16]),
            c_re=di('c_re', [DEPTH, 32, 16, 64]), c_im=di('c_im', [DEPTH, 32, 16, 64]),
            s5_d=di('s5_d', [DEPTH, 512]), w_glu=di('w_glu', [DEPTH, 512, 512]), b_glu=di('b_glu', [DEPTH, 512]),
            st_s5_re=di('st_s5_re', [DEPTH, NSEQ, 2048]), st_s5_im=di('st_s5_im', [DEPTH, NSEQ, 2048]),
            dn_conv_w=di('dn_conv_w', [DEPTH, 4, 1536]), dn_a_log=di('dn_a_log', [DEPTH, 4]), dn_dt_bias=di('dn_dt_bias', [DEPTH, 4]),
            dn_norm_w=di('dn_norm_w', [DEPTH, 128]),
            st_dn_conv=di('st_dn_conv', [DEPTH, NSEQ, 3, 1536]), st_dn_s=di('st_dn_s', [DEPTH, NSEQ, 4, 128, 128]),
            ml_bias_i=di('ml_bias_i', [DEPTH, 4]), ml_bias_f=di('ml_bias_f', [DEPTH, 4]), ml_norm_w=di('ml_norm_w', [DEPTH, 128]),
            st_ml_c=di('st_ml_c', [DEPTH, NSEQ, 4, 128, 128]), st_ml_n=di('st_ml_n', [DEPTH, NSEQ, 4, 128]), st_ml_m=di('st_ml_m', [DEPTH, NSEQ, 4]),
        )
        self.O = dict(
            p_s5_re=self.dram_out('p_s5_re', [DEPTH, 16, 128]), p_s5_im=self.dram_out('p_s5_im', [DEPTH, 16, 128]),
            s_s5_re=self.dram_out('s_s5_re', [DEPTH, NSEQ, 2048]), s_s5_im=self.dram_out('s_s5_im', [DEPTH, NSEQ, 2048]),
            p_dn_conv=self.dram_out('p_dn_conv', [DEPTH, 3, 1536]), p_dn_s=self.dram_out('p_dn_s', [DEPTH, 4, 128, 128]),
            s_dn_conv=self.dram_out('s_dn_conv', [DEPTH, NSEQ, 3, 1536]), s_dn_s=self.dram_out('s_dn_s', [DEPTH, NSEQ, 4, 128, 128]),
            p_ml_c=self.dram_out('p_ml_c', [DEPTH, 4, 128, 128]), p_ml_n=self.dram_out('p_ml_n', [DEPTH, 4, 128]), p_ml_m=self.dram_out('p_ml_m', [DEPTH, 4]),
            s_ml_c=self.dram_out('s_ml_c', [DEPTH, NSEQ, 4, 128, 128]), s_ml_n=self.dram_out('s_ml_n', [DEPTH, NSEQ, 4, 128]),
            s_ml_m=self.dram_out('s_ml_m', [DEPTH, NSEQ, 4]),
        )
        y_p = self.dram_out('y_p', [TPR, D])
        y_s = self.dram_out('y_s', [TS, D])
        xT_d = nc.dram_tensor('xT_d', [D, TT], F32, kind="Internal").ap()
        xT_v = xT_d.rearrange("(c p) t -> p c t", p=128)
        if self.dbg:
            self.dbg_o = self.dram_out('dbg', self.dbg[1])

        self.consts()
        self.stage_small = self.sb([128, 128], F32, 'stsm')
        nw = self.sb([128, DEPTH * 8], F32, 'nw')
        self.load_cols(norm_w.rearrange("l (c p) -> (l c) p", p=128), DEPTH * 8, nw[:, :], 'nw')
        fw = self.sb([128, 8], F32, 'fw')
        self.load_cols(fnw.rearrange("l (c p) -> (l c) p", p=128), 8, fw[:, :], 'fw')
        bg = self.sb([128, DEPTH * 24], F32, 'bg')
        self.load_cols(b_gate.rearrange("l (c p) -> (l c) p", p=128), DEPTH * 24, bg[:, :], 'bg')
        self.dcol = self.sb([128, DEPTH * 4], F32, 'dcol')
        self.load_cols(self.P['s5_d'].rearrange("l (c p) -> (l c) p", p=128), DEPTH * 4, self.dcol[:, :], 'dcol')
        self.bglu = self.sb([128, DEPTH * 4], F32, 'bglu')
        self.load_cols(self.P['b_glu'].rearrange("l (c p) -> (l c) p", p=128), DEPTH * 4, self.bglu[:, :], 'bglu')

        self.cwT = self.sb([128, DEPTH * 48], F32, 'cwT')
        cwv = self.P['dn_conv_w'].rearrange("l j (c p) -> (l j c) p", p=128)
        self.load_cols(cwv[0:96], 96, self.cwT[:, 0:96], 'cwT')
        self.load_cols(cwv[96:192], 96, self.cwT[:, 96:192], 'cwT')
        self.mlw = self.sb([128, DEPTH], F32, 'mlw')
        self.load_cols(self.P['ml_norm_w'], DEPTH, self.mlw[:, :], 'mlw')
        self.dnw = self.sb([128, DEPTH], F32, 'dnw')
        self.load_cols(self.P['dn_norm_w'], DEPTH, self.dnw[:, :], 'dnw')
        self.hT = hT = self.sb([128, 8, TT], BF16, 'hT')
        self.wst = [self.sb([128, 8, 128], F32, 'wst%d' % i) for i in range(2)]
        self.wbf = [self.sb([128, 8, 128], BF16, 'wbf%d' % i) for i in range(4)]
        self.ar = ar = self.sb([128, ARENA], F32, 'arena')
        obT = []
        for b in range(3):
            obT.append(ar[:, b * OB_W:(b + 1) * OB_W].bitcast(BF16).rearrange("p (c t) -> p c t", c=4))
        self.obT = obT
        cv = Carver(ar, M0, ARENA)
        mixed = cv.bf16(8, TT)
        acc = cv.f32(TT)
        xb = cv.f32(8, 512)
        sq = cv.f32(2, 512)
        rs = cv.f32(512)
        gsb = cv.f32(512)
        NXR = 6
        xr = [cv.f32(512) for _ in range(NXR)]
        xin2 = [cv.f32(D) for _ in range(2)]
        xo2 = [cv.f32(8, 128) for _ in range(2)]
        yT = xo2[0]
        yo = xin2

        srcs = [(meta, NMETA, 0)] + [(xp[i * 128:(i + 1) * 128, :], 128, NMETA + 128 * i) for i in range(16)] \
            + [(xs, 128, TP)]
        idt = self.ident
        self.dma(xin2[0][0:srcs[0][1], :], srcs[0][0], w=[('xin', 0)])
        for ti, (src, n, c0) in enumerate(srcs):
            bb = ti % 2
            xin, xo = xin2[bb], xo2[bb]
            if ti + 1 < len(srcs):
                self.dma(xin2[1 - bb][0:srcs[ti + 1][1], :], srcs[ti + 1][0], w=[('xin', 1 - bb)])
            for half in range(2):
                pst, pk = self.ps()
                for q in range(4):
                    fc = half * 4 + q
                    self.tr(pst[:, q * 128:q * 128 + n], xin[0:n, fc * 128:(fc + 1) * 128], idt[0:n, 0:n],
                            r=[('xin', bb), 'ident'], w=[pk])
                src_ps = pst[:, :].rearrange("p (q t) -> p q t", q=4)[:, :, 0:n]
                dst = xo[:, half * 4:half * 4 + 4, 0:n]
                self.cp('act' if half == 0 else 'dve', dst, src_ps, r=[pk], w=[('xo', bb, half)])
            self.dma(xT_v[:, :, c0:c0 + n], xo[:, :, 0:n], r=[('xo', bb, 0), ('xo', bb, 1)], w=[('xT', ti)])
        xT_keys = [('xT', ti) for ti in range(len(srcs))]

        sqh = sq.rearrange("p a n -> p (a n)").bitcast(BF16).rearrange("p (a n) -> p a n", a=4)

        def rstd_of(x_, n, xkey):
            pst, pk = self.ps()
            on = self.onesb
            for fc in range(8):
                self.actf(sqh[:, fc % 4, 0:n], x_[:, fc, 0:n], AF.Square, r=[xkey], w=[('sq', fc % 4)])
                self.mm(pst[:, 0:n], on[:, :], sqh[:, fc % 4, 0:n], fc == 0, fc == 7, r=[('sq', fc % 4), 'onesb'], w=[pk])
            self.actf(rs[:, 0:n], pst[:, 0:n], AF.Ln, bias=self.epsb[:, 0:1], scale=1.0 / D, r=[pk, 'epsb'], w=['rs'])
            self.actf(rs[:, 0:n], rs[:, 0:n], AF.Exp, scale=-0.5, r=['rs'], w=['rs'])

        for l in range(NL):
            for gi, (c0, n) in enumerate(TGS):
                rk = (xT_keys if l == 0 else []) + [('xTf', fo_, gi) for fo_ in range(8)]
                self.dma(xb[:, :, 0:n], xT_v[:, :, c0:c0 + n], r=rk, w=['xb'])
                rstd_of(xb, n, 'xb')
                for fc in range(8):
                    self.stt(hT[:, fc, c0:c0 + n], xb[:, fc, 0:n], nw[:, l * 8 + fc:l * 8 + fc + 1], rs[:, 0:n],
                             ALU.mult, ALU.mult, r=['xb', 'rs', 'nw'], w=[('hT', fc, gi)])
            self.s.barrier()
            self.s5_phase(l)
            self.s.barrier()
            self.dn_phase(l)
            self.s.barrier()
            self.ml_phase(l)
            self.s.barrier()
            if self.dbg and self.dbg[0].startswith('obT') and l == self.dbg[2]:
                b = int(self.dbg[0][3])
                cvd = Carver(ar, M0, ARENA)
                hf = cvd.f32(4, TT)
                self.cp('dve', hf, obT[b], w=['hf'])
                self.dma(self.dbg_o.rearrange("(c p) t -> p c t", p=128), hf, r=['hf'])
                self.s.barrier()

            for fo in range(8):
                for b in range(3):
                    wg, wgk = self.wblock(w_in[l][:, C_GATE + b * D + fo * 128: C_GATE + b * D + fo * 128 + 128], 8, 128)
                    wp, wpk = self.wblock(w_br[b][l][:, fo * 128:(fo + 1) * 128], 4, 128)
                    for gi, (c0, n) in enumerate(TGS):
                        pg, pgk = self.ps()
                        for fc in range(8):
                            self.mm(pg[:, 0:n], wg[:, fc, :], hT[:, fc, c0:c0 + n], fc == 0, fc == 7,
                                    r=[wgk, ('hT', fc, gi)], w=[pgk])
                        pp, ppk = self.ps()
                        for kc in range(4):
                            self.mm(pp[:, 0:n], wp[:, kc, :], obT[b][:, kc, c0:c0 + n], kc == 0, kc == 3,
                                    r=[wpk, ('obT', b)], w=[ppk])
                        bcol = l * 24 + b * 8 + fo
                        self.actf(gsb[:, 0:n], pg[:, 0:n], AF.Sigmoid, bias=bg[:, bcol:bcol + 1], r=[pgk, 'bg'], w=['gsb'])
                        if b == 0:
                            self.tt('dve', acc[:, c0:c0 + n], gsb[:, 0:n], pp[:, 0:n], ALU.mult, r=['gsb', ppk], w=[('acc', gi)])
                        else:
                            self.tt('dve', gsb[:, 0:n], gsb[:, 0:n], pp[:, 0:n], ALU.mult, r=['gsb', ppk], w=['gsb'])
                            if b == 1:
                                self.tt('pool', acc[:, c0:c0 + n], acc[:, c0:c0 + n], gsb[:, 0:n], ALU.add,
                                        r=['gsb', ('acc', gi)], w=[('acc', gi)])
                            else:
                                self.tt('pool', mixed[:, fo, c0:c0 + n], acc[:, c0:c0 + n], gsb[:, 0:n], ALU.add,
                                        r=['gsb', ('acc', gi)], w=[('mixed', fo, gi)])
            wits = [(fo, gi, c0, n) for fo in range(8) for gi, (c0, n) in enumerate(TGS)]

            def xload(j):
                fo, gi, c0, n = wits[j]
                i = j % NXR
                rk = (xT_keys if l == 0 else []) + [('xTf', fo, gi)]
                self.dma(xr[i][:, 0:n], xT_d[fo * 128:(fo + 1) * 128, c0:c0 + n], r=rk, w=[('xr', i)])
            for j in range(3):
                xload(j)
            wo = wok = None
            for j, (fo, gi, c0, n) in enumerate(wits):
                if gi == 0:
                    wo, wok = self.wblock(w_out[l][:, fo * 128:(fo + 1) * 128], 8, 128)
                po, pok = self.ps()
                for kc in range(8):
                    self.mm(po[:, 0:n], wo[:, kc, :], mixed[:, kc, c0:c0 + n], kc == 0, kc == 7,
                            r=[wok, ('mixed', kc, gi)], w=[pok])
                i = j % NXR
                xr_ = xr[i]
                self.tt('dve', xr_[:, 0:n], xr_[:, 0:n], po[:, 0:n], ALU.add, r=[('xr', i), pok], w=[('xr', i)])
                if j + 3 < len(wits):
                    xload(j + 3)
                self.dma(xT_d[fo * 128:(fo + 1) * 128, c0:c0 + n], xr_[:, 0:n], r=[('xr', i)], w=[('xTf', fo, gi)])
            self.s.barrier()

        outs = [(y_p[i * 128:(i + 1) * 128, :], NMETA + 128 * i) for i in range(16)] + [(y_s, TP)]
        allx = xT_keys if NL == 0 else [('xTf', fo, gi) for fo in range(8) for gi in range(5)]
        self.s.barrier()
        self.dma(xb[:, :, 0:128], xT_v[:, :, outs[0][1]:outs[0][1] + 128], r=allx, w=[('xbh', 0)])
        for ti, (dst, c0) in enumerate(outs):
            b = ti % 2
            n = 128
            xbh = xb[:, :, 128 * b:128 * b + 128]
            if ti + 1 < len(outs):
                c1_ = outs[ti + 1][1]
                self.dma(xb[:, :, 128 * (1 - b):128 * (1 - b) + 128], xT_v[:, :, c1_:c1_ + 128], r=allx, w=[('xbh', 1 - b)])
            pst, pk = self.ps()
            on = self.onesb
            for fc in range(8):
                self.actf(sqh[:, fc % 4, 0:n], xbh[:, fc, :], AF.Square, r=[('xbh', b)], w=[('sq', fc % 4)])
                self.mm(pst[:, 0:n], on[:, :], sqh[:, fc % 4, 0:n], fc == 0, fc == 7, r=[('sq', fc % 4), 'onesb'], w=[pk])
            self.actf(rs[:, 0:n], pst[:, 0:n], AF.Ln, bias=self.epsb[:, 0:1], scale=1.0 / D, r=[pk, 'epsb'], w=['rs'])
            self.actf(rs[:, 0:n], rs[:, 0:n], AF.Exp, scale=-0.5, r=['rs'], w=['rs'])
            for fc in range(8):
                self.stt(yT[:, fc, :], xbh[:, fc, :], fw[:, fc:fc + 1], rs[:, 0:128], ALU.mult, ALU.mult,
                         r=[('xbh', b), 'rs', 'fw'], w=[('yT', fc)])
            yo_ = yo[b]
            for half in range(2):
                pst, pk = self.ps()
                for q in range(4):
                    fc = half * 4 + q
                    self.tr(pst[:, q * 128:(q + 1) * 128], yT[:, fc, :], idt[:, :], r=[('yT', fc), 'ident'], w=[pk])
                self.cp('act' if half == 0 else 'dve', yo_[:, half * 512:(half + 1) * 512], pst[:, :], r=[pk], w=[('yo', b, half)])
            self.dma(dst, yo_[:, :], r=[('yo', b, 0), ('yo', b, 1)])

        self.s.emit(nc, self.es)
        self.es.close()
        return nc

    def dump(self, name, ap, rkeys):
        if not self.dbg or self.dbg[0] != 'dump':
            return
        o = self.dram_out('D_' + name, list(ap.shape))
        self.dma(o, ap, r=rkeys)

    def zero_ob(self, b):
        o = self.obT[b]
        self.memset('pool', o, 0.0, w=[('obT', b)])

    def build_masks(self, cv, key):
        on, idt = self.ones, self.ident
        k = lambda nm: (key, nm)
        m = {}
        GE, EQ = ALU.is_ge, ALU.is_equal

        def asel(dst, src, pattern, op, base, cm, r, w):
            self.op('pool', lambda e: e.affine_select(dst, src, pattern, op, 0.0, base=base, channel_multiplier=cm), r=r, w=w)
        for nm, pat, base, cm in [('tril', [[-1, 128]], 0, 1), ('trils', [[-1, 128]], -1, 1),
                                  ('triu', [[1, 128]], 0, -1), ('trius', [[1, 128]], -1, -1)]:
            m[nm] = cv.f32(128)
            asel(m[nm], on[:, :], pat, GE, base, cm, ['ones'], [k(nm)])
        E = cv.f32(128)
        E7 = cv.f32(128)
        tmpE = cv.f32(128)
        asel(tmpE[0:16, :], on[0:16, :], [[1, 128]], GE, 0, -8, ['ones'], [k('tmpE')])
        asel(E[0:16, :], tmpE[0:16, :], [[-1, 128]], GE, 7, 8, [k('tmpE')], [k('E')])
        asel(E7[0:16, :], on[0:16, :], [[1, 128]], EQ, -7, -8, ['ones'], [k('E7')])
        pst, pk = self.ps()
        self.mm(pst[:, 0:128], E[0:16, :], E[0:16, :], True, True, r=[k('E')], w=[pk])
        self.mm(pst[:, 128:256], E7[0:16, :], E[0:16, :], True, True, r=[k('E'), k('E7')], w=[pk])
        self.tr(pst[:, 256:272], E[0:16, :], idt[0:16, 0:16], r=[k('E'), 'ident'], w=[pk])
        for nm in ['tril', 'trils', 'triu', 'trius']:
            m[nm + '_s'] = cv.f32(128)
            self.tt('dve', m[nm + '_s'], m[nm], pst[:, 0:128], ALU.mult, r=[k(nm), pk], w=[k(nm + '_s')])
        m['sel_s'] = cv.f32(128)
        self.cp('dve', m['sel_s'], pst[:, 128:256], r=[pk], w=[k('sel_s')])
        m['ET'] = cv.f32(16)
        self.cp('dve', m['ET'], pst[:, 256:272], r=[pk], w=[k('ET')])
        m['sel_p'] = cv.f32(128)
        asel(m['sel_p'], on[:, :], [[0, 128]], EQ, -127, 1, ['ones'], [k('sel_p')])
        m['sel_m'] = cv.f32(16)
        asel(m['sel_m'][0:16, :], on[0:16, 0:16], [[0, 16]], EQ, -15, 1, ['ones'], [k('sel_m')])
        m['id16'] = cv.f32(16, 16)
        self.memset('pool', m['id16'], 1.0, w=[k('id16')])
        asel(m['id16'], m['id16'], [[1, 16], [-1, 16]], EQ, 0, 0, [k('id16')], [k('id16')])
        m['selh'] = cv.f32(4, 128)
        asel(m['selh'][0:4, :, :], on[0:4, :].unsqueeze(1).to_broadcast([4, 4, 128]), [[1, 4], [0, 128]], EQ, 0, -1, ['ones'], [k('selh')])
        m['keys'] = [k(n_) for n_ in ['tril', 'trils', 'triu', 'trius', 'tril_s', 'trils_s', 'triu_s', 'trius_s',
                                      'sel_s', 'sel_p', 'sel_m', 'ET', 'id16', 'selh']]
        return m

    def dn_phase(self, l):
        if 'dn' not in self.phases:
            return self.zero_ob(0)
        P, O = self.P, self.O
        ar, hT = self.ar, self.hT
        ob0 = self.obT[0]
        idt, on, idb, onb = self.ident, self.ones, self.identb, self.onesb
        MUL, ADD, SUB = ALU.mult, ALU.add, ALU.subtract
        cvA = Carver(ar, OB_W, 2 * OB_W)
        cv = Carver(ar, M0, ARENA)
        S_ = 'dn'
        k = lambda nm: (S_, nm)
        qkv = cv.bf16(12, TT)
        G = cvA.f32(TT)
        BETA = cvA.f32(TT)
        m = self.build_masks(cv, S_)
        MK = m['keys']
        wlo = cv.p
        raw_p = cv.f32(TP + 3)
        raw_s = cv.f32(16, 11)
        acc = cv.f32(TT)
        off_cres = cv.p
        cres = cv.f32(TT)
        GT = cv.f32(TT)
        hst = cv.f32(128)
        t48 = cv.f32(48)
        sqb = cv.bf16(512)
        rtmp = cv.f32(512)
        cst = cv.f32(128)
        dnsc = cv.f32(8)
        self.memset('pool', raw_p[:, 0:3], 0.0, w=[k('raw_p0')])
        self.memset('pool', cres[0:4, :], 1.0, w=[k('cres')])
        self.memset('pool', cres[0:4, 0:1], 0.0, w=[k('cres')])
        self.memset('pool', cres[0:4, 16:16 + 2048].rearrange("p (c t) -> p c t", t=128)[:, :, 0:1], 0.0, w=[k('cres')])
        self.memset('pool', cres[0:4, TP:TT].rearrange("p (c t) -> p c t", t=8)[:, :, 0:1], 0.0, w=[k('cres')])
        self.dma(dnsc[0:4, 0:1], P['dn_dt_bias'][l:l + 1, :].rearrange("o h -> h o"), w=[k('dnsc0')])
        self.dma(dnsc[0:4, 1:2], P['dn_a_log'][l:l + 1, :].rearrange("o h -> h o"), w=[k('dnsc1')])
        self.actf(dnsc[0:4, 2:3], dnsc[0:4, 1:2], AF.Exp, r=[k('dnsc1')], w=[k('dnsc2')])
        self.ts('dve', dnsc[0:4, 3:4], dnsc[0:4, 2:3], -1.0, MUL, r=[k('dnsc2')], w=[k('dnsc3')])

        def cons_a(gi, c0, n, pst, pk):
            self.actf(GT[0:4, c0:c0 + n], pst[0:4, 0:n], AF.Exp, bias=dnsc[0:4, 0:1], r=[pk, k('dnsc0')], w=[k('GT')])
        self.proj(l, C_DNA, 4, cons_a)
        self.actf(GT[0:4, :], GT[0:4, :], AF.Ln, bias=1.0, r=[k('GT')], w=[k('GT')])
        self.ts('dve', GT[0:4, :], GT[0:4, :], dnsc[0:4, 3:4], MUL, r=[k('GT'), k('dnsc3')], w=[k('GT')])
        self.scan(G[0:4, :], cres[0:4, :], GT[0:4, :], 0.0, r=[k('GT'), k('cres')], w=[k('G')])

        def cons_b(gi, c0, n, pst, pk):
            self.actf(BETA[0:4, c0:c0 + n], pst[0:4, 0:n], AF.Sigmoid, r=[pk], w=[k('BETA')])
        self.proj(l, C_DNB, 4, cons_b)
        cw = self.cwT
        self.s.barrier()
        cb = Carver(ar, off_cres, off_cres + 2 * TT)
        RP = [raw_p, cb.f32(TP + 3)]
        RS = [raw_s, cb.f32(16, 11)]
        SQ = [sqb, cb.bf16(512)]
        RT = [rtmp, cb.f32(512)]
        self.memset('pool', RP[1][:, 0:3], 0.0, w=[k('raw_p0')])
        for which in range(3):
            for h in range(4):
                fb = which * 4 + h
                bb = fb % 2
                raw_p, raw_s = RP[bb], RS[bb]
                k = lambda nm, bb=bb: (S_, nm + (str(bb) if nm.startswith('raw') else ''))
                self.dma(hst[0:48, :], P['st_dn_conv'][l][:, :, fb * 128:(fb + 1) * 128].rearrange("s j c -> (s j) c"), w=[k('hst')])
                pst, pk = self.ps()
                self.tr(pst[:, 0:48], hst[0:48, :], idt[0:48, 0:48], r=[k('hst'), 'ident'], w=[pk])
                self.cp('dve', raw_s[:, :, 0:3], pst[:, 0:48].rearrange("p (s j) -> p s j", j=3), r=[pk], w=[k('raw_sh')])

                def cons(gi, c0, n, pst, pk):
                    npr = min(c0 + n, TP) - c0
                    self.cp('act', raw_p[:, 3 + c0:3 + c0 + npr], pst[:, 0:npr], r=[pk], w=[k('raw_p%d' % gi)])
                    if gi == 4:
                        self.cp('act', raw_s[:, :, 3:11], pst[:, npr:npr + 128].rearrange("p (s t) -> p s t", t=8), r=[pk], w=[k('raw_s')])
                self.proj(l, C_DNQKV + which * 512 + h * 128, 128, cons)
                rall = [k('raw_p%d' % gi) for gi in range(5)] + [k('raw_p0'), k('raw_s'), k('raw_sh')]
                accs = acc[:, TP:TT].rearrange("p (s t) -> p s t", t=8)
                for j in range(4):
                    wc = cw[:, l * 48 + j * 12 + fb:l * 48 + j * 12 + fb + 1]
                    if j == 0:
                        self.op('act', lambda e, raw_p=raw_p, wc=wc: e.mul(acc[:, 0:TP], raw_p[:, 0:TP], wc), r=rall + ['cwT'], w=[k('acc')])
                        self.ts('pool', accs, raw_s[:, :, 0:8], wc, MUL, r=rall + ['cwT'], w=[k('accs')])
                    else:
                        self.stt(acc[:, 0:TP], raw_p[:, j:j + TP], wc, acc[:, 0:TP], MUL, ADD, r=rall + ['cwT', k('acc')], w=[k('acc')])
                        self.stt(accs, raw_s[:, :, j:j + 8], wc, accs, MUL, ADD, r=rall + ['cwT', k('accs')], w=[k('accs')])
                pst, pk = self.ps()
                self.tr(pst[0:3, 0:128], raw_p[:, TP:TP + 3], idt[:, :], r=rall + ['ident'], w=[pk])
                self.cp('dve', cst[0:3, :], pst[0:3, 0:128], r=[pk], w=[k('cst')])
                self.dma(O['p_dn_conv'][l][:, fb * 128:(fb + 1) * 128], cst[0:3, :], r=[k('cst')], q='pool')
                self.cp('pool', t48.rearrange("p (s j) -> p s j", j=3), raw_s[:, :, 8:11], r=rall, w=[k('t48')])
                pst, pk = self.ps()
                self.tr(pst[0:48, 0:128], t48, idt[:, :], r=[k('t48'), 'ident'], w=[pk])
                self.cp('dve', cst[0:48, :], pst[0:48, 0:128], r=[pk], w=[k('cst')])
                self.dma(O['s_dn_conv'][l][:, :, fb * 128:(fb + 1) * 128].rearrange("s j c -> (s j) c"), cst[0:48, :], r=[k('cst')], q='pool')
                dstq = qkv[:, fb, :]
                if which == 2:
                    self.actf(dstq, acc[:, :], AF.Silu, r=[k('acc'), k('accs')], w=[k('qkv')])
                else:
                    self.actf(acc[:, :], acc[:, :], AF.Silu, r=[k('acc'), k('accs')], w=[k('acc'), k('accs')])
                    for gi, (c0, n) in enumerate(TGS):
                        sqb, rtmp = SQ[gi % 2], RT[gi % 2]
                        sqk, rtk = k('sqb%d' % (gi % 2)), k('rtmp%d' % (gi % 2))
                        self.tt('pool', sqb[:, 0:n], acc[:, c0:c0 + n], acc[:, c0:c0 + n], MUL, r=[k('acc'), k('accs')], w=[sqk])
                        pst, pk = self.ps()
                        self.mm(pst[:, 0:n], onb[:, :], sqb[:, 0:n], True, True, r=[sqk, 'onesb'], w=[pk])
                        self.actf(rtmp[:, 0:n], pst[:, 0:n], AF.Ln, bias=self.epsb[:, 0:1], r=[pk, 'epsb'], w=[rtk])
                        self.actf(rtmp[:, 0:n], rtmp[:, 0:n], AF.Exp, scale=-0.5, r=[rtk], w=[rtk])
                        sc = (128.0 ** -0.5) if which == 0 else 1.0
                        self.stt(qkv[:, fb, c0:c0 + n], acc[:, c0:c0 + n], sc, rtmp[:, 0:n], MUL, MUL,
                                 r=[k('acc'), k('accs'), rtk], w=[k('qkv%d' % gi)])
        k = lambda nm: (S_, nm)
        rtmp = RT[0]
        self.s.barrier()
        cv.p = wlo
        c2 = cvA
        W = {}
        for nm in ['E1a', 'E1b', 'eGbc', 'Pm', 'PTm', 'Zf']:
            W[nm] = cv.f32(4, 128)
        W['rstd'] = W['E1a']
        W['nbb'] = W['Zf']
        for nm in ['Ds', 'DTm', 'DTs', 'Zb', 'nKbG', 'Kdec', 'Vb', 'nWT', 'vnew', 'attnT', 'qG', 'sqv', 'Sb']:
            W[nm] = cv.bf16(4, 128)
        W['S'] = cv.f32(4, 128)
        cols = cv.f32(8, 4)
        Sseg = cv.f32(8, 128)
        Ssegb = cv.bf16(8, 128)
        nWTs = cv.bf16(8, 128)
        qGs = cv.bf16(8, 128)
        Kds = cv.bf16(8, 128)
        dnw = self.dnw
        self.memset('dve', W['S'], 0.0, w=[k('S')])
        self.memset('pool', W['Sb'], 0.0, w=[k('Sb')])
        CH = [(0, 16, 'm')] + [(16 + 128 * i, 128, 'p') for i in range(16)] + [(TP, 128, 's')]
        Wfull = W
        cols2 = [cols, cv.f32(8, 4)]

        def chunk(c0, L, kind, h0, NH, cols, kt):
            W = {nm: Wfull[nm][:, h0:h0 + NH, :] for nm in Wfull}
            kk = lambda nm: k(nm + kt)
            F32R = mybir.dt.float32r
            Wo = dict(W)
            if self.use_f32r:
                for nm_ in ('Pm', 'PTm', 'Zf'):
                    Wo[nm_] = W[nm_].bitcast(F32R)
            sfx = '_s' if kind == 's' else ''
            nlev = {'m': 3, 'p': 6, 's': 2}[kind]
            selm = {'m': m['sel_m'][0:16, 0:16], 'p': m['sel_p'], 's': m['sel_s']}[kind]
            v3 = lambda t: t[0:L, :, 0:L]
            vf = lambda t: t[:, :, 0:L]
            vt = lambda t: t[0:L, :, :]
            hv = lambda p_, rows: p_[rows, 0:NH * L].rearrange("p (h t) -> p h t", h=NH)
            hd = lambda p_, rows: p_[rows, 0:NH * 128].rearrange("p (h d) -> p h d", h=NH)
            bview = lambda p_: p_[:, :].bitcast(BF16)
            mk = lambda nm: m[nm + sfx][0:L, 0:L].unsqueeze(1).to_broadcast([L, NH, L])
            AL = slice(0, L)
            FU = slice(0, 128)
            pc, pck = self.ps()
            self.tr(pc[0:L, 0:4], G[0:4, c0:c0 + L], idt[0:4, 0:4], r=[k('G'), 'ident'], w=[pck])
            self.tr(pc[0:L, 4:8], BETA[0:4, c0:c0 + L], idt[0:4, 0:4], r=[k('BETA'), 'ident'], w=[pck])
            Gcol, bcol, nGcol, nbeG, edec, glc, nbcol = [cols[0:L, i, h0:h0 + NH] for i in range(7)]
            ck = kk('cols')
            self.cp('dve', cols[0:L, 0:2, :], pc[0:L, 0:8].rearrange("p (a h) -> p a h", h=4), r=[pck], w=[ck])
            self.ts('dve', nGcol, Gcol, -1.0, MUL, r=[ck], w=[ck])
            self.ts('dve', nbcol, bcol, -1.0, MUL, r=[ck], w=[ck])
            pg, pgk = self.ps()
            self.mm(pg[0:L, 0:NH], selm, Gcol, True, True, r=[ck] + MK, w=[pgk])
            self.tt('dve', glc, pg[0:L, 0:NH], Gcol, SUB, r=[pgk, ck], w=[ck])
            self.actf(edec, glc, AF.Exp, r=[ck], w=[ck])
            self.actf(nbeG, Gcol, AF.Exp, r=[ck], w=[ck])
            self.tt('dve', nbeG, nbeG, nbcol, MUL, r=[ck], w=[ck])
            pm, pmk = self.ps()
            pbt, pbk = self.ps()
            for hh in range(NH):
                h = h0 + hh
                self.mm(pm[:, hh * L:(hh + 1) * L], m['selh'][0:4, h, :], G[0:4, c0:c0 + L], True, True, r=[k('G')] + MK, w=[pmk])
                self.mm(pbt[:, hh * L:(hh + 1) * L], m['selh'][0:4, h, :], BETA[0:4, c0:c0 + L], True, True, r=[k('BETA')] + MK, w=[pbk])
            pmv = hv(pm, FU)
            pbv = hv(pbt, FU)
            self.actf(W['eGbc'][:, :, 0:L], pmv, AF.Exp, r=[pmk], w=[kk('eGbc')])
            self.op('act', lambda e: e.mul(Wo['Zf'][0:L, :, 0:L], pbv[0:L], -1.0), r=[pbk], w=[kk('Zf')])
            for hh in range(NH):
                self.actf(W['E1a'][0:L, hh, 0:L], pmv[0:L, hh, :], AF.Exp, bias=Gcol[:, hh:hh + 1], scale=-1.0, r=[pmk, ck], w=[kk('E1a')])
                self.actf(W['E1b'][0:L, hh, 0:L], pmv[0:L, hh, :], AF.Exp, bias=nGcol[:, hh:hh + 1], scale=1.0, r=[pmk, ck], w=[kk('E1b')])
            self.stt(v3(W['Ds']), v3(W['E1a']), 1.0, mk('trils'), ALU.min, MUL, r=[kk('E1a')] + MK, w=[kk('Ds')])
            self.stt(v3(W['DTm']), v3(W['E1b']), 1.0, mk('triu'), ALU.min, MUL, r=[kk('E1b')] + MK, w=[kk('DTm')])
            self.stt(v3(W['DTs']), v3(W['E1b']), 1.0, mk('trius'), ALU.min, MUL, r=[kk('E1b')] + MK, w=[kk('DTs')])
            pkk, pkkk = self.ps()
            pqk, pqkk = self.ps()
            for hh in range(NH):
                h = h0 + hh
                kTh = qkv[:, 4 + h, c0:c0 + L]
                qTh = qkv[:, h, c0:c0 + L]
                self.mm(pkk[0:L, hh * L:(hh + 1) * L], kTh, kTh, True, True, r=[k('qkv')], w=[pkkk])
                self.mm(pqk[0:L, hh * L:(hh + 1) * L], kTh, qTh, True, True, r=[k('qkv')], w=[pqkk])
            pkv = hv(pkk, AL)
            pqv = hv(pqk, AL)
            self.tt('dve', v3(W['E1a']), pkv, v3(W['Ds']), MUL, r=[pkkk, kk('Ds')], w=[kk('E1a')])
            self.tt('dve', v3(Wo['Pm']), v3(W['E1a']), nbcol.unsqueeze(2).to_broadcast([L, NH, L]), MUL, r=[kk('E1a'), ck], w=[kk('Pm')])
            self.tt('dve', v3(W['E1b']), pkv, v3(W['DTs']), MUL, r=[pkkk, kk('DTs')], w=[kk('E1b')])
            self.tt('dve', v3(Wo['PTm']), v3(W['E1b']), v3(W['nbb']), MUL, r=[kk('E1b'), kk('Zf')], w=[kk('PTm')])
            self.tt('dve', v3(W['attnT']), pqv, v3(W['DTm']), MUL, r=[pqkk, kk('DTm')], w=[kk('attnT')])
            self.tt('dve', v3(Wo['Zf']), v3(W['PTm']), idt[0:L, 0:L].unsqueeze(1).to_broadcast([L, NH, L]), ADD, r=[kk('PTm'), 'ident'], w=[kk('Zf')])
            NF = 3
            for lev in range(nlev):
                if lev < NF:
                    Pa, PTa, Za, pk_, ptk_, zk_ = W['Pm'], W['PTm'], W['Zf'], kk('Pm'), kk('PTm'), kk('Zf')
                else:
                    Pa, PTa, Za, pk_, ptk_, zk_ = W['Ds'], W['DTs'], W['Zb'], kk('Ds'), kk('DTs'), kk('Zb')
                    if lev == NF:
                        self.cp('act', v3(W['Ds']), v3(W['Pm']), r=[kk('Pm')], w=[kk('Ds')])
                        self.cp('pool', v3(W['DTs']), v3(W['PTm']), r=[kk('PTm')], w=[kk('DTs')])
                    self.cp('act', v3(W['Zb']), v3(W['Zf']), r=[kk('Zf')], w=[kk('Zb')])
                pP, pPk = self.ps()
                pT, pTk = self.ps()
                for hh in range(NH):
                    self.mm(pP[0:L, hh * L:(hh + 1) * L], PTa[0:L, hh, 0:L], Pa[0:L, hh, 0:L], True, True, r=[pk_, ptk_], w=[pPk])
                    self.mm(pT[0:L, hh * L:(hh + 1) * L], Pa[0:L, hh, 0:L], PTa[0:L, hh, 0:L], True, True, r=[pk_, ptk_], w=[pTk])
                self.cp('act', v3(Pa), hv(pP, AL), r=[pPk], w=[pk_])
                self.cp('dve', v3(PTa), hv(pT, AL), r=[pTk], w=[ptk_])
                pZ, pZk = self.ps()
                for hh in range(NH):
                    self.mm(pZ[0:L, hh * L:(hh + 1) * L], Pa[0:L, hh, 0:L], Za[0:L, hh, 0:L], True, True, r=[pk_, zk_], w=[pZk])
                self.tt('dve', v3(W['Zf']), v3(W['Zf']), hv(pZ, AL), ADD, r=[kk('Zf'), pZk], w=[kk('Zf')])
            self.cp('act', v3(W['Zb']), v3(W['Zf']), r=[kk('Zf')], w=[kk('Zb')])
            ptk, ptkk = self.ps()
            ptv, ptvk = self.ps()
            for hh in range(NH):
                h = h0 + hh
                self.tr(bview(ptk)[0:L, hh * 128:(hh + 1) * 128], qkv[:, 4 + h, c0:c0 + L], idb[:, :], r=[k('qkv'), 'identb'], w=[ptkk])
                self.tr(bview(ptv)[0:L, hh * 128:(hh + 1) * 128], qkv[:, 8 + h, c0:c0 + L], idb[:, :], r=[k('qkv'), 'identb'], w=[ptvk])
            ktv = hd(bview(ptk), AL)
            vtv = hd(bview(ptv), AL)
            bc = lambda col: col.unsqueeze(2).to_broadcast([L, NH, 128])
            self.tt('dve', vt(W['nKbG']), ktv, bc(nbeG), MUL, r=[ptkk, ck], w=[kk('nKbG')])
            self.tt('dve', vt(W['Kdec']), ktv, bc(edec), MUL, r=[ptkk, ck], w=[kk('Kdec')])
            self.tt('dve', vt(W['Vb']), vtv, bc(bcol), MUL, r=[ptvk, ck], w=[kk('Vb')])
            pw, pwk = self.ps()
            for hh in range(NH):
                self.mm(pw[:, hh * L:(hh + 1) * L], W['nKbG'][0:L, hh, :], W['Zb'][0:L, hh, 0:L], True, True, r=[kk('nKbG'), kk('Zb')], w=[pwk])
            self.cp('act', vf(W['nWT']), hv(pw, FU), r=[pwk], w=[kk('nWT')])
            self.tt('dve', vf(W['qG']), qkv[:, h0:h0 + NH, c0:c0 + L], vf(W['eGbc']), MUL, r=[k('qkv'), kk('eGbc')], w=[kk('qG')])
            if kind != 's':
                pvn, pvnk = self.ps()
                for hh in range(NH):
                    self.mm(pvn[0:L, hh * 128:(hh + 1) * 128], W['Zb'][0:L, hh, 0:L], W['Vb'][0:L, hh, :], True, False, r=[kk('Zb'), kk('Vb')], w=[pvnk])
                    self.mm(pvn[0:L, hh * 128:(hh + 1) * 128], W['nWT'][:, hh, 0:L], W['Sb'][:, hh, :], False, True, r=[kk('nWT'), kk('Sb')], w=[pvnk])
                self.cp('act', vt(W['vnew']), hd(pvn, AL), r=[pvnk], w=[kk('vnew')])
                po, pok = self.ps()
                for hh in range(NH):
                    self.mm(po[:, hh * L:(hh + 1) * L], W['Sb'][:, hh, :], W['qG'][:, hh, 0:L], True, False, r=[kk('Sb'), kk('qG')], w=[pok])
                    self.mm(po[:, hh * L:(hh + 1) * L], W['vnew'][0:L, hh, :], W['attnT'][0:L, hh, 0:L], False, True, r=[kk('vnew'), kk('attnT')], w=[pok])
                pS, pSk = self.ps()
                for hh in range(NH):
                    self.mm(pS[:, hh * 128:(hh + 1) * 128], W['Kdec'][0:L, hh, :], W['vnew'][0:L, hh, :], True, True, r=[kk('Kdec'), kk('vnew')], w=[pSk])
                for hh in range(NH):
                    self.stt(W['S'][:, hh, :], W['S'][:, hh, :], W['eGbc'][:, hh, L - 1:L], pS[:, hh * 128:(hh + 1) * 128], MUL, ADD,
                             r=[kk('S'), kk('eGbc'), pSk], w=[kk('S')])
                self.cp('act', W['Sb'], W['S'], r=[kk('S')], w=[kk('Sb')])
            else:
                po, pok = self.ps(pin=True)
                id16 = m['id16']
                for h in range(4):
                    pvn, pvnk = self.ps()
                    self.mm(pvn[:, 0:128], W['Zb'][:, h, :], W['Vb'][:, h, :], True, False, r=[kk('Zb'), kk('Vb')], w=[pvnk])
                    for half in range(2):
                        mview = id16[:, 8 * half:8 * half + 8, :].unsqueeze(3).to_broadcast([128, 8, 16, 8])
                        self.tt('pool', nWTs.rearrange("p s (a t) -> p s a t", t=8),
                                W['nWT'][:, h, :].rearrange("p (a t) -> p a t", t=8).unsqueeze(1).to_broadcast([128, 8, 16, 8]),
                                mview, MUL, r=[kk('nWT')] + MK, w=[k('nWTs')])
                        self.tt('pool', qGs.rearrange("p s (a t) -> p s a t", t=8),
                                W['qG'][:, h, :].rearrange("p (a t) -> p a t", t=8).unsqueeze(1).to_broadcast([128, 8, 16, 8]),
                                mview, MUL, r=[kk('qG')] + MK, w=[k('qGs')])
                        self.dma(Sseg, P['st_dn_s'][l][8 * half:8 * half + 8, h].rearrange("s a b -> a s b"), w=[k('Sseg')])
                        self.cp('act', Ssegb, Sseg, r=[k('Sseg')], w=[k('Ssegb')])
                        for s8 in range(8):
                            s = 8 * half + s8
                            self.mm(pvn[:, 0:128], nWTs[:, s8, :], Ssegb[:, s8, :], False, (s == 15), r=[k('nWTs'), k('Ssegb')], w=[pvnk])
                            self.mm(po[:, h * 128:(h + 1) * 128], Ssegb[:, s8, :], qGs[:, s8, :], (s == 0), False, r=[k('qGs'), k('Ssegb')], w=[pok])
                    self.cp('act', W['vnew'][:, h, :], pvn[:, 0:128], r=[pvnk], w=[kk('vnew')])
                    self.mm(po[:, h * 128:(h + 1) * 128], W['vnew'][:, h, :], W['attnT'][:, h, :], False, True, r=[kk('vnew'), kk('attnT')], w=[pok])
                    for half in range(2):
                        self.tt('pool', Kds, W['Kdec'][:, h, :].unsqueeze(1).to_broadcast([128, 8, 128]),
                                m['ET'][:, 8 * half:8 * half + 8].unsqueeze(2).to_broadcast([128, 8, 128]), MUL, r=[kk('Kdec')] + MK, w=[k('Kds')])
                        self.dma(Sseg, P['st_dn_s'][l][8 * half:8 * half + 8, h].rearrange("s a b -> a s b"), w=[k('Sseg')])
                        for g2 in range(2):
                            pS, pSk = self.ps()
                            for s4 in range(4):
                                s8 = 4 * g2 + s4
                                self.mm(pS[:, s4 * 128:(s4 + 1) * 128], Kds[:, s8, :], W['vnew'][:, h, :], True, True, r=[k('Kds'), kk('vnew')], w=[pSk])
                            for s4 in range(4):
                                s8 = 4 * g2 + s4
                                s = 8 * half + s8
                                self.stt(Sseg[:, s8, :], Sseg[:, s8, :], W['eGbc'][:, h, 8 * s + 7:8 * s + 8], pS[:, s4 * 128:(s4 + 1) * 128], MUL, ADD,
                                         r=[k('Sseg'), kk('eGbc'), pSk], w=[k('Sseg')])
                        self.dma(O['s_dn_s'][l][8 * half:8 * half + 8, h].rearrange("s a b -> a s b"), Sseg, r=[k('Sseg')])
            pov = hv(po, FU)
            self.actf(vf(W['sqv']), pov, AF.Square, r=[pok], w=[kk('sqv')])
            pss, pssk = self.ps()
            for hh in range(NH):
                self.mm(pss[:, hh * L:(hh + 1) * L], onb[:, :], W['sqv'][:, hh, 0:L], True, True, r=[kk('sqv'), 'onesb'], w=[pssk])
            psv = hv(pss, FU)
            self.actf(vf(W['rstd']), psv, AF.Ln, bias=self.epsb[:, 0:1], scale=1.0 / 128, r=[pssk, 'epsb'], w=[kk('E1a')])
            self.actf(vf(W['rstd']), vf(W['rstd']), AF.Exp, scale=-0.5, r=[kk('E1a')], w=[kk('E1a')])
            self.stt(ob0[:, h0:h0 + NH, c0:c0 + L], pov, dnw[:, l:l + 1], vf(W['rstd']), MUL, MUL, r=[pok, kk('E1a'), 'dnw'], w=[kk('ob0')])
            self.unpin(pok)

        for (c0, L, kind) in CH:
            if kind == 's':
                chunk(c0, L, kind, 0, 4, cols2[0], '')
            else:
                self.interleave([lambda c0=c0, L=L, kind=kind: chunk(c0, L, kind, 0, 2, cols2[0], 'A'),
                                 lambda c0=c0, L=L, kind=kind: chunk(c0, L, kind, 2, 2, cols2[1], 'B')])
                if c0 + L == TP:
                    self.s.barrier()
                    self.dma(O['p_dn_s'][l].rearrange("h a b -> a h b"), Wfull['S'], w=[])
                    self.s.barrier()
        self.s.barrier()
        for h in range(4):
            def consz(gi, c0, n, pst, pk, h=h):
                self.actf(rtmp[:, 0:n], pst[:, 0:n], AF.Silu, r=[pk], w=[k('rtmp')])
                self.tt('dve', ob0[:, h, c0:c0 + n], ob0[:, h, c0:c0 + n], rtmp[:, 0:n], MUL, r=[k('rtmp'), k('ob0')], w=[k('ob0'), ('obT', 0)])
            self.proj(l, C_DNZ + h * 128, 128, consz)

    def ml_phase(self, l):
        if 'ml' not in self.phases:
            return self.zero_ob(1)
        P, O = self.P, self.O
        ar, hT = self.ar, self.hT
        ob1 = self.obT[1]
        idt, on, idb, onb = self.ident, self.ones, self.identb, self.onesb
        MUL, ADD, SUB, MAX = ALU.mult, ALU.add, ALU.subtract, ALU.max
        cv = Carver(ar, M0, ARENA)
        S_ = 'ml'
        k = lambda nm: (S_, nm)
        m = self.build_masks(cv, S_)
        MK = m['keys']
        ALPHA = cv.f32(TT)
        WI = cv.f32(TT)
        COLS = cv.f32(18, 12)
        Nseg = cv.f32(16, 4)
        Np = cv.f32(4)
        msc = cv.f32(8)
        minit = cv.f32(16)
        wlo = cv.p
        CH = [(0, 16, 'm')] + [(16 + 128 * i, 128, 'p') for i in range(16)] + [(TP, 128, 's')]
        IG, LF, B, M, GAMMA, MP, EM = [cv.f32(TT) for _ in range(7)]
        cres = cv.f32(TT)
        st64 = cv.f32(128)
        mo = cv.f32(16)
        self.memset('pool', cres[0:4, :], 1.0, w=[k('cres')])
        self.memset('pool', cres[0:4, 0:1], 0.0, w=[k('cres')])
        self.memset('pool', cres[0:4, 16:16 + 2048].rearrange("p (c t) -> p c t", t=128)[:, :, 0:1], 0.0, w=[k('cres')])
        self.memset('pool', cres[0:4, TP:TT].rearrange("p (c t) -> p c t", t=8)[:, :, 0:1], 0.0, w=[k('cres')])
        self.dma(msc[0:4, 0:1], P['ml_bias_i'][l:l + 1, :].rearrange("o h -> h o"), w=[k('msc0')])
        self.dma(msc[0:4, 1:2], P['ml_bias_f'][l:l + 1, :].rearrange("o h -> h o"), w=[k('msc1')])
        self.ts('dve', msc[0:4, 2:3], msc[0:4, 1:2], -1.0, MUL, r=[k('msc1')], w=[k('msc2')])
        self.dma(st64[0:16, 0:4], P['st_ml_m'][l], w=[k('st64')])
        pst, pk = self.ps()
        self.tr(pst[0:4, 0:16], st64[0:16, 0:4], idt[0:16, 0:16], r=[k('st64'), 'ident'], w=[pk])
        self.cp('dve', minit[0:4, :], pst[0:4, 0:16], r=[pk], w=[k('minit')])
        self.dma(st64[0:64, :], P['st_ml_n'][l].rearrange("s h d -> (s h) d"), r=[k('st64')], w=[k('st64')])
        pst, pk = self.ps()
        self.tr(pst[:, 0:64], st64[0:64, :], idt[0:64, 0:64], r=[k('st64'), 'ident'], w=[pk])
        self.cp('dve', Nseg, pst[:, 0:64].rearrange("p (s h) -> p s h", h=4), r=[pk], w=[k('Nseg')])

        def cons_i(gi, c0, n, pst, pk):
            self.actf(IG[0:4, c0:c0 + n], pst[0:4, 0:n], AF.Identity, bias=msc[0:4, 0:1], r=[pk, k('msc0')], w=[k('IG')])
        self.proj(l, C_MLI, 4, cons_i)

        def cons_f(gi, c0, n, pst, pk):
            self.actf(LF[0:4, c0:c0 + n], pst[0:4, 0:n], AF.Exp, bias=msc[0:4, 2:3], scale=-1.0, r=[pk, k('msc2')], w=[k('LF')])
        self.proj(l, C_MLF, 4, cons_f)
        self.actf(LF[0:4, :], LF[0:4, :], AF.Ln, bias=1.0, r=[k('LF')], w=[k('LF')])
        self.ts('dve', LF[0:4, :], LF[0:4, :], -1.0, MUL, r=[k('LF')], w=[k('LF')])
        self.scan(B[0:4, :], cres[0:4, :], LF[0:4, :], 0.0, r=[k('LF'), k('cres')], w=[k('B')])
        self.op('dve', lambda e: e.tensor_tensor_scan(M[0:4, 0:TP], LF[0:4, 0:TP], IG[0:4, 0:TP], -1e30, ADD, MAX),
                r=[k('LF'), k('IG')], w=[k('M')])
        for s in range(16):
            sl = slice(TP + 8 * s, TP + 8 * s + 8)
            self.op('dve', lambda e, sl=sl, s=s: e.tensor_tensor_scan(M[0:4, sl], LF[0:4, sl], IG[0:4, sl], minit[0:4, s:s + 1], ADD, MAX),
                    r=[k('LF'), k('IG'), k('minit')], w=[k('M')])
        self.tt('dve', ALPHA[0:4, :], B[0:4, :], M[0:4, :], SUB, r=[k('B'), k('M')], w=[k('ALPHA')])
        self.tt('pool', GAMMA[0:4, :], IG[0:4, :], B[0:4, :], SUB, r=[k('B'), k('IG')], w=[k('GAMMA')])
        self.actf(EM[0:4, :], M[0:4, :], AF.Exp, scale=-1.0, r=[k('M')], w=[k('EM')])
        for ci, (c0, L, kind) in enumerate(CH):
            if kind == 'm':
                self.memset('pool', MP[0:4, 0:L], -1e30, w=[k('MP')])
            elif kind == 'p':
                self.cp('pool', MP[0:4, c0:c0 + L], M[0:4, c0 - 1:c0].to_broadcast([4, L]), r=[k('M')], w=[k('MP')])
            else:
                self.cp('pool', MP[0:4, c0:c0 + L].rearrange("p (s t) -> p s t", t=8), minit[0:4, :].unsqueeze(2).to_broadcast([4, 16, 8]),
                        r=[k('minit')], w=[k('MP')])
        self.tt('dve', MP[0:4, :], MP[0:4, :], ALPHA[0:4, :], ADD, r=[k('MP'), k('ALPHA')], w=[k('MP')])
        self.actf(WI[0:4, :], MP[0:4, :], AF.Exp, r=[k('MP')], w=[k('WI')])
        for ci, (c0, L, kind) in enumerate(CH):
            pc, pck = self.ps()
            self.tr(pc[0:L, 0:4], GAMMA[0:4, c0:c0 + L], idt[0:4, 0:4], r=[k('GAMMA'), 'ident'], w=[pck])
            self.tr(pc[0:L, 4:8], EM[0:4, c0:c0 + L], idt[0:4, 0:4], r=[k('EM'), 'ident'], w=[pck])
            self.tr(pc[0:L, 8:12], ALPHA[0:4, c0:c0 + L], idt[0:4, 0:4], r=[k('ALPHA'), 'ident'], w=[pck])
            self.cp('dve', COLS[0:L, ci, :], pc[0:L, 0:12], r=[pck], w=[k('COLS')])
        self.dma(O['p_ml_m'][l:l + 1, :].rearrange("o h -> h o"), M[0:4, TP - 1:TP], r=[k('M')])
        self.cp('dve', mo[0:4, :], M[0:4, TP:TT].rearrange("p (s t) -> p s t", t=8)[:, :, 7], r=[k('M')], w=[k('mo')])
        pst, pk = self.ps()
        self.tr(pst[0:16, 0:4], mo[0:4, :], idt[0:4, 0:4], r=[k('mo'), 'ident'], w=[pk])
        self.cp('dve', st64[0:16, 0:4], pst[0:16, 0:4], r=[pk], w=[k('st64')])
        self.dma(O['s_ml_m'][l], st64[0:16, 0:4], r=[k('st64')])
        self.s.barrier()
        mlw = self.mlw
        for h0 in (0, 2):
            cv.p = wlo
            qkv = cv.bf16(6, TT)
            W = {}
            for nm in ['E1', 'hf', 'hq', 'wil']:
                W[nm] = cv.f32(2, 128)
            for nm in ['Wm', 'scT', 'qw', 'kws', 'hn', 'Cb']:
                W[nm] = cv.bf16(2, 128)
            v1 = cv.bf16(2, 129)
            C = cv.f32(2, 128)
            sc = cv.f32(16, 2)
            nb = cv.bf16(64)
            Cseg2 = [cv.f32(8, 128), cv.f32(8, 128)]
            Csegb2 = [cv.bf16(8, 128), cv.bf16(8, 128)]
            qws = cv.bf16(8, 128)
            kwss = cv.bf16(8, 128)
            for which in range(3):
                for hh in range(2):
                    def cons(gi, c0, n, pst, pk, which=which, hh=hh):
                        dst = qkv[:, which * 2 + hh, c0:c0 + n]
                        if which == 1:
                            self.op('act', lambda e: e.mul(dst, pst[:, 0:n], 128.0 ** -0.5), r=[pk], w=[k('qkv')])
                        else:
                            self.cp('act', dst, pst[:, 0:n], r=[pk], w=[k('qkv')])
                    self.proj(l, C_MLQKV + which * 512 + (h0 + hh) * 128, 128, cons)
            self.memset('pool', v1[:, :, 128:129], 1.0, w=[k('v1o')])
            self.memset('dve', C, 0.0, w=[k('C')])
            self.memset('pool', W['Cb'], 0.0, w=[k('Cb')])
            self.memset('dve', Np[:, h0:h0 + 2], 0.0, w=[k('Np')])
            self.memset('pool', nb[:, 0:2], 0.0, w=[k('nb')])
            for hh in range(2):
                self.cp('pool', nb[:, 2:34].rearrange("p (s h) -> p s h", h=2)[:, :, hh], Nseg[:, :, h0 + hh], r=[k('Nseg')], w=[k('nb')])
            Wfull = W

            def chunk(ci, c0, L, kind, hl, NH, kt):
                W = {nm: Wfull[nm][:, hl:hl + NH, :] for nm in Wfull}
                kk = lambda nm: k(nm + kt)
                v1_ = v1[:, hl:hl + NH, :]
                C_ = C[:, hl:hl + NH, :]
                sfx = '_s' if kind == 's' else ''
                selm = {'m': m['sel_m'][0:16, 0:16], 'p': m['sel_p'], 's': m['sel_s']}[kind]
                hg = h0 + hl
                gcol = COLS[0:L, ci, 0 + hg:0 + hg + NH]
                emcol = COLS[0:L, ci, 4 + hg:4 + hg + NH]
                scc = lambda i: sc[0:L, i, hl:hl + NH]
                v3 = lambda t: t[0:L, :, 0:L]
                vf = lambda t: t[:, :, 0:L]
                vt = lambda t: t[0:L, :, :]
                bview = lambda p_: p_[:, :].bitcast(BF16)
                hv = lambda p_, rows: p_[rows, 0:NH * L].rearrange("p (h t) -> p h t", h=NH)
                hd = lambda p_, rows: p_[rows, 0:NH * 128].rearrange("p (h d) -> p h d", h=NH)
                AL, FU = slice(0, L), slice(0, 128)
                pg, pgk = self.ps()
                self.mm(pg[0:L, 0:4], selm, COLS[0:L, ci, 8:12], True, True, r=[k('COLS')] + MK, w=[pgk])
                self.tt('dve', scc(0), pg[0:L, hg:hg + NH], gcol, ADD, r=[pgk, k('COLS')], w=[kk('sc0')])
                self.actf(scc(0), scc(0), AF.Exp, r=[kk('sc0')], w=[kk('sc0')])
                pm, pmk = self.ps()
                pwi, pwik = self.ps()
                pkq, pkqk = self.ps()
                for hh in range(NH):
                    h = hg + hh
                    self.mm(pm[:, hh * L:(hh + 1) * L], m['selh'][0:4, h, :], ALPHA[0:4, c0:c0 + L], True, True, r=[k('ALPHA')] + MK, w=[pmk])
                    self.mm(pwi[:, hh * L:(hh + 1) * L], m['selh'][0:4, h, :], WI[0:4, c0:c0 + L], True, True, r=[k('WI')] + MK, w=[pwik])
                    self.mm(pkq[0:L, hh * L:(hh + 1) * L], qkv[:, 2 + hl + hh, c0:c0 + L], qkv[:, hl + hh, c0:c0 + L], True, True, r=[k('qkv')], w=[pkqk])
                pmv = hv(pm, FU)
                pwv = hv(pwi, FU)
                pkv = hv(pkq, AL)
                for hh in range(NH):
                    self.actf(W['E1'][0:L, hh, 0:L], pmv[0:L, hh, :], AF.Exp, bias=gcol[:, hh:hh + 1], r=[pmk, k('COLS')], w=[kk('E1')])
                self.stt(v3(W['Wm']), v3(W['E1']), 1.0, m['triu' + sfx][0:L, 0:L].unsqueeze(1).to_broadcast([L, NH, L]), ALU.min, MUL,
                         r=[kk('E1')] + MK, w=[kk('Wm')])
                self.tt('dve', v3(W['scT']), pkv, v3(W['Wm']), MUL, r=[pkqk, kk('Wm')], w=[kk('scT')])
                self.tt('dve', vf(W['qw']), qkv[:, hl:hl + NH, c0:c0 + L], pwv, MUL, r=[k('qkv'), pwik], w=[kk('qw')])
                if kind == 's':
                    self.cp('act', W['wil'][:, :, 0:16], pwv.rearrange("p h (s t) -> p h s t", t=8)[:, :, :, 7], r=[pwik], w=[kk('wil')])
                else:
                    self.cp('act', W['wil'][:, :, 0:1], pwv[:, :, L - 1:L], r=[pwik], w=[kk('wil')])
                ptk, ptkk = self.ps()
                ptv, ptvk = self.ps()
                for hh in range(NH):
                    self.tr(bview(ptk)[0:L, hh * 128:(hh + 1) * 128], qkv[:, 2 + hl + hh, c0:c0 + L], idb[:, :], r=[k('qkv'), 'identb'], w=[ptkk])
                    self.tr(bview(ptv)[0:L, hh * 128:(hh + 1) * 128], qkv[:, 4 + hl + hh, c0:c0 + L], idb[:, :], r=[k('qkv'), 'identb'], w=[ptvk])
                ktv = hd(bview(ptk), AL)
                vtv = hd(bview(ptv), AL)
                self.tt('dve', vt(W['kws']), ktv, scc(0).unsqueeze(2).to_broadcast([L, NH, 128]), MUL, r=[ptkk, kk('sc0')], w=[kk('kws')])
                self.cp('act', v1_[0:L, :, 0:128], vtv, r=[ptvk], w=[kk('v1')])
                pn, pnk = self.ps(pin=True)
                pd, pdk = self.ps(pin=True)
                for hh in range(NH):
                    if kind != 's':
                        self.mm(pn[0:L, hh * 128:(hh + 1) * 128], W['qw'][:, hh, 0:L], W['Cb'][:, hh, :], True, False, r=[kk('qw'), kk('Cb')], w=[pnk])
                        self.mm(pn[0:L, hh * 128:(hh + 1) * 128], W['scT'][0:L, hh, 0:L], v1_[0:L, hh, 0:128], False, True, r=[kk('scT'), kk('v1')], w=[pnk])
                        self.mm(pd[0:L, hh:hh + 1], W['qw'][:, hh, 0:L], nb[:, hl + hh:hl + hh + 1], True, False, r=[kk('qw'), kk('nb')], w=[pdk])
                        self.mm(pd[0:L, hh:hh + 1], W['scT'][0:L, hh, 0:L], v1_[0:L, hh, 128:129], False, True, r=[kk('scT'), k('v1o')], w=[pdk])
                    else:
                        self.mm(pn[:, hh * 128:(hh + 1) * 128], W['scT'][:, hh, :], v1_[:, hh, 0:128], True, False, r=[kk('scT'), kk('v1')], w=[pnk])
                        self.mm(pd[:, hh:hh + 1], W['scT'][:, hh, :], v1_[:, hh, 128:129], True, False, r=[kk('scT'), k('v1o')], w=[pdk])
                        for half in range(2):
                            mview = m['id16'][:, 8 * half:8 * half + 8, :].unsqueeze(3).to_broadcast([128, 8, 16, 8])
                            self.tt('pool', qws.rearrange("p s (a t) -> p s a t", t=8),
                                    W['qw'][:, hh, :].rearrange("p (a t) -> p a t", t=8).unsqueeze(1).to_broadcast([128, 8, 16, 8]),
                                    mview, MUL, r=[kk('qw')] + MK, w=[k('qws')])
                            Cseg, Csegb = Cseg2[half], Csegb2[half]
                            self.dma(Cseg, P['st_ml_c'][l][8 * half:8 * half + 8, h0 + hh].rearrange("s a b -> a s b"), w=[k('Cseg%d' % half)])
                            self.cp('act', Csegb, Cseg, r=[k('Cseg%d' % half)], w=[k('Csegb%d' % half)])
                            for s8 in range(8):
                                s = 8 * half + s8
                                self.mm(pn[:, hh * 128:(hh + 1) * 128], qws[:, s8, :], Csegb[:, s8, :], False, (s == 15), r=[k('qws'), k('Csegb%d' % half)], w=[pnk])
                                self.mm(pd[:, hh:hh + 1], qws[:, s8, :], nb[:, 2 + s * 2 + hh:2 + s * 2 + hh + 1], False, (s == 15), r=[k('qws'), kk('nb')], w=[pdk])
                self.ts('dve', scc(3), pd[0:L, 0:NH], -1.0, MUL, r=[pdk], w=[kk('sc3')])
                self.tt('dve', scc(1), pd[0:L, 0:NH], scc(3), MAX, r=[pdk, kk('sc3')], w=[kk('sc1')])
                self.tt('dve', scc(1), scc(1), emcol, MAX, r=[kk('sc1'), k('COLS')], w=[kk('sc1')])
                self.op('dve', lambda e: e.reciprocal(scc(1), scc(1)), r=[kk('sc1')], w=[kk('sc1')])
                pnv = hd(pn, AL)
                self.tt('dve', vt(W['hf']), pnv, scc(1).unsqueeze(2).to_broadcast([L, NH, 128]), MUL, r=[pnk, kk('sc1')], w=[kk('hf')])
                self.unpin(pnk)
                self.unpin(pdk)
                self.tt('pool', vt(W['hq']), vt(W['hf']), vt(W['hf']), MUL, r=[kk('hf')], w=[kk('hq')])
                self.op('dve', lambda e: e.reduce_sum(scc(2), W['hq'][0:L, :, :], AX.X), r=[kk('hq')], w=[kk('sc2')])
                self.actf(scc(2), scc(2), AF.Ln, bias=self.epsb[0:L, 0:1], scale=1.0 / 128, r=[kk('sc2'), 'epsb'], w=[kk('sc2')])
                self.actf(scc(2), scc(2), AF.Exp, scale=-0.5, r=[kk('sc2')], w=[kk('sc2')])
                self.tt('dve', vt(W['hn']), vt(W['hf']), scc(2).unsqueeze(2).to_broadcast([L, NH, 128]), MUL, r=[kk('hf'), kk('sc2')], w=[kk('hn')])
                pht, phtk = self.ps()
                for hh in range(NH):
                    self.tr(bview(pht)[:, hh * 128:hh * 128 + L], W['hn'][0:L, hh, :], idb[0:L, 0:L], r=[kk('hn'), 'identb'], w=[phtk])
                phv = bview(pht)[:, 0:NH * 128].rearrange("p (h t) -> p h t", h=NH)[:, :, 0:L]
                self.ts('dve', ob1[:, hg:hg + NH, c0:c0 + L], phv, mlw[:, l:l + 1], MUL, r=[phtk, 'mlw'], w=[kk('ob1')])
                if kind != 's':
                    pS, pSk = self.ps()
                    for hh in range(NH):
                        self.mm(pS[:, hh * 129:(hh + 1) * 129], W['kws'][0:L, hh, :], v1_[0:L, hh, :], True, True, r=[kk('kws'), kk('v1'), k('v1o')], w=[pSk])
                    for hh in range(NH):
                        self.stt(C_[:, hh, :], C_[:, hh, :], W['wil'][:, hh, 0:1], pS[:, hh * 129:hh * 129 + 128], MUL, ADD, r=[kk('C'), kk('wil'), pSk], w=[kk('C')])
                        self.stt(Np[:, hg + hh:hg + hh + 1], Np[:, hg + hh:hg + hh + 1], W['wil'][:, hh, 0:1], pS[:, hh * 129 + 128:hh * 129 + 129], MUL, ADD,
                                 r=[kk('Np'), kk('wil'), pSk], w=[kk('Np')])
                    self.cp('act', W['Cb'], C_, r=[kk('C')], w=[kk('Cb')])
                    self.cp('act', nb[:, hl:hl + NH], Np[:, hg:hg + NH], r=[kk('Np')], w=[kk('nb')])
                else:
                    for hh in range(NH):
                        for half in range(2):
                            self.tt('pool', kwss, W['kws'][:, hh, :].unsqueeze(1).to_broadcast([128, 8, 128]),
                                    m['ET'][:, 8 * half:8 * half + 8].unsqueeze(2).to_broadcast([128, 8, 128]), MUL, r=[kk('kws')] + MK, w=[k('kwss')])
                            Cseg = Cseg2[half]
                            ck_ = k('Cseg%d' % half)
                            self.dma(Cseg, P['st_ml_c'][l][8 * half:8 * half + 8, h0 + hh].rearrange("s a b -> a s b"), w=[ck_])
                            for g3 in range(3):
                                pS, pSk = self.ps()
                                segs = [s8 for s8 in range(3 * g3, min(3 * g3 + 3, 8))]
                                for i3, s8 in enumerate(segs):
                                    self.mm(pS[:, i3 * 129:(i3 + 1) * 129], kwss[:, s8, :], v1_[:, hh, :], True, True, r=[k('kwss'), kk('v1'), k('v1o')], w=[pSk])
                                for i3, s8 in enumerate(segs):
                                    s = 8 * half + s8
                                    self.stt(Cseg[:, s8, :], Cseg[:, s8, :], W['wil'][:, hh, s:s + 1], pS[:, i3 * 129:i3 * 129 + 128], MUL, ADD,
                                             r=[ck_, kk('wil'), pSk], w=[ck_])
                                    self.stt(Nseg[:, s, h0 + hh:h0 + hh + 1], Nseg[:, s, h0 + hh:h0 + hh + 1], W['wil'][:, hh, s:s + 1],
                                             pS[:, i3 * 129 + 128:i3 * 129 + 129], MUL, ADD, r=[k('Nseg'), kk('wil'), pSk, kk('nb')], w=[k('Nseg')])
                            self.dma(O['s_ml_c'][l][8 * half:8 * half + 8, h0 + hh].rearrange("s a b -> a s b"), Cseg, r=[ck_])

            for ci, (c0, L, kind) in enumerate(CH):
                if kind == 's':
                    self.s.barrier()
                    self.dma(O['p_ml_c'][l][h0:h0 + 2].rearrange("h a b -> a h b"), C, w=[])
                    chunk(ci, c0, L, kind, 0, 2, '')
                else:
                    self.interleave([lambda ci=ci, c0=c0, L=L, kind=kind: chunk(ci, c0, L, kind, 0, 1, 'A'),
                                     lambda ci=ci, c0=c0, L=L, kind=kind: chunk(ci, c0, L, kind, 1, 1, 'B')])
            self.s.barrier()
        pst, pk = self.ps()
        self.tr(pst[0:4, 0:128], Np, idt[:, :], r=[k('Np'), 'ident'], w=[pk])
        self.cp('dve', st64[0:4, :], pst[0:4, 0:128], r=[pk], w=[k('st64')])
        self.dma(O['p_ml_n'][l], st64[0:4, :], r=[k('st64')])
        pst, pk = self.ps()
        self.tr(pst[0:64, 0:128], Nseg.rearrange("p s h -> p (s h)"), idt[:, :], r=[k('Nseg'), 'ident'], w=[pk])
        self.cp('dve', st64[0:64, :], pst[0:64, 0:128], r=[pk], w=[k('st64')])
        self.dma(O['s_ml_n'][l].rearrange("s h d -> (s h) d"), st64[0:64, :], r=[k('st64')])
        self.s.barrier()
        cv.p = wlo
        g1 = cv.f32(512)
        g2 = cv.f32(512)
        for h in range(4):
            def conso(gi, c0, n, pst, pk, h=h):
                self.actf(g1[:, 0:n], pst[:, 0:n], AF.Sigmoid, r=[pk], w=[k('g1')])
                self.tt('dve', ob1[:, h, c0:c0 + n], ob1[:, h, c0:c0 + n], g1[:, 0:n], MUL, r=[k('g1'), k('ob1')], w=[k('ob1')])
            self.proj(l, C_MLO + h * 128, 128, conso)

            def consz(gi, c0, n, pst, pk, h=h):
                self.actf(g2[:, 0:n], pst[:, 0:n], AF.Silu, r=[pk], w=[k('g2')])
                self.tt('dve', ob1[:, h, c0:c0 + n], ob1[:, h, c0:c0 + n], g2[:, 0:n], MUL, r=[k('g2'), k('ob1')], w=[k('ob1'), ('obT', 1)])
            self.proj(l, C_MLZ + h * 128, 128, consz)

    def s5_phase(self, l):
        if 's5' not in self.phases:
            return self.zero_ob(2)
        P, O = self.P, self.O
        ar, hT = self.ar, self.hT
        ob2 = self.obT[2]
        idt, on = self.ident, self.ones
        c1 = Carver(ar, 0, 2 * OB_W)
        cv = Carver(ar, M0, ARENA)
        uT = c1.bf16(4, TT)
        COS = cv.f32(16, 128)
        SIN = cv.f32(16, 128)
        BTf = cv.bf16(4096)
        CTf = cv.bf16(4096)
        XRf = cv.f32(4096)
        BT = BTf.rearrange("p (a r c) -> p a r c", r=2, c=128)
        CT = CTf.rearrange("p (a r c) -> p a r c", r=2, c=128)
        XR = XRf.rearrange("p (a r c) -> p a r c", r=2, c=128)
        wglu = cv.bf16(4, 512)
        wz = cv.bf16(8, 512)
        sm = {}
        for nm in ['lre', 'lim', 'dtb', 'are', 'th', 'mag', 'c', 's', 'cc', 'ss', 'cs', 'lbr', 'lbi', 'nr', 'den',
                   'fr', 'fi', 't1', 't2', 'cr', 'ci', 'crn', 'cin']:
            sm[nm] = c1.f32(16)
        dtbc = c1.f32(32)
        drow = c1.f32(32)
        Brn, Bin_ = c1.f32(16, 16), c1.f32(16, 16)
        bbr, bbi = c1.f32(16, 16), c1.f32(16, 16)
        tA, tB = c1.f32(16, 16), c1.f32(16, 16)
        CN = [c1.f32(16, 16), c1.f32(16, 16)]
        cin_t = c1.f32(128)
        H0 = [c1.f32(16, 16), c1.f32(16, 16)]
        MH = [c1.f32(16, 16), c1.f32(16, 16)]
        HS = [c1.f32(16, 16), c1.f32(16, 16)]
        st16 = cv.f32(2048)
        tmpS = cv.f32(128)
        W1, W2, W3, W4 = cv.f32(4, 128), cv.f32(4, 128), cv.f32(4, 128), cv.f32(4, 128)
        ZR, ZI = cv.f32(4, 128), cv.f32(4, 128)
        FR, FI = cv.f32(4, 128), cv.f32(4, 128)
        HRb, HIb = cv.bf16(4, 128), cv.bf16(4, 128)
        y5, yq, yt = cv.f32(128), cv.f32(128), cv.f32(128)
        sgb, szb = cv.f32(512), cv.f32(512)
        dt_all = cv.f32(2048)
        dtmp1 = dt_all[:, 0:1024].rearrange("p (a c) -> p a c", a=16)
        dtmp2 = dt_all[:, 1024:2048].rearrange("p (a c) -> p a c", a=16)
        tf = dt_all.rearrange("p (f n) -> p f n", f=4)

        S = 's5'
        k = lambda nm: (S, nm)
        for fo in range(4):
            wb, wk = self.wblock(P['w_glu'][l][:, fo * 128:(fo + 1) * 128], 4, 128)
            self.cp('act', wglu[:, :, fo * 128:(fo + 1) * 128], wb[:, 0:4, :], r=[wk], w=[k('wglu')])
            wb, wk = self.wblock(self.w_in[l][:, C_S5Z + fo * 128:C_S5Z + (fo + 1) * 128], 8, 128)
            self.cp('act', wz[:, :, fo * 128:(fo + 1) * 128], wb[:, :, :], r=[wk], w=[k('wz')])
        for q in range(4):
            def cons(gi, c0, n_, pst, pk, q=q):
                self.cp('act', uT[:, q, c0:c0 + n_], pst[:, 0:n_], r=[pk], w=[k('uT%d' % q)])
            self.proj(l, C_S5U + 128 * q, 128, cons)
        self.load_cols(P['lam_re'][l].rearrange("(a b) p -> a (b p)", b=2), 16, sm['lre'], (S, 'lre'))
        self.load_cols(P['lam_im'][l].rearrange("(a b) p -> a (b p)", b=2), 16, sm['lim'], (S, 'lim'))
        self.dma(drow[0:1, :], P['log_dt'][l:l + 1, :], w=[(S, 'drow')])
        pst, pk = self.ps()
        self.mm(pst[:, 0:32], on[0:1, :], drow[0:1, :], True, True, r=[(S, 'drow'), 'ones'], w=[pk])
        self.actf(dtbc, pst[:, 0:32], AF.Exp, r=[pk], w=[(S, 'dtbc')])
        dv = dtbc.rearrange("p (a b) -> p a b", b=2)
        self.cp('dve', sm['dtb'][0:64, :], dv[0:64, :, 0], r=[(S, 'dtbc')], w=[(S, 'dtb')])
        self.cp('dve', sm['dtb'][64:128, :], dv[64:128, :, 1], r=[(S, 'dtbc')], w=[(S, 'dtb')])
        k = lambda nm: (S, nm)

        def T2(eng, o, a, b, op):
            self.tt(eng, sm[o], sm[a], sm[b], op, r=[k(a), k(b)], w=[k(o)])
        T2('dve', 'are', 'lre', 'dtb', ALU.mult)
        T2('dve', 'th', 'lim', 'dtb', ALU.mult)
        self.actf(sm['mag'], sm['are'], AF.Exp, r=[k('are')], w=[k('mag')])
        self.actf(sm['s'], sm['th'], AF.Sin, scale=1.0 / 16, r=[k('th')], w=[k('s')])
        self.actf(sm['t1'], sm['th'], AF.Sin, scale=1.0 / 32, r=[k('th')], w=[k('t1')])
        self.dump('th', sm['th'], [k('th')])
        self.dump('s16', sm['s'], [k('s')])
        self.dump('s32', sm['t1'], [k('t1')])
        T2('dve', 't2', 't1', 't1', ALU.mult)
        self.ts('dve', sm['c'], sm['t2'], -2.0, ALU.mult, 1.0, ALU.add, r=[k('t2')], w=[k('c')])
        self.dump('c16', sm['c'], [k('c')])
        for _ in range(4):
            T2('dve', 'cc', 'c', 'c', ALU.mult)
            T2('dve', 'ss', 's', 's', ALU.mult)
            T2('dve', 'cs', 'c', 's', ALU.mult)
            T2('dve', 'c', 'cc', 'ss', ALU.subtract)
            self.ts('dve', sm['s'], sm['cs'], 2.0, ALU.mult, r=[k('cs')], w=[k('s')])
        T2('dve', 'lbr', 'mag', 'c', ALU.mult)
        T2('dve', 'lbi', 'mag', 's', ALU.mult)
        self.ts('dve', sm['nr'], sm['lbr'], -1.0, ALU.add, r=[k('lbr')], w=[k('nr')])
        T2('dve', 'cc', 'lre', 'lre', ALU.mult)
        T2('dve', 'ss', 'lim', 'lim', ALU.mult)
        T2('dve', 'den', 'cc', 'ss', ALU.add)
        self.op('dve', lambda e: e.reciprocal(sm['den'], sm['den']), r=[k('den')], w=[k('den')])
        T2('dve', 't1', 'nr', 'lre', ALU.mult)
        T2('dve', 't2', 'lbi', 'lim', ALU.mult)
        T2('dve', 't1', 't1', 't2', ALU.add)
        T2('dve', 'fr', 't1', 'den', ALU.mult)
        T2('dve', 't1', 'lbi', 'lre', ALU.mult)
        T2('dve', 't2', 'nr', 'lim', ALU.mult)
        T2('dve', 't1', 't1', 't2', ALU.subtract)
        T2('dve', 'fi', 't1', 'den', ALU.mult)
        self.dma(Brn, P['b_re'][l].rearrange("(a b) p c -> (b p) a c", b=2), w=[k('Brn')])
        self.dma(Bin_, P['b_im'][l].rearrange("(a b) p c -> (b p) a c", b=2), w=[k('Bin')])
        frb = sm['fr'].unsqueeze(2).to_broadcast([128, 16, 16])
        fib = sm['fi'].unsqueeze(2).to_broadcast([128, 16, 16])
        self.tt('dve', tA, Brn, frb, ALU.mult, r=[k('Brn'), k('fr')], w=[k('tA')])
        self.tt('dve', tB, Bin_, fib, ALU.mult, r=[k('Bin'), k('fi')], w=[k('tB')])
        self.tt('dve', bbr, tA, tB, ALU.subtract, r=[k('tA'), k('tB')], w=[k('bbr')])
        self.tt('dve', tA, Bin_, frb, ALU.mult, r=[k('Bin'), k('fr'), k('bbr')], w=[k('tA')])
        self.tt('dve', tB, Brn, fib, ALU.mult, r=[k('Brn'), k('fi'), k('bbr')], w=[k('tB')])
        self.tt('dve', bbi, tA, tB, ALU.add, r=[k('tA'), k('tB')], w=[k('bbi')])

        def pad_fill(srcs_, key):
            self.memset('pool', XRf, 0.0, w=[k('XR')])
            for ri in range(2):
                for b in range(2):
                    base = XR[64 * b:64 * b + 64, :, ri, :]
                    d4 = base.rearrange("p (q j) c -> p q j c", j=4)
                    for j in range(4):
                        dst = d4[:, :, j, 32 * j + 16 * b:32 * j + 16 * b + 16]
                        src = srcs_[ri][64 * b:64 * b + 64, :, :].rearrange("p (q j) c -> p q j c", j=4)[:, :, j, :]
                        self.cp('pool', dst, src, r=[key[ri], k('XR')], w=[k('XR')])
        pad_fill([bbr, bbi], [k('bbr'), k('bbi')])
        for a in range(16):
            for ri in range(2):
                if (a * 2 + ri) % 4 == 0:
                    pst, pk = self.ps()
                j = (a * 2 + ri) % 4
                self.tr(pst[:, j * 128:(j + 1) * 128], XR[:, a, ri, :], idt[:, :], r=[k('XR'), 'ident'], w=[pk])
                if j == 3:
                    a0 = a - 1
                    self.cp('act', BTf[:, a0 * 256:(a0 + 2) * 256], pst[:, :], r=[pk], w=[k('BT')])
        for ri, nm in enumerate(['c_re', 'c_im']):
            for h in range(2):
                for a8 in range(8):
                    a = 8 * h + a8
                    self.dma(cin_t[16 * a8:16 * a8 + 16, :].rearrange("o (b p) -> o b p", b=2),
                             P[nm][l][2 * a:2 * a + 2].rearrange("b o p -> o b p"), w=[k('cin')])
                pst, pk = self.ps()
                self.tr(pst[:, 0:128], cin_t[:, :], idt[:, :], r=[k('cin'), 'ident'], w=[pk])
                dst = CN[ri][:, 8 * h:8 * h + 8, :]
                if ri == 0:
                    self.cp('act', dst, pst[:, 0:128].rearrange("p (a o) -> p a o", o=16), r=[pk], w=[k('CN%d' % ri)])
                else:
                    self.op('act', lambda e, dst=dst, pst=pst: e.mul(dst, pst[:, 0:128].rearrange("p (a o) -> p a o", o=16), -1.0),
                            r=[pk], w=[k('CN%d' % ri)])
        pad_fill(CN, [k('CN0'), k('CN1')])
        self.cp('dve', CTf, XRf, r=[k('XR')], w=[k('CT')])
        self.cp('dve', COS[:, :, 0:1], sm['c'].unsqueeze(2), r=[k('c')], w=[k('COS')])
        self.cp('dve', SIN[:, :, 0:1], sm['s'].unsqueeze(2), r=[k('s')], w=[k('SIN')])
        n = 1
        while n < 128:
            C0, S0 = COS[:, :, 0:n], SIN[:, :, 0:n]
            cm = COS[:, :, n - 1:n].to_broadcast([128, 16, n])
            smm = SIN[:, :, n - 1:n].to_broadcast([128, 16, n])
            d1, d2 = dtmp1[:, :, 0:n], dtmp2[:, :, 0:n]
            rk = [k('COS'), k('SIN')]
            self.tt('dve', d1, C0, cm, ALU.mult, r=rk, w=[k('d1')])
            self.tt('pool', d2, S0, smm, ALU.mult, r=rk, w=[k('d2')])
            self.tt('dve', COS[:, :, n:2 * n], d1, d2, ALU.subtract, r=[k('d1'), k('d2')], w=[k('COS')])
            self.tt('dve', d1, S0, cm, ALU.mult, r=rk, w=[k('d1')])
            self.tt('pool', d2, C0, smm, ALU.mult, r=rk, w=[k('d2')])
            self.tt('dve', SIN[:, :, n:2 * n], d1, d2, ALU.add, r=[k('d1'), k('d2')], w=[k('SIN')])
            n *= 2
        for nm_ in ['lre', 'lim', 'dtb', 'mag', 'c', 's', 'fr', 'fi']:
            self.dump(nm_, sm[nm_], [k(nm_)])
        self.dump('COS', COS, [k('COS')])
        self.dump('SIN', SIN, [k('SIN')])
        self.dump('bbr', bbr, [k('bbr')])
        self.dump('CN0', CN[0], [k('CN0')])
        self.dump('CN1', CN[1], [k('CN1')])
        self.dump('XR', XRf, [k('XR')])
        for ri, nm in enumerate(['st_s5_re', 'st_s5_im']):
            self.dma(st16[0:16, :], P[nm][l], w=[k('st16')])
            pst, pk = self.ps()
            for a in range(16):
                self.tr(pst[:, a * 16:(a + 1) * 16], st16[0:16, a * 128:(a + 1) * 128], idt[0:16, 0:16],
                        r=[k('st16'), 'ident'], w=[pk])
            self.cp('dve', H0[ri], pst[:, 0:256].rearrange("p (a s) -> p a s", s=16), r=[pk], w=[k('H0%d' % ri)])
            self.tt('dve', MH[ri], H0[ri], sm['mag'].unsqueeze(2).to_broadcast([128, 16, 16]), ALU.mult,
                    r=[k('H0%d' % ri), k('mag')], w=[k('MH%d' % ri)])
        self.memset('dve', sm['cr'], 0.0, w=[k('cr')])
        self.memset('dve', sm['ci'], 0.0, w=[k('ci')])
        MUL, ADD, SUB = ALU.mult, ALU.add, ALU.subtract
        self.s.barrier()
        cx = Carver(ar, 0, 0)
        xoff = [0]

        def xal(n):
            v = XRf[:, xoff[0]:xoff[0] + n]
            xoff[0] += n
            assert xoff[0] <= 4096
            return v
        MW = [xal(512).rearrange("p (a t) -> p a t", a=4) for _ in range(4)]
        ZRs = [ZR, xal(512).rearrange("p (a t) -> p a t", a=4)]
        ZIs = [ZI, xal(512).rearrange("p (a t) -> p a t", a=4)]
        y5s = [y5, xal(128), xal(128)]
        its = [(c0, n, is_s, q) for (c0, n, is_s) in FRAMES for q in range(4)]
        NI = len(its)
        tk = [k('COS'), k('SIN')]

        def views(n, is_s, q):
            if not is_s:
                v = lambda t: t[:, :, 0:n]
                pv = lambda p_: p_[:, :].rearrange("p (a t) -> p a t", a=4)[:, :, 0:n]
                C4, S4 = COS[:, 4 * q:4 * q + 4, 0:n], SIN[:, 4 * q:4 * q + 4, 0:n]
            else:
                v = lambda t: t.rearrange("p a (s t) -> p a s t", t=8)
                pv = lambda p_: p_[:, :].rearrange("p (a s t) -> p a s t", a=4, t=8)
                C4 = COS[:, 4 * q:4 * q + 4, 0:8].unsqueeze(2).to_broadcast([128, 4, 16, 8])
                S4 = SIN[:, 4 * q:4 * q + 4, 0:8].unsqueeze(2).to_broadcast([128, 4, 16, 8])
            return v, pv, C4, S4

        def stage_M(i):
            c0, n, is_s, q = its[i]
            par = i % 2
            ZRp, ZIp = ZRs[par], ZIs[par]
            zrk, zik = k('ZR%d' % par), k('ZI%d' % par)
            pzr, pzrk = self.ps()
            pzi, pzik = self.ps()
            for j in range(4):
                a = 4 * q + j
                self.mm(pzr[:, j * 128:j * 128 + n], BT[:, a, 0, :], uT[:, q, c0:c0 + n], True, True, r=[k('BT'), k('uT%d' % q)], w=[pzrk])
                self.mm(pzi[:, j * 128:j * 128 + n], BT[:, a, 1, :], uT[:, q, c0:c0 + n], True, True, r=[k('BT'), k('uT%d' % q)], w=[pzik])
            v, pv, C4, S4 = views(n, is_s, q)
            self.tt('dve', v(MW[0]), pv(pzr), C4, MUL, r=[pzrk] + tk, w=[k('MW0')])
            self.tt('dve', v(MW[1]), pv(pzi), S4, MUL, r=[pzik] + tk, w=[k('MW1')])
            self.tt('pool', v(ZRp), v(MW[0]), v(MW[1]), ADD, r=[k('MW0'), k('MW1')], w=[zrk])
            self.tt('dve', v(MW[2]), pv(pzi), C4, MUL, r=[pzik] + tk, w=[k('MW2')])
            self.tt('dve', v(MW[3]), pv(pzr), S4, MUL, r=[pzrk] + tk, w=[k('MW3')])
            self.tt('pool', v(ZIp), v(MW[2]), v(MW[3]), SUB, r=[k('MW2'), k('MW3')], w=[zik])
            if is_s:
                zr0 = ZRp.rearrange("p a (s t) -> p a s t", t=8)[:, :, :, 0]
                zi0 = ZIp.rearrange("p a (s t) -> p a s t", t=8)[:, :, :, 0]
                self.tt('pool', zr0, zr0, MH[0][:, 4 * q:4 * q + 4, :], ADD, r=[zrk, k('MH0')], w=[zrk])
                self.tt('pool', zi0, zi0, MH[1][:, 4 * q:4 * q + 4, :], ADD, r=[zik, k('MH1')], w=[zik])

        def stage_S(i):
            c0, n, is_s, q = its[i]
            par = i % 2
            ZRp, ZIp = ZRs[par], ZIs[par]
            zrk, zik = k('ZR%d' % par), k('ZI%d' % par)
            y5p, y5k = y5s[par], k('y5%d' % par)
            v, pv, C4, S4 = views(n, is_s, q)
            if is_s and q == 0:
                self.cp('dve', sm['crn'], sm['cr'], r=[k('cr')], w=[k('crn')])
                self.cp('dve', sm['cin'], sm['ci'], r=[k('ci')], w=[k('cin2')])
            for j in range(4):
                a = 4 * q + j
                if is_s:
                    self.ts('dve', tmpS[:, :], self.segmask[:, :], sm['mag'][:, a:a + 1], MUL, r=['segmask', k('mag')], w=[k('tmpS')])
                    coef = tmpS[:, 0:n]
                    ir, ii = 0.0, 0.0
                    rk = [k('tmpS')]
                else:
                    coef = sm['mag'][:, a:a + 1].to_broadcast([128, n])
                    ir, ii = sm['cr'][:, a:a + 1], sm['ci'][:, a:a + 1]
                    rk = [k('mag'), k('cr'), k('ci')]
                self.scan(FR[:, j, 0:n], coef, ZRp[:, j, 0:n], ir, r=[zrk] + rk, w=[k('FR')])
                self.scan(FI[:, j, 0:n], coef, ZIp[:, j, 0:n], ii, r=[zik] + rk, w=[k('FI')])
            self.tt('dve', v(W1), v(FR), C4, MUL, r=[k('FR')] + tk, w=[k('W1')])
            self.tt('pool', v(W2), v(FI), S4, MUL, r=[k('FI')] + tk, w=[k('W2')])
            self.tt('dve', v(W3), v(FI), C4, MUL, r=[k('FI')] + tk, w=[k('W3')])
            self.tt('pool', v(W4), v(FR), S4, MUL, r=[k('FR')] + tk, w=[k('W4')])

        def stage_S2(i):
            c0, n, is_s, q = its[i]
            par = i % 3
            y5p, y5k = y5s[par], k('y5%d' % par)
            v, pv, C4, S4 = views(n, is_s, q)
            self.tt('dve', v(HRb), v(W1), v(W2), SUB, r=[k('W1'), k('W2')], w=[k('HRb')])
            self.tt('dve', v(HIb), v(W3), v(W4), ADD, r=[k('W3'), k('W4')], w=[k('HIb')])
            if is_s:
                l7 = lambda t: t.rearrange("p a (s t) -> p a s t", t=8)[:, :, :, 7]
                self.tt('dve', HS[0][:, 4 * q:4 * q + 4, :], l7(W1), l7(W2), SUB, r=[k('W1'), k('W2')], w=[k('HS0')])
                self.tt('dve', HS[1][:, 4 * q:4 * q + 4, :], l7(W3), l7(W4), ADD, r=[k('W3'), k('W4')], w=[k('HS1')])
            else:
                self.tt('dve', sm['cr'][:, 4 * q:4 * q + 4], W1[:, :, n - 1], W2[:, :, n - 1], SUB, r=[k('W1'), k('W2')], w=[k('cr')])
                self.tt('dve', sm['ci'][:, 4 * q:4 * q + 4], W3[:, :, n - 1], W4[:, :, n - 1], ADD, r=[k('W3'), k('W4')], w=[k('ci')])
            py, pyk = self.ps()
            for j in range(4):
                a = 4 * q + j
                self.mm(py[:, 0:n], CT[:, a, 0, :], HRb[:, j, 0:n], j == 0, False, r=[k('CT'), k('HRb')], w=[pyk])
                self.mm(py[:, 0:n], CT[:, a, 1, :], HIb[:, j, 0:n], False, j == 3, r=[k('CT'), k('HIb')], w=[pyk])
            dq = self.dcol[:, l * 4 + q:l * 4 + q + 1]
            self.stt(y5p[:, 0:n], uT[:, q, c0:c0 + n], dq, py[:, 0:n], MUL, ADD, r=[k('uT%d' % q), 'dcol', pyk], w=[y5k])

        def stage_Y(i):
            c0, n, is_s, q = its[i]
            par = i % 3
            y5p, y5k = y5s[par], k('y5%d' % par)
            self.actf(yq[:, 0:n], y5p[:, 0:n], AF.Square, r=[y5k], w=[k('yq')])
            self.actf(yq[:, 0:n], yq[:, 0:n], AF.Identity, bias=self.oneb[:, 0:1], scale=0.044715, r=[k('yq'), 'oneb'], w=[k('yq')])
            self.tt('dve', yt[:, 0:n], yq[:, 0:n], y5p[:, 0:n], MUL, r=[k('yq'), y5k], w=[k('yt')])
            self.actf(yt[:, 0:n], yt[:, 0:n], AF.Sigmoid, scale=1.5957691216057308, r=[k('yt')], w=[k('yt')])
            self.tt('dve', ob2[:, q, c0:c0 + n], y5p[:, 0:n], yt[:, 0:n], MUL, r=[y5k, k('yt')], w=[('y5g', q)])

        stage_M(0)
        if NI > 1:
            stage_M(1)
        for i in range(NI + 2):
            if i < NI:
                stage_S(i)
                if i + 2 < NI:
                    stage_M(i + 2)
                stage_S2(i)
            if 0 <= i - 2 < NI:
                stage_Y(i - 2)
        self.s.barrier()
        for gi, (c0, n) in enumerate(TGS):
            for fo in range(4):
                pa, pak = self.ps()
                for kc in range(4):
                    self.mm(pa[:, 0:n], wglu[:, kc, fo * 128:(fo + 1) * 128], ob2[:, kc, c0:c0 + n], kc == 0, kc == 3,
                            r=[k('wglu'), ('y5g', kc)], w=[pak])
                pb, pbk = self.ps()
                for fc in range(8):
                    self.mm(pb[:, 0:n], wz[:, fc, fo * 128:(fo + 1) * 128], hT[:, fc, c0:c0 + n], fc == 0, fc == 7,
                            r=[k('wz'), ('hT', fc, gi)], w=[pbk])
                bcol = self.bglu[:, l * 4 + fo:l * 4 + fo + 1]
                self.actf(sgb[:, 0:n], pa[:, 0:n], AF.Sigmoid, bias=bcol, r=[pak, 'bglu'], w=[k('sgb')])
                self.actf(szb[:, 0:n], pb[:, 0:n], AF.Sigmoid, r=[pbk], w=[k('szb')])
                self.tt('pool', szb[:, 0:n], szb[:, 0:n], sgb[:, 0:n], MUL, r=[k('sgb'), k('szb')], w=[k('szb')])
                self.tt('dve', tf[:, fo, 0:n], pb[:, 0:n], szb[:, 0:n], MUL, r=[pbk, k('szb')], w=[k('tf%d' % fo)])
            for fo in range(4):
                self.tt('dve', ob2[:, fo, c0:c0 + n], ob2[:, fo, c0:c0 + n], tf[:, fo, 0:n], MUL,
                        r=[('y5g', fo), k('tf%d' % fo)], w=[('y5g', fo), ('obT', 2)])
        for ri, (nm_p, nm_s, src_p) in enumerate([('p_s5_re', 's_s5_re', 'crn'), ('p_s5_im', 's_s5_im', 'cin')]):
            pst, pk = self.ps()
            pkey = k('crn') if ri == 0 else k('cin2')
            self.tr(pst[0:16, 0:128], sm[src_p], idt[:, :], r=[pkey, 'ident'], w=[pk])
            self.cp('dve', st16[0:16, 0:128], pst[0:16, 0:128], r=[pk], w=[k('st16')])
            self.dma(O[nm_p][l], st16[0:16, 0:128], r=[k('st16')])
            for qq in range(4):
                pst, pk = self.ps()
                for j in range(4):
                    a = 4 * qq + j
                    self.tr(pst[0:16, j * 128:(j + 1) * 128], HS[ri][:, a, :], idt[:, :], r=[k('HS%d' % ri), 'ident'], w=[pk])
                self.cp('dve', st16[0:16, qq * 512:(qq + 1) * 512], pst[0:16, :], r=[pk], w=[k('st16')])
            self.dma(O[nm_s][l], st16[0:16, :], r=[k('st16')])


def build_program(nlayers=DEPTH, dbg=None, phases=('s5', 'dn', 'ml')):
    return K(nlayers, dbg, phases).build()


def make_in_maps(inp):
    maps = []
    c_ = np.ascontiguousarray
    for c in range(8):
        sl = slice(16 * c, 16 * c + 16)
        m = {
            'xp': c_(inp['x_prompt'][c]),
            'xs': c_(inp['x_sample'][sl].reshape(TS, D)),
            'meta': c_(inp['meta_tokens']),
            'norm_w': c_(inp['norm_w']),
            'w_in': c_(inp['w_in']),
            'b_gate': c_(inp['b_gate']),
            'w_br0': c_(inp['w_branch_dn']),
            'w_br1': c_(inp['w_branch_ml']),
            'w_br2': c_(inp['w_branch_s5']),
            'w_out': c_(inp['w_out']),
            'fnw': c_(inp['final_norm_w'].reshape(1, D)),
            'lam_re': c_(inp['s5_lambda_re']), 'lam_im': c_(inp['s5_lambda_im']), 'log_dt': c_(inp['s5_log_dt']),
            'b_re': c_(inp['s5_b_re']), 'b_im': c_(inp['s5_b_im']), 'c_re': c_(inp['s5_c_re']), 'c_im': c_(inp['s5_c_im']),
            's5_d': c_(inp['s5_d']), 'w_glu': c_(inp['s5_w_glu']), 'b_glu': c_(inp['s5_b_glu']),
            'st_s5_re': c_(inp['state_s5_re'][:, sl].reshape(DEPTH, NSEQ, 2048)),
            'st_s5_im': c_(inp['state_s5_im'][:, sl].reshape(DEPTH, NSEQ, 2048)),
            'dn_conv_w': c_(inp['dn_conv_w']), 'dn_a_log': c_(inp['dn_a_log']), 'dn_dt_bias': c_(inp['dn_dt_bias']),
            'dn_norm_w': c_(inp['dn_norm_w']),
            'st_dn_conv': c_(inp['state_dn_conv'][:, sl]), 'st_dn_s': c_(inp['state_dn_s'][:, sl]),
            'ml_bias_i': c_(inp['ml_bias_i']), 'ml_bias_f': c_(inp['ml_bias_f']), 'ml_norm_w': c_(inp['ml_norm_w']),
            'st_ml_c': c_(inp['state_ml_c'][:, sl]), 'st_ml_n': c_(inp['state_ml_n'][:, sl]), 'st_ml_m': c_(inp['state_ml_m'][:, sl]),
        }
        maps.append(m)
    return maps


def kernel(**inp):
    inp = {k: np.asarray(v) for k, v in inp.items()}
    import os
    ph = os.environ.get('K_PHASES')
    nc = build_program() if ph is None else build_program(nlayers=int(os.environ.get('K_NL', '1')), phases=tuple(p for p in ph.split(',') if p))
    res = run_bass_kernel_spmd(nc, make_in_maps(inp), core_ids=list(range(8)))
    r = res.results
    f = np.float32
    y_p = np.stack([r[c]['y_p'] for c in range(8)]).astype(f)
    y_s = np.concatenate([r[c]['y_s'].reshape(16, 8, D) for c in range(8)]).astype(f)

    def pst(name, shape):
        return np.stack([r[c][name].reshape((DEPTH,) + shape) for c in range(8)], axis=1).astype(f)

    def sst(name, shape):
        return np.concatenate([r[c][name].reshape((DEPTH, NSEQ) + shape) for c in range(8)], axis=1).astype(f)
    return (y_p, y_s,
            pst('p_dn_conv', (3, 1536)), pst('p_dn_s', (4, 128, 128)), pst('p_ml_c', (4, 128, 128)), pst('p_ml_n', (4, 128)),
            pst('p_ml_m', (4,)), pst('p_s5_re', (32, 64)), pst('p_s5_im', (32, 64)),
            sst('s_dn_conv', (3, 1536)), sst('s_dn_s', (4, 128, 128)), sst('s_ml_c', (4, 128, 128)), sst('s_ml_n', (4, 128)),
            sst('s_ml_m', (4,)), sst('s_s5_re', (32, 64)), sst('s_s5_im', (32, 64)))
```
